# Optimizing a Trainium2 kernel written in Bass

```python
import jax, jax.numpy as jnp
from jax import lax
import numpy as np

D_MODEL = 1024
BATCH = 4
SEQ = 4096
DEPTH = 1

PLE_DIM = 256
D_FF = 4 * D_MODEL
EPS = 1e-6
RET_HEADS = 4
RET_DK = 128
RET_DV = 256
RET_CHUNK = 128
RET_QK = RET_HEADS * RET_DK
RET_V = RET_HEADS * RET_DV
ATT_PATTERNS = ((128, 1), (512, 4), (2048, 16))
N_GROUPS = len(ATT_PATTERNS)
ATT_HEADS = 4
ATT_HD = 128
ATT_W = ATT_HEADS * ATT_HD
ROPE_THETA = 10000.0
N_BRANCH = 2
IN_SIZES = (RET_QK, RET_QK, RET_V, RET_V,
            N_GROUPS * ATT_W, N_GROUPS * ATT_W, N_GROUPS * ATT_W,
            N_BRANCH * D_MODEL)
D_IN = int(sum(IN_SIZES))
IN_OFFSETS = tuple(int(o) for o in np.cumsum(IN_SIZES)[:-1])

kernel_name = "hybrid_gated_retention_dilated_attn_block"


def rmsnorm(x, g):
    xf = x.astype(jnp.float32)
    y = xf * lax.rsqrt(jnp.mean(xf * xf, axis=-1, keepdims=True) + EPS)
    return (y * g.astype(jnp.float32)).astype(x.dtype)


def rotate(x, inv_freq):
    S = x.shape[1]
    pos = jnp.arange(S, dtype=jnp.float32)
    ang = pos[:, None] * inv_freq[None, :]
    cos = jnp.cos(ang)[:, None, :].astype(x.dtype)
    sin = jnp.sin(ang)[:, None, :].astype(x.dtype)
    x1, x2 = jnp.split(x, 2, axis=-1)
    return jnp.concatenate([x1 * cos - x2 * sin, x2 * cos + x1 * sin], axis=-1)


def retention_chunkwise(q, k, v):
    B, S, H, dk = q.shape
    dv = v.shape[-1]
    C = RET_CHUNK
    N = S // C
    dt = v.dtype
    log_g = jnp.log1p(-jnp.exp2(-5.0 - jnp.arange(H, dtype=jnp.float32)))
    idx = jnp.arange(C, dtype=jnp.float32)
    diff = idx[:, None] - idx[None, :]
    inner_decay = jnp.where(diff >= 0, jnp.exp(log_g[:, None, None] * jnp.maximum(diff, 0.0)), 0.0)
    q_decay = jnp.exp(log_g[None, :] * (idx[:, None] + 1.0))
    k_decay = jnp.exp(log_g[None, :] * (C - 1.0 - idx[:, None]))
    chunk_decay = jnp.exp(log_g * C)
    qc = q.reshape(B, N, C, H, dk)
    kc = k.reshape(B, N, C, H, dk)
    vc = v.reshape(B, N, C, H, dv)
    scores = jnp.einsum('bnihd,bnjhd->bnhij', qc, kc) * inner_decay.astype(dt)
    inner = jnp.einsum('bnhij,bnjhe->bnihe', scores, vc)
    kv = jnp.einsum('bnjhd,bnjhe->nbhde', kc * k_decay[:, :, None].astype(dt), vc).astype(jnp.float32)

    def step(state, kv_n):
        return state * chunk_decay[:, None, None] + kv_n, state

    _, prev = lax.scan(step, jnp.zeros((B, H, dk, dv), jnp.float32), kv)
    cross = jnp.einsum('bnihd,nbhde->bnihe', qc * q_decay[:, :, None].astype(dt), prev.astype(dt))
    return (inner + cross).reshape(B, S, H, dv)


def dilated_window_attention(q, k, v, window, dilation):
    B, S, H, hd = q.shape
    w_sub = window // dilation
    L = S // dilation
    nb = -(-L // w_sub)
    Lp = nb * w_sub

    def to_sub(t):
        t = t.reshape(B, L, dilation, H, hd).transpose(0, 2, 1, 3, 4)
        t = jnp.pad(t, ((0, 0), (0, 0), (0, Lp - L), (0, 0), (0, 0)))
        return t.reshape(B, dilation, nb, w_sub, H, hd)

    def with_prev(t):
        prev = jnp.pad(t, ((0, 0), (0, 0), (1, 0), (0, 0), (0, 0), (0, 0)))[:, :, :-1]
        return jnp.concatenate([prev, t], axis=3)

    qs = to_sub(q)
    kb = with_prev(to_sub(k))
    vb = with_prev(to_sub(v))
    s = jnp.einsum('brnqhd,brnkhd->brnhqk', qs, kb).astype(jnp.float32) * (hd ** -0.5)
    blk = jnp.arange(nb)[:, None]
    qpos = blk * w_sub + jnp.arange(w_sub)[None, :]
    kpos = (blk - 1) * w_sub + jnp.arange(2 * w_sub)[None, :]
    dist = qpos[:, :, None] - kpos[:, None, :]
    valid = (dist >= 0) & (dist <= w_sub) & (kpos[:, None, :] >= 0)
    s = jnp.where(valid[:, None], s, -jnp.inf)
    m = jnp.max(s, axis=-1, keepdims=True)
    e = jnp.exp(s - m)
    l = jnp.sum(e, axis=-1, keepdims=True)
    o = jnp.einsum('brnhqk,brnkhd->brnqhd', (e / l).astype(v.dtype), vb)
    lse = (m + jnp.log(l))[..., 0].transpose(0, 1, 2, 4, 3)

    def from_sub(t):
        t = t.reshape((B, dilation, Lp) + t.shape[4:])[:, :, :L]
        t = jnp.moveaxis(t, 1, 2)
        return t.reshape((B, S) + t.shape[3:])

    return from_sub(o), from_sub(lse)


def hybrid_layer(x, p_i, w_in, b_gate, g_mix, q_gain, k_gain, ret_gn, w_ret_out, w_att_out, w_o,
                 g_mlp, w_up, w_down, g_ple, w_ple_proj, w_ple_gate):
    B, S, _ = x.shape
    dt = x.dtype
    h = rmsnorm(x, g_mix)
    z = h @ w_in
    rq, rk, rv, rg, aq, ak, av, gl = jnp.split(z, IN_OFFSETS, axis=-1)

    ret_freq = 1.0 / (10000.0 ** jnp.linspace(0.0, 1.0, RET_DK // 2, dtype=jnp.float32))
    rq = rotate(rq.reshape(B, S, RET_HEADS, RET_DK), ret_freq)
    rk = rotate(rk.reshape(B, S, RET_HEADS, RET_DK), ret_freq) * (RET_DK ** -0.5)
    rv = rv.reshape(B, S, RET_HEADS, RET_DV)
    y = retention_chunkwise(rq, rk, rv)
    y = rmsnorm(y, ret_gn.reshape(RET_HEADS, RET_DV)).reshape(B, S, RET_V)
    ret_branch = (jax.nn.silu(rg) * y) @ w_ret_out

    aq = rmsnorm(aq.reshape(B, S, N_GROUPS, ATT_HEADS, ATT_HD), q_gain[:, None, :])
    ak = rmsnorm(ak.reshape(B, S, N_GROUPS, ATT_HEADS, ATT_HD), k_gain[:, None, :])
    av = av.reshape(B, S, N_GROUPS, ATT_HEADS, ATT_HD)
    rope_freq = ROPE_THETA ** (-jnp.arange(0, ATT_HD, 2, dtype=jnp.float32) / ATT_HD)
    aq = rotate(aq.reshape(B, S, N_GROUPS * ATT_HEADS, ATT_HD), rope_freq).reshape(B, S, N_GROUPS, ATT_HEADS, ATT_HD)
    ak = rotate(ak.reshape(B, S, N_GROUPS * ATT_HEADS, ATT_HD), rope_freq).reshape(B, S, N_GROUPS, ATT_HEADS, ATT_HD)
    outs, lses = [], []
    for g, (window, dilation) in enumerate(ATT_PATTERNS):
        o_g, lse_g = dilated_window_attention(aq[:, :, g], ak[:, :, g], av[:, :, g], window, dilation)
        outs.append(o_g)
        lses.append(lse_g)
    wts = jax.nn.softmax(jnp.stack(lses, axis=0), axis=0)
    o = jnp.einsum('gbsh,gbshd->bshd', wts.astype(dt), jnp.stack(outs, axis=0))
    att_branch = o.reshape(B, S, ATT_W) @ w_att_out

    gate_r, gate_a = jnp.split(jax.nn.sigmoid(gl + b_gate), N_BRANCH, axis=-1)
    x = x + (gate_r * ret_branch + gate_a * att_branch) @ w_o

    u = rmsnorm(x, g_mlp) @ w_up
    x = x + jnp.square(jax.nn.relu(u)) @ w_down

    gate_p = jax.nn.sigmoid(rmsnorm(x, g_ple) @ w_ple_gate)
    return x + gate_p * (p_i @ w_ple_proj)


def setup_inputs(seed: int = 0) -> dict:
    key = jax.random.key(seed)
    ks = jax.random.split(key, 20)
    f32 = jnp.float32

    def nrm(k, shape, scale):
        return jax.random.normal(k, shape, f32) * scale

    def gain(k, shape):
        return 1.0 + 0.02 * jax.random.normal(k, shape, f32)

    return {
        "x": nrm(ks[0], (BATCH, SEQ, D_MODEL), 1.0),
        "p": nrm(ks[1], (DEPTH, BATCH, SEQ, PLE_DIM), 1.0),
        "w_in": nrm(ks[2], (DEPTH, D_MODEL, D_IN), D_MODEL ** -0.5),
        "b_gate": nrm(ks[3], (DEPTH, N_BRANCH * D_MODEL), 0.01),
        "g_mix": gain(ks[4], (DEPTH, D_MODEL)),
        "q_gain": gain(ks[5], (DEPTH, N_GROUPS, ATT_HD)),
        "k_gain": gain(ks[6], (DEPTH, N_GROUPS, ATT_HD)),
        "ret_gn": gain(ks[7], (DEPTH, RET_V)),
        "w_ret_out": nrm(ks[8], (DEPTH, RET_V, D_MODEL), RET_V ** -0.5),
        "w_att_out": nrm(ks[9], (DEPTH, ATT_W, D_MODEL), ATT_W ** -0.5),
        "w_o": nrm(ks[10], (DEPTH, D_MODEL, D_MODEL), D_MODEL ** -0.5),
        "g_mlp": gain(ks[11], (DEPTH, D_MODEL)),
        "w_up": nrm(ks[12], (DEPTH, D_MODEL, D_FF), D_MODEL ** -0.5),
        "w_down": nrm(ks[13], (DEPTH, D_FF, D_MODEL), D_FF ** -0.5),
        "g_ple": gain(ks[14], (DEPTH, D_MODEL)),
        "w_ple_proj": nrm(ks[15], (DEPTH, PLE_DIM, D_MODEL), PLE_DIM ** -0.5),
        "w_ple_gate": nrm(ks[16], (DEPTH, D_MODEL, D_MODEL), D_MODEL ** -0.5),
    }


def reference(x, p, w_in, b_gate, g_mix, q_gain, k_gain, ret_gn, w_ret_out, w_att_out, w_o,
              g_mlp, w_up, w_down, g_ple, w_ple_proj, w_ple_gate):
    for i in range(DEPTH):
        x = hybrid_layer(x, p[i], w_in[i], b_gate[i], g_mix[i], q_gain[i], k_gain[i], ret_gn[i],
                         w_ret_out[i], w_att_out[i], w_o[i], g_mlp[i], w_up[i], w_down[i],
                         g_ple[i], w_ple_proj[i], w_ple_gate[i])
    return x
```

```python
import os
import numpy as np
import concourse.bass as bass
import concourse.mybir as mybir
from concourse.bass_utils import run_bass_kernel_spmd
from contextlib import ExitStack

F32 = mybir.dt.float32
BF16 = mybir.dt.bfloat16
AF = mybir.ActivationFunctionType
ALU = mybir.AluOpType

EPS = 1e-6
D = 1024
TOK = 2048
NT = 16
GAMMAS = [1.0 - 2.0 ** (-5.0 - h) for h in range(4)]
ATT_SCALE = 128.0 ** -0.5


class SemObj:
    def __init__(self, sem, step):
        self.sem = sem
        self.val = 0
        self.step = step


class Trk:
    __slots__ = ("name", "w", "r", "dsem")

    def __init__(self, name):
        self.name = name
        self.w = None
        self.r = {}
        self.dsem = None


class Eng:
    def __init__(self, name, h, prog):
        self.name = name
        self.h = h
        self.prog = prog
        self.seen = {}


class Sched:
    def __init__(self, nc, es):
        self.nc = nc
        self.es = es
        self.nsem = 0
        self.pe = Eng("pe", nc.tensor, self._sem("pe", 1))
        self.act = Eng("act", nc.scalar, self._sem("act", 1))
        self.dve = Eng("dve", nc.vector, self._sem("dve", 1))
        self.pool = Eng("pool", nc.gpsimd, self._sem("pool", 1))
        self.sp = Eng("sp", nc.sync, None)
        self.engs = [self.pe, self.act, self.dve, self.pool, self.sp]
        self.dsems = []

    def _sem(self, name, step):
        self.nsem += 1
        s = self.es.enter_context(self.nc.semaphore("s_%s_%d" % (name, self.nsem)))
        return SemObj(s, step)

    def _deps(self, reads, writes):
        deps = {}

        def add(so, v):
            if deps.get(so, 0) < v:
                deps[so] = v
        for t in reads:
            if t.w is not None:
                add(*t.w)
        for t in writes:
            if t.w is not None:
                add(*t.w)
            for so, v in t.r.items():
                add(so, v)
        return deps

    def _wait(self, eng, deps):
        for so, v in deps.items():
            if so is eng.prog and eng is self.pe:
                continue
            if eng.seen.get(so, 0) >= v:
                continue
            eng.h.wait_ge(so.sem, v)
            eng.seen[so] = v

    def _mark(self, ev, reads, writes):
        so, v = ev
        for t in reads:
            if t.r.get(so, 0) < v:
                t.r[so] = v
        for t in writes:
            t.w = ev
            t.r = {}

    def op(self, eng, fn, reads=(), writes=()):
        self._wait(eng, self._deps(reads, writes))
        ins = fn()
        eng.prog.val += 1
        ins.then_inc(eng.prog.sem, 1)
        self._mark((eng.prog, eng.prog.val), reads, writes)

    def dma(self, eng, out, in_, owner, reads=(), writes=()):
        self._wait(eng, self._deps(reads, writes))
        if owner.dsem is None:
            owner.dsem = {}
        kind = "sw" if eng is self.pool else "hw"
        if kind not in owner.dsem:
            owner.dsem[kind] = self._sem("d", 16)
            self.dsems.append(owner.dsem[kind])
        so = owner.dsem[kind]
        ins = eng.h.dma_start(out=out, in_=in_)
        so.val += 16
        ins.then_inc(so.sem, 16)
        self._mark((so, so.val), reads, writes)

    def barrier(self):
        sems = [e.prog for e in self.engs if e.prog is not None and e.prog.val > 0]
        sems += [s for s in self.dsems if s.val > 0]
        for e in self.engs:
            self._wait(e, {so: so.val for so in sems})


class Ring:
    def __init__(self, alloc, name, shape, dtype, n):
        self.tiles = [alloc("%s_%d" % (name, i), shape, dtype) for i in range(n)]
        self.trks = [Trk("%s_%d" % (name, i)) for i in range(n)]
        self.i = 0
        self.n = n

    def next(self):
        k = self.i % self.n
        self.i += 1
        return self.tiles[k], self.trks[k]


def swap_view(ap, nh):
    a = ap.ap
    return bass.AP(ap.tensor, ap.offset + 64, [list(a[0]), [128, nh], [-64, 2], [1, 64]])


def hv(ap, nh, w):
    return ap.rearrange("p (h w) -> p h w", h=nh, w=w)


def build_program(stop_after=None):
    nc = bass.Bass("TRN2", target_bir_lowering=False)

    def din(name, shape):
        return nc.dram_tensor(name, shape, F32, kind="ExternalInput").ap()

    x_own = din("x_own", [TOK, D])
    x_pre = din("x_pre", [TOK, D])
    p_own = din("p_own", [TOK, 256])
    w_in = din("w_in", [D, 9728])
    w_ro = din("w_ro", [1024, D])
    w_ao = din("w_ao", [512, D])
    w_o = din("w_o", [D, D])
    w_up = din("w_up", [D, 4096])
    w_down = din("w_down", [4096, D])
    w_pp = din("w_pp", [256, D])
    w_pg = din("w_pg", [D, D])
    g_mix = din("g_mix", [1, D])
    g_mlp = din("g_mlp", [1, D])
    g_ple = din("g_ple", [1, D])
    b_gate = din("b_gate", [1, 2048])
    q_gain = din("q_gain", [1, 384])
    k_gain = din("k_gain", [1, 384])
    ret_gn = din("ret_gn", [1, D])
    rcos = din("rcos", [4096, 128])
    rsin = din("rsin", [4096, 128])
    acos = din("acos", [4096, 128])
    asin = din("asin", [4096, 128])
    mt_tab = din("mt_tab", [128, 512])
    rtab = din("rtab", [128, 12])
    mask01 = din("mask01", [128, 256])
    ident_in = din("ident", [128, 128])
    flag_in = din("flag", [128, 4])
    y_out = nc.dram_tensor("y", [TOK, D], F32, kind="ExternalOutput").ap()
    scr = nc.dram_tensor("scr_att", [3, TOK, 528], F32, kind="Internal").ap()
    scr_o = nc.dram_tensor("scr_o", [NT, 128, 512], BF16, kind="Internal").ap()
    dbg = None
    if stop_after is not None:
        dbg = nc.dram_tensor("dbg", [128, 16384], F32, kind="ExternalOutput").ap()

    def bcast_rows(ap, n):
        return bass.AP(ap.tensor, ap.offset, [[0, 128], [1, n]])

    ges = ExitStack()
    with ges:
        S = Sched(nc, ges)

        uniq = [0]

        def mk_alloc(es, side):
            def alloc(name, shape, dtype):
                uniq[0] += 1
                return es.enter_context(nc.sbuf_tensor("sb%d_%s" % (uniq[0], name), shape, dtype, side=side))
            return alloc
        galloc = mk_alloc(ges, "left")

        def palloc(name, shape, dtype):
            return ges.enter_context(nc.psum_tensor(name, shape, dtype))

        pz = [palloc("pz%d" % i, [128, 512], F32) for i in range(3)]
        pz_t = [Trk("pz%d" % i) for i in range(3)]
        pt = palloc("pt", [128, 1024], BF16)
        pt_t = Trk("pt")
        ps = [palloc("ps%d" % i, [128, 512], F32) for i in range(2)]
        ps_t = [Trk("ps%d" % i) for i in range(2)]
        po = [palloc("po%d" % i, [128, 512], F32) for i in range(2)]
        po_t = [Trk("po%d" % i) for i in range(2)]

        consts_t = Trk("consts")
        ident = galloc("ident", [128, 128], BF16)
        msk = galloc("msk", [128, 256], BF16)
        flag = galloc("flag", [128, 4], BF16)
        epsc = galloc("epsc", [128, 1], F32)
        junkr = Ring(galloc, "junk", [128, 1024], BF16, 2)
        S.dma(S.pool, ident[:], ident_in[:, :], consts_t, writes=[consts_t])
        S.dma(S.pool, msk[:], mask01[:, :], consts_t, writes=[consts_t])
        S.dma(S.pool, flag[:], flag_in[:, :], consts_t, writes=[consts_t])
        S.op(S.pool, lambda: nc.gpsimd.memset(epsc[:], EPS), writes=[consts_t])
        nhalf = galloc("nhalf", [128, 4], F32)
        S.op(S.pool, lambda: nc.gpsimd.memset(nhalf[:], -0.5), writes=[consts_t])

        def rstd_pool(out_ap, in_ap, scale, n, in_t, mid, mid_t, out_t):
            S.op(S.pool, lambda: nc.gpsimd.tensor_scalar(out=mid, in0=in_ap, scalar1=float(scale), scalar2=EPS,
                                                         op0=ALU.mult, op1=ALU.add),
                 reads=[in_t], writes=[mid_t])
            S.op(S.pool, lambda: nc.gpsimd.tensor_tensor(out=out_ap, in0=mid, in1=nhalf[:, 0:n], op=ALU.pow),
                 reads=[mid_t, consts_t], writes=[out_t])

        actT = galloc("actT", [128, 8 * TOK], BF16)
        actT3 = actT[:].rearrange("p (c t) -> p c t", c=8)
        actT_t = [Trk("actT%d" % i) for i in range(NT)]
        les = ExitStack()
        ges.enter_context(les)
        lalloc = mk_alloc(les, "left")
        hT_own = lalloc("hT_own", [128, 8, TOK], BF16)
        pes = ExitStack()
        ges.enter_context(pes)
        hT_pre = mk_alloc(pes, "left")("hT_pre", [128, 8, TOK], BF16)

        def mm_group(out_ap, pairs, reads, out_trk):
            def fn():
                n = len(pairs)
                ins = None
                for i, (l, r) in enumerate(pairs):
                    ins = nc.tensor.matmul(out_ap, l, r, start=(i == 0), stop=(i == n - 1))
                return ins
            S.op(S.pe, fn, reads=reads, writes=[out_trk])

        def transposes(src_ap, nblk, src_trk, dst=None, dst_t=None, off=0):
            if dst is None:
                dst, dst_t = pt[:], pt_t

            def fn():
                ins = None
                for k in range(nblk):
                    ins = nc.tensor.transpose(dst[:, off + k * 128:off + (k + 1) * 128], src_ap[:, k * 128:(k + 1) * 128], ident[:])
                return ins
            S.op(S.pe, fn, reads=[src_trk, consts_t], writes=[dst_t])

        def pipeline(n, stages, skews=None):
            skews = skews or list(range(len(stages)))
            for step in range(n + max(skews)):
                for f, sk in reversed(list(zip(stages, skews))):
                    i = step - sk
                    if 0 <= i < n:
                        f(i)

        ptb = [pt[:], po[1][:].bitcast(BF16)]
        ptb_t = [pt_t, po_t[1]]

        def norm_transpose(tmp, x_ap, x_trk, g_b, g_trk, dst_ap, dst_trk):
            ss, ss_t = tmp["ss"].next()
            sd, sd_t = tmp["sd"].next()
            rs, rs_t = tmp["rs"].next()
            xn, xn_t = tmp["xn"].next()
            junk, junk_t = junkr.next()
            S.op(S.act, lambda: nc.scalar.activation(out=junk[:], in_=x_ap, func=AF.Square, accum_out=ss[:, 0:1]),
                 reads=[x_trk], writes=[ss_t, junk_t])
            rstd_pool(rs[:, 0:1], ss[:, 0:1], 1.0 / D, 1, ss_t, sd[:, 0:1], sd_t, rs_t)
            S.op(S.dve, lambda: nc.vector.scalar_tensor_tensor(out=xn[:], in0=x_ap, scalar=rs[:, 0:1], in1=g_b[:],
                                                               op0=ALU.mult, op1=ALU.mult),
                 reads=[x_trk, rs_t, g_trk], writes=[xn_t])
            transposes(xn[:], 8, xn_t)
            S.op(S.act, lambda: nc.scalar.copy(out=dst_ap, in_=pt[:].rearrange("p (c t) -> p c t", c=8)),
                 reads=[pt_t], writes=[dst_trk])

        def dbg_dump(ap_bf16_or_f32, ncols, is_bf16, alloc):
            t_ = Trk("dbgt")
            CH = 2048
            stg = alloc("dbg_stg", [128, CH], F32)
            for c0 in range(0, ncols, CH):
                w = min(CH, ncols - c0)
                S.op(S.dve, lambda: nc.vector.tensor_copy(stg[:, 0:w], ap_bf16_or_f32[:, c0:c0 + w]), writes=[t_])
                S.dma(S.sp, dbg[:, c0:c0 + w], stg[:, 0:w], t_, reads=[t_])
            S.barrier()

        def finish():
            S.barrier()
            for so in S.dsems:
                S.sp.h.wait_ge(so.sem, so.val)

        res = ExitStack()
        ges.enter_context(res)
        R32 = mk_alloc(res, "right")("R32", [128, 1024], F32)
        R32_t = Trk("R32")
        S.op(S.pool, lambda: nc.gpsimd.memset(R32[:], 0.0), writes=[R32_t])
        aes = ExitStack()
        ges.enter_context(aes)
        aa = mk_alloc(aes, "right")
        wK = aa("wK", [128, 8, 512], BF16)
        wV = aa("wV", [128, 8, 512], BF16)
        wQ = aa("wQ", [128, 8, 512], BF16)
        wK_t, wV_t, wQ_t = Trk("wK"), Trk("wV"), Trk("wQ")
        with ExitStack() as es:
            ra = mk_alloc(es, "right")
            gmix_b = ra("gmix_b", [128, D], F32)
            gmix_t = Trk("gmix")
            S.dma(S.sp, gmix_b[:], bcast_rows(g_mix, D), gmix_t, writes=[gmix_t])
            tmp = {"ss": Ring(ra, "ss", [128, 1], F32, 2), "sd": Ring(ra, "sd", [128, 1], F32, 2),
                   "rs": Ring(ra, "rs", [128, 1], F32, 2), "xn": Ring(ra, "xn", [128, D], BF16, 2)}
            xt = Ring(ra, "xt", [128, D], F32, 3)
            tmp["xn"] = Ring(ra, "xn3", [128, D], BF16, 3)
            WrKV = ra("WrKV", [128, 8, 1536], BF16)
            WrKV_t = Trk("WrKV")
            for k in range(3):
                S.dma(S.pool, WrKV[:, :, k * 512:(k + 1) * 512],
                      w_in[:, 512 + k * 512:512 + (k + 1) * 512].rearrange("(c p) n -> p c n", p=128), WrKV_t, writes=[WrKV_t])
            for dst_, dst_t_, col0_ in ((wK, wK_t, 4608 + 2 * 512), (wV, wV_t, 6144 + 2 * 512), (wQ, wQ_t, 3072 + 2 * 512)):
                S.dma(S.pool, dst_[:], w_in[:, col0_:col0_ + 512].rearrange("(c p) n -> p c n", p=128), dst_t_, writes=[dst_t_])
            prt = ra("prt", [128, 12], F32)
            prt_t = Trk("prt")
            S.dma(S.sp, prt[:], rtab[:, :], prt_t, writes=[prt_t])
            prc = Ring(ra, "prc", [128, 128], F32, 3)
            prs = Ring(ra, "prs", [128, 128], F32, 3)
            ptA = Ring(ra, "ptA", [128, 512], F32, 2)
            ptB = Ring(ra, "ptB", [128, 512], F32, 2)
            pkf = Ring(ra, "pkf", [128, 512], BF16, 3)
            pkd = Ring(ra, "pkd", [128, 512], BF16, 3)
            pvr = Ring(ra, "pv", [128, 1024], BF16, 5)
            PBK, PBV0, PBV1 = (pz[0], pz_t[0]), (pz[2], pz_t[2]), (ps[0], ps_t[0])
            PKV = [(ps[1], ps_t[1]), (po[0], po_t[0])]
            px = {}

            def pp0(t):
                rc, rc_tk = prc.next()
                rs_, rs_tk = prs.next()
                S.dma(S.sp, rc[:], rcos[t * 128:(t + 1) * 128, :], rc_tk, writes=[rc_tk])
                S.dma(S.sp, rs_[:], rsin[t * 128:(t + 1) * 128, :], rs_tk, writes=[rs_tk])
                for bk, c0 in ((PBK, 0), (PBV0, 512), (PBV1, 1024)):
                    mm_group(bk[0][:], [(hT_pre[:, c, t * 128:(t + 1) * 128], WrKV[:, c, c0:c0 + 512]) for c in range(8)],
                             [WrKV_t, hpre_tt[t]], bk[1])
                px[t] = dict(tab=(rc, rc_tk, rs_, rs_tk))

            def pp1(t):
                X = px[t]
                rc, rc_tk, rs_, rs_tk = X["tab"]
                v, v_t = pvr.next()
                S.op(S.act, lambda: nc.scalar.copy(out=v[:, 0:512], in_=PBV0[0][:]), reads=[PBV0[1]], writes=[v_t])
                S.op(S.act, lambda: nc.scalar.copy(out=v[:, 512:1024], in_=PBV1[0][:]), reads=[PBV1[1]], writes=[v_t])
                tA, tA_t = ptA.next()
                tB, tB_t = ptB.next()
                kf, kf_t = pkf.next()
                kd, kd_t = pkd.next()
                z = PBK[0]
                cb = rc[:].unsqueeze(1).to_broadcast([128, 4, 128])
                sb_ = rs_[:].rearrange("p (a f) -> p a f", a=2).unsqueeze(1).to_broadcast([128, 4, 2, 64])
                S.op(S.dve, lambda: nc.vector.tensor_tensor(out=hv(tA[:], 4, 128), in0=hv(z[:], 4, 128), in1=cb, op=ALU.mult),
                     reads=[PBK[1], rc_tk], writes=[tA_t])
                S.op(S.dve, lambda: nc.vector.tensor_tensor(out=tB[:].rearrange("p (h a f) -> p h a f", h=4, a=2),
                                                            in0=swap_view(z[:], 4), in1=sb_, op=ALU.mult),
                     reads=[PBK[1], rs_tk], writes=[tB_t])
                S.op(S.pool, lambda: nc.gpsimd.tensor_tensor(out=kf[:], in0=tA[:], in1=tB[:], op=ALU.add),
                     reads=[tA_t, tB_t], writes=[kf_t])
                S.op(S.pool, lambda: nc.gpsimd.tensor_tensor(out=hv(kd[:], 4, 128), in0=hv(kf[:], 4, 128),
                                                             in1=prt[:, 0:4].unsqueeze(2).to_broadcast([128, 4, 128]), op=ALU.mult),
                     reads=[kf_t, prt_t], writes=[kd_t])
                X.update(v=(v, v_t), kd=(kd, kd_t))

            def ppu(t, pr):
                X = px[t]
                kd, kd_t = X["kd"]
                v, v_t = X["v"]
                bank = PKV[pr]

                def kvmm():
                    ins = None
                    for hh in range(2):
                        h = 2 * pr + hh
                        ins = nc.tensor.matmul(bank[0][:, hh * 256:(hh + 1) * 256], kd[:, h * 128:(h + 1) * 128],
                                               v[:, h * 256:(h + 1) * 256], start=True, stop=True)
                    return ins
                S.op(S.pe, kvmm, reads=[kd_t, v_t], writes=[bank[1]])
                for hh in range(2):
                    h = 2 * pr + hh
                    gC = float(GAMMAS[h] ** 128)
                    S.op(S.dve, lambda: nc.vector.scalar_tensor_tensor(
                        out=R32[:, h * 256:(h + 1) * 256], in0=R32[:, h * 256:(h + 1) * 256], scalar=gC,
                        in1=bank[0][:, hh * 256:(hh + 1) * 256], op0=ALU.mult, op1=ALU.add),
                        reads=[bank[1], R32_t], writes=[R32_t])
                if pr == 1:
                    px.pop(t)

            def pre_only(f, *a):
                return lambda t: f(t, *a) if t < NT else None
            hpre_tt = [Trk("hpre%d" % i) for i in range(NT)]
            hown_t = Trk("hown")
            ctxA = {}

            def a_s0(t):
                src = x_pre if t < NT else x_own
                tt = t % NT
                xa, xa_t = xt.next()
                S.dma(S.sp, xa[:], src[tt * 128:(tt + 1) * 128, :], xa_t, writes=[xa_t])
                ss, ss_t = tmp["ss"].next()
                sd, sd_t = tmp["sd"].next()
                rs, rs_t = tmp["rs"].next()
                xn, xn_t = tmp["xn"].next()
                junk, junk_t = junkr.next()
                S.op(S.act, lambda: nc.scalar.activation(out=junk[:], in_=xa[:], func=AF.Square, accum_out=ss[:, 0:1]),
                     reads=[xa_t], writes=[ss_t, junk_t])
                rstd_pool(rs[:, 0:1], ss[:, 0:1], 1.0 / D, 1, ss_t, sd[:, 0:1], sd_t, rs_t)
                S.op(S.dve, lambda: nc.vector.scalar_tensor_tensor(out=xn[:], in0=xa[:], scalar=rs[:, 0:1], in1=gmix_b[:],
                                                                   op0=ALU.mult, op1=ALU.mult),
                     reads=[xa_t, rs_t, gmix_t], writes=[xn_t])
                ctxA[t] = (xn, xn_t)

            def a_s1(t):
                xn, xn_t = ctxA.pop(t)
                tt = t % NT
                k = t % 2
                transposes(xn[:], 8, xn_t, ptb[k], ptb_t[k])
                dstT = hT_pre if t < NT else hT_own
                S.op(S.act, lambda: nc.scalar.copy(out=dstT[:, :, tt * 128:(tt + 1) * 128],
                                                   in_=ptb[k].rearrange("p (c t) -> p c t", c=8)),
                     reads=[ptb_t[k]], writes=[hpre_tt[t] if t < NT else hown_t])
            pipeline(2 * NT, [a_s0, pre_only(pp0), pre_only(ppu, 1), pre_only(ppu, 0), a_s1, pre_only(pp1)], [0, 4, 7, 7, 2, 5])
            S.barrier()
            if stop_after == "A":
                dbg_dump(hT_own[:].rearrange("p c t -> p (c t)"), 16384, True, ra)
                finish()
                return nc

        with ExitStack() as es:
            ra = mk_alloc(es, "right")
            qg_b = ra("qg_b", [128, 384], F32)
            kg_b = ra("kg_b", [128, 384], F32)
            gain_t = Trk("gains")
            S.dma(S.sp, qg_b[:], bcast_rows(q_gain, 384), gain_t, writes=[gain_t])
            S.dma(S.sp, kg_b[:], bcast_rows(k_gain, 384), gain_t, writes=[gain_t])
            vaug = ra("vaug", [128, 32, 4, 132], BF16)
            vaug_t = Trk("vaug")
            kst = actT[:].rearrange("p (i h t) -> p i h t", i=32, h=4)
            kst_t = Trk("kst")
            ctr = Ring(ra, "ct", [128, 128], F32, 3)
            strg = Ring(ra, "st", [128, 128], F32, 3)
            cgr = Ring(ra, "cg", [128, 128], F32, 3)
            sgr = Ring(ra, "sg", [128, 128], F32, 3)
            ss4r = Ring(ra, "ss4", [128, 4], F32, 3)
            ln4r = Ring(ra, "ln4", [128, 4], F32, 3)
            rs4r = Ring(ra, "rs4", [128, 4], F32, 3)
            tAr = Ring(ra, "tA", [128, 512], F32, 2)
            tBr = Ring(ra, "tB", [128, 512], F32, 2)
            qkfr = Ring(ra, "qkf", [128, 512], BF16, 5)
            qTr = Ring(ra, "qT", [128, 512], BF16, 4)
            Er = Ring(ra, "E", [128, 512], BF16, 4)
            Pr = Ring(ra, "P", [128, 512], BF16, 8)
            utr = Ring(ra, "ut", [128, 4, 132], F32, 2)
            for ut_, ut_t_ in zip(utr.tiles, utr.trks):
                S.op(S.pool, lambda: nc.gpsimd.memset(ut_[:], 0.0), writes=[ut_t_])

            def load_w(dst, dst_t, col0):
                S.dma(S.pool, dst[:], w_in[:, col0:col0 + 512].rearrange("(c p) n -> p c n", p=128), dst_t, writes=[dst_t])

            def hsel(base, d, c):
                if base < TOK:
                    return hT_pre[:, c, base:base + 127 * d + 1:d]
                b = base - TOK
                return hT_own[:, c, b:b + 127 * d + 1:d]

            def tables(base, d, gain_b, g):
                ct, ct_t = ctr.next()
                st, st_t = strg.next()
                S.dma(S.sp, ct[:], acos[base:base + 127 * d + 1:d, :], ct_t, writes=[ct_t])
                S.dma(S.sp, st[:], asin[base:base + 127 * d + 1:d, :], st_t, writes=[st_t])
                cg, cg_t = cgr.next()
                sg, sg_t = sgr.next()
                gsl = gain_b[:, g * 128:(g + 1) * 128]
                gsw = bass.AP(gsl.tensor, gsl.offset + 64, [list(gsl.ap[0]), [-64, 2], [1, 64]])
                S.op(S.pool, lambda: nc.gpsimd.tensor_tensor(out=cg[:], in0=ct[:], in1=gsl, op=ALU.mult),
                     reads=[ct_t, gain_t], writes=[cg_t])
                S.op(S.pool, lambda: nc.gpsimd.tensor_tensor(out=sg[:].rearrange("p (a f) -> p a f", a=2),
                                                             in0=st[:].rearrange("p (a f) -> p a f", a=2), in1=gsw, op=ALU.mult),
                     reads=[st_t, gain_t], writes=[sg_t])
                return cg, cg_t, sg, sg_t

            def norm_rope(z, z_t, cg, cg_t, sg, sg_t):
                ss4, ss4_t = ss4r.next()
                ln4, ln4_t = ln4r.next()
                rs4, rs4_t = rs4r.next()
                tA, tA_t = tAr.next()
                tB, tB_t = tBr.next()
                of, of_t = qkfr.next()

                def sq():
                    ins = None
                    for h in range(4):
                        ins = nc.scalar.activation(out=junk[:, h * 128:(h + 1) * 128], in_=z[:, h * 128:(h + 1) * 128], func=AF.Square,
                                                   accum_out=ss4[:, h:h + 1])
                    return ins
                junk, junk_t = junkr.next()
                S.op(S.act, sq, reads=[z_t], writes=[ss4_t, junk_t])
                if False:
                    def pw():
                        nc.gpsimd.tensor_scalar(out=ln4[:], in0=ss4[:], scalar1=1.0 / 128, scalar2=EPS, op0=ALU.mult, op1=ALU.add)
                        return nc.gpsimd.tensor_tensor(out=rs4[:], in0=ln4[:], in1=nhalf[:, 0:4], op=ALU.pow)
                    S.op(S.pool, pw, reads=[ss4_t, consts_t], writes=[ln4_t, rs4_t])
                elif True:
                    S.op(S.act, lambda: nc.scalar.activation(out=ln4[:], in_=ss4[:], func=AF.Sqrt, bias=epsc[:, 0:1], scale=1.0 / 128),
                         reads=[ss4_t, consts_t], writes=[ln4_t])
                    S.op(S.dve, lambda: nc.vector.reciprocal(out=rs4[:], in_=ln4[:]), reads=[ln4_t], writes=[rs4_t])
                else:
                    S.op(S.act, lambda: nc.scalar.activation(out=ln4[:], in_=ss4[:], func=AF.Ln, bias=epsc[:, 0:1], scale=1.0 / 128),
                         reads=[ss4_t, consts_t], writes=[ln4_t])
                    S.op(S.act, lambda: nc.scalar.activation(out=rs4[:], in_=ln4[:], func=AF.Exp, scale=-0.5),
                         reads=[ln4_t], writes=[rs4_t])
                cgb = cg[:].unsqueeze(1).to_broadcast([128, 4, 128])
                sgb = sg[:].rearrange("p (a f) -> p a f", a=2).unsqueeze(1).to_broadcast([128, 4, 2, 64])
                S.op(S.dve, lambda: nc.vector.tensor_tensor(out=hv(tA[:], 4, 128), in0=hv(z, 4, 128), in1=cgb, op=ALU.mult),
                     reads=[z_t, cg_t], writes=[tA_t])
                S.op(S.dve, lambda: nc.vector.tensor_tensor(out=tB[:].rearrange("p (h a f) -> p h a f", h=4, a=2),
                                                            in0=swap_view(z, 4), in1=sgb, op=ALU.mult),
                     reads=[z_t, sg_t], writes=[tB_t])
                S.op(S.pool, lambda: nc.gpsimd.tensor_tensor(out=tA[:], in0=tA[:], in1=tB[:], op=ALU.add),
                     reads=[tA_t, tB_t], writes=[tA_t])
                S.op(S.dve, lambda: nc.vector.tensor_tensor(out=hv(of[:], 4, 128), in0=hv(tA[:], 4, 128),
                                                            in1=rs4[:].unsqueeze(2).to_broadcast([128, 4, 128]), op=ALU.mult),
                     reads=[tA_t, rs4_t], writes=[of_t])
                return of, of_t

            zkb = [(pz[0], pz_t[0]), (pz[1], pz_t[1])]
            zvb = [(pz[2], pz_t[2]), (ps[0], ps_t[0])]

            for g, d in ((2, 16), (1, 4), (0, 1)):
                Bt = 128 * d
                nb = TOK // Bt
                npre = d
                bases = [TOK - Bt + r for r in range(d)] + [TOK + n * Bt + r for n in range(nb) for r in range(d)]
                ntl = len(bases)
                S.op(S.pool, lambda: nc.gpsimd.memset(vaug[:, 0:ntl, :, 128:129], 1.0), writes=[vaug_t])
                S.op(S.pool, lambda: nc.gpsimd.tensor_copy(
                    out=vaug[:, 0:npre, :, 128:129],
                    in_=flag[:, 0:4].unsqueeze(1).unsqueeze(3).to_broadcast([128, npre, 4, 1])),
                    reads=[consts_t], writes=[vaug_t])
                cx = {}

                def k_s0(i):
                    base = bases[i]
                    tb = tables(base, d, kg_b, g)
                    zk, zk_t = zkb[i % 2]
                    zv, zv_t = zvb[i % 2]
                    mm_group(zk[:], [(hsel(base, d, c), wK[:, c, :]) for c in range(8)], [wK_t], zk_t)
                    mm_group(zv[:], [(hsel(base, d, c), wV[:, c, :]) for c in range(8)], [wV_t], zv_t)
                    cx[i] = tb

                def k_s1(i):
                    cg, cg_t, sg, sg_t = cx[i]
                    zk, zk_t = zkb[i % 2]
                    zv, zv_t = zvb[i % 2]
                    S.op(S.act, lambda: nc.scalar.copy(out=vaug[:, i, :, 0:128], in_=hv(zv[:], 4, 128)),
                         reads=[zv_t], writes=[vaug_t])
                    cx[i] = norm_rope(zk[:], zk_t, cg, cg_t, sg, sg_t)

                def k_s2(i):
                    kf, kf_t = cx.pop(i)
                    k = i % 2
                    transposes(kf[:], 4, kf_t, ptb[k], ptb_t[k])
                    S.op(S.dve, lambda: nc.vector.tensor_copy(out=kst[:, i, :, :], in_=ptb[k][:, 0:512].rearrange("p (h t) -> p h t", h=4)),
                         reads=[ptb_t[k]], writes=[kst_t])
                pipeline(ntl, [k_s0, k_s1, k_s2], [0, 1, 4])

                qx = {}
                nq = nb * d

                def q_s0(j):
                    i = npre + j
                    base = bases[i]
                    tb = tables(base, d, qg_b, g)
                    zq, zq_t = zkb[j % 2]
                    mm_group(zq[:], [(hsel(base, d, c), wQ[:, c, :]) for c in range(8)], [wQ_t], zq_t)
                    qx[j] = dict(tb=tb)

                def q_s1(j):
                    if j % 2 == 1:
                        return
                    for jj in (j, j + 1):
                        cg, cg_t, sg, sg_t = qx[jj]["tb"]
                        zq, zq_t = zkb[jj % 2]
                        qx[jj]["qf"] = norm_rope(zq[:], zq_t, cg, cg_t, sg, sg_t)

                def q_s2(j):
                    qf, qf_t = qx[j]["qf"]
                    k = j % 2
                    transposes(qf[:], 4, qf_t, ptb[k], ptb_t[k])
                    qT, qT_t = qTr.next()
                    S.op(S.dve, lambda: nc.vector.tensor_copy(out=qT[:], in_=ptb[k][:, 0:512]), reads=[ptb_t[k]], writes=[qT_t])
                    qx[j]["qT"] = (qT, qT_t)

                sbank = [(pz[2], pz_t[2]), (ps[0], ps_t[0])]
                ubank = [(ps[1], ps_t[1]), (po[0], po_t[0])]

                def q_s3(j):
                    i = npre + j
                    qT, qT_t = qx[j]["qT"]
                    for pr in range(2):
                        sb_, sb_t = sbank[pr]

                        def smm():
                            ins = None
                            for hh in range(2):
                                h = 2 * pr + hh
                                for kk in range(2):
                                    ki = i - d if kk == 0 else i
                                    c0 = (hh * 2 + kk) * 128
                                    ins = nc.tensor.matmul(sb_[:, c0:c0 + 128], kst[:, ki, h, :], qT[:, h * 128:(h + 1) * 128],
                                                           start=True, stop=True)
                            return ins
                        S.op(S.pe, smm, reads=[kst_t, qT_t], writes=[sb_t])

                def q_s4(j):
                    Ps = []
                    for pr in range(2):
                        sb_, sb_t = sbank[pr]
                        E, E_t = Er.next()
                        P, P_t = Pr.next()
                        S.op(S.act, lambda: nc.scalar.activation(out=E[:], in_=sb_[:], func=AF.Exp, scale=ATT_SCALE),
                             reads=[sb_t], writes=[E_t])
                        meng, mh = (S.pool, nc.gpsimd) if pr == 0 else (S.dve, nc.vector)
                        S.op(meng, lambda: mh.tensor_tensor(
                            out=hv(P[:], 2, 256), in0=hv(E[:], 2, 256),
                            in1=msk[:].unsqueeze(1).to_broadcast([128, 2, 256]), op=ALU.mult),
                            reads=[E_t, consts_t], writes=[P_t])
                        Ps.append((P, P_t))
                    qx[j]["P"] = Ps

                def q_s5(j):
                    i = npre + j
                    for pr in range(2):
                        P, P_t = qx[j]["P"][pr]
                        ub, ub_t = ubank[pr]

                        def umm():
                            ins = None
                            for hh in range(2):
                                h = 2 * pr + hh
                                o_ap = ub[:, hh * 256:hh * 256 + 129]
                                nc.tensor.matmul(o_ap, P[:, hh * 256:hh * 256 + 128], vaug[:, i - d, h, 0:129], start=True, stop=False)
                                ins = nc.tensor.matmul(o_ap, P[:, hh * 256 + 128:hh * 256 + 256], vaug[:, i, h, 0:129], start=False, stop=True)
                            return ins
                        S.op(S.pe, umm, reads=[P_t, vaug_t], writes=[ub_t])

                def q_s6(j):
                    i = npre + j
                    ut, ut_t = utr.next()
                    for pr in range(2):
                        ub, ub_t = ubank[pr]
                        S.op(S.dve, lambda: nc.vector.tensor_copy(out=ut[:, 2 * pr:2 * pr + 2, 0:129],
                                                                  in_=hv(ub[:], 2, 256)[:, :, 0:129]),
                             reads=[ub_t], writes=[ut_t])
                    b0 = bases[i] - TOK
                    S.dma(S.sp, scr[g, b0:b0 + 127 * d + 1:d, :], ut[:].rearrange("p h w -> p (h w)"), ut_t, reads=[ut_t])
                    qx.pop(j)
                if g > 0:
                    load_w(wK, wK_t, 4608 + (g - 1) * 512)
                    load_w(wV, wV_t, 6144 + (g - 1) * 512)
                pipeline(nq, [q_s0, q_s2, q_s1, q_s3, q_s4, q_s5, q_s6], [0, 4, 2, 6, 7, 9, 10])
                if g > 0:
                    load_w(wQ, wQ_t, 3072 + (g - 1) * 512)
            S.barrier()

        aes.close()
        bes = ExitStack()
        ges.enter_context(bes)
        Wr = mk_alloc(bes, "right")("Wr", [128, 8, 3072], BF16)
        Wr_t = Trk("Wr")
        wd_t = Trk("wd1")
        for k in range(6):
            S.dma(S.pool, Wr[:, :, k * 512:(k + 1) * 512], w_in[:, k * 512:(k + 1) * 512].rearrange("(c p) n -> p c n", p=128),
                  Wr_t, writes=[Wr_t])
        with ExitStack() as es:
            ra = mk_alloc(es, "right")
            mgr = Ring(ra, "mg", [128, 3, 528], F32, 4)
            usr = Ring(ra, "us", [128, 528], F32, 2)
            rlr = Ring(ra, "rl", [128, 4], F32, 2)
            obr = Ring(ra, "ob", [128, 512], BF16, 4)
            otr_ = Ring(ra, "oTt", [128, 512], BF16, 2)
            scro_t = [Trk("scro%d" % i) for i in range(NT)]
            mx = {}

            def m_s0(t):
                mg, mg_t = mgr.next()
                S.dma(S.sp, mg[:], scr[:, t * 128:(t + 1) * 128, :].transpose([1, 0, 2]), mg_t, writes=[mg_t])
                mx[t] = (mg, mg_t)

            def m_s1(t):
                mg, mg_t = mx[t]
                us, us_t = usr.next()
                rl, rl_t = rlr.next()
                ob, ob_t = obr.next()
                S.op(S.pool, lambda: nc.gpsimd.tensor_tensor(out=us[:], in0=mg[:, 0, :], in1=mg[:, 1, :], op=ALU.add),
                     reads=[mg_t], writes=[us_t])
                S.op(S.pool, lambda: nc.gpsimd.tensor_tensor(out=us[:], in0=us[:], in1=mg[:, 2, :], op=ALU.add),
                     reads=[mg_t, us_t], writes=[us_t])
                us3 = us[:].rearrange("p (h w) -> p h w", h=4)
                S.op(S.dve, lambda: nc.vector.reciprocal(out=rl[:].unsqueeze(2), in_=us3[:, :, 128:129]), reads=[us_t], writes=[rl_t])
                S.op(S.dve, lambda: nc.vector.tensor_tensor(out=hv(ob[:], 4, 128), in0=us3[:, :, 0:128],
                                                            in1=rl[:].unsqueeze(2).to_broadcast([128, 4, 128]), op=ALU.mult),
                     reads=[us_t, rl_t], writes=[ob_t])
                mx[t] = (ob, ob_t)

            def m_s2(t):
                ob, ob_t = mx.pop(t)
                k = t % 2
                transposes(ob[:], 4, ob_t, ptb[k], ptb_t[k])
                ot_, ot_t_ = otr_.next()
                S.op(S.act, lambda: nc.scalar.copy(out=ot_[:], in_=ptb[k][:, 0:512]), reads=[ptb_t[k]], writes=[ot_t_])
                S.dma(S.sp, scr_o[t], ot_[:], ot_t_, reads=[ot_t_], writes=[scro_t[t]])
            pipeline(NT, [m_s0, m_s1, m_s2], [0, 2, 4])
            S.barrier()

        with ExitStack() as es:
            ra = mk_alloc(es, "right")
            rc_t = Trk("rconst")
            MT = ra("MT", [128, 512], F32)
            rt = ra("rt", [128, 12], F32)
            gn_b = ra("gn_b", [128, D], F32)
            S.dma(S.sp, MT[:], mt_tab[:, :], rc_t, writes=[rc_t])
            S.dma(S.sp, rt[:], rtab[:, :], rc_t, writes=[rc_t])
            S.dma(S.sp, gn_b[:], bcast_rows(ret_gn, D), rc_t, writes=[rc_t])
            RD = 5
            rcr = Ring(ra, "rc", [128, 128], F32, 3)
            rsr = Ring(ra, "rs_", [128, 128], F32, 3)
            tAr = Ring(ra, "rtA", [128, 512], F32, 2)
            tBr = Ring(ra, "rtB", [128, 512], F32, 2)
            kfr = Ring(ra, "kf", [128, 512], BF16, 3)
            kdr = Ring(ra, "kd", [128, 512], BF16, 3)
            vr = Ring(ra, "v", [128, 1024], BF16, 5)

            def rope(z, z_t, rc, rc_tk, rs, rs_tk, ring):
                tA, tA_t = tAr.next()
                tB, tB_t = tBr.next()
                of, of_t = ring.next()
                cb = rc[:].unsqueeze(1).to_broadcast([128, 4, 128])
                sb_ = rs[:].rearrange("p (a f) -> p a f", a=2).unsqueeze(1).to_broadcast([128, 4, 2, 64])
                S.op(S.dve, lambda: nc.vector.tensor_tensor(out=hv(tA[:], 4, 128), in0=hv(z[:], 4, 128), in1=cb, op=ALU.mult),
                     reads=[z_t, rc_tk], writes=[tA_t])
                S.op(S.dve, lambda: nc.vector.tensor_tensor(out=tB[:].rearrange("p (h a f) -> p h a f", h=4, a=2),
                                                            in0=swap_view(z[:], 4), in1=sb_, op=ALU.mult),
                     reads=[z_t, rs_tk], writes=[tB_t])
                S.op(S.pool, lambda: nc.gpsimd.tensor_tensor(out=of[:], in0=tA[:], in1=tB[:], op=ALU.add),
                     reads=[tA_t, tB_t], writes=[of_t])
                return of, of_t

            BK, BQ, BV0, BV1 = (pz[0], pz_t[0]), (pz[1], pz_t[1]), (pz[2], pz_t[2]), (ps[0], ps_t[0])
            BS = (ps[1], ps_t[1])
            rx = {}

            def mk_stages(hsrc, own, toff):
                def hT_(t, c):
                    return hsrc[:, c, t * 128:(t + 1) * 128]

                def r0(t):
                    rc, rc_tk = rcr.next()
                    rs, rs_tk = rsr.next()
                    S.dma(S.sp, rc[:], rcos[(toff + t) * 128:(toff + t + 1) * 128, :], rc_tk, writes=[rc_tk])
                    S.dma(S.sp, rs[:], rsin[(toff + t) * 128:(toff + t + 1) * 128, :], rs_tk, writes=[rs_tk])
                    mm_group(BK[0][:], [(hT_(t, c), Wr[:, c, 512:1024]) for c in range(8)], [Wr_t], BK[1])
                    if own:
                        mm_group(BQ[0][:], [(hT_(t, c), Wr[:, c, 0:512]) for c in range(8)], [Wr_t], BQ[1])
                    mm_group(BV0[0][:], [(hT_(t, c), Wr[:, c, 1024:1536]) for c in range(8)], [Wr_t], BV0[1])
                    mm_group(BV1[0][:], [(hT_(t, c), Wr[:, c, 1536:2048]) for c in range(8)], [Wr_t], BV1[1])
                    rx[(own, t)] = dict(tab=(rc, rc_tk, rs, rs_tk))

                def r1(t):
                    X = rx[(own, t)]
                    rc, rc_tk, rs, rs_tk = X["tab"]
                    v, v_t = vr.next()
                    S.op(S.act, lambda: nc.scalar.copy(out=v[:, 0:512], in_=BV0[0][:]), reads=[BV0[1]], writes=[v_t])
                    S.op(S.act, lambda: nc.scalar.copy(out=v[:, 512:1024], in_=BV1[0][:]), reads=[BV1[1]], writes=[v_t])
                    kf, kf_t = rope(BK[0], BK[1], rc, rc_tk, rs, rs_tk, kfr)
                    kd, kd_t = kdr.next()
                    S.op(S.pool, lambda: nc.gpsimd.tensor_tensor(out=hv(kd[:], 4, 128), in0=hv(kf[:], 4, 128),
                                                                 in1=rt[:, 0:4].unsqueeze(2).to_broadcast([128, 4, 128]), op=ALU.mult),
                         reads=[kf_t, rc_t], writes=[kd_t])
                    X.update(v=(v, v_t), kf=(kf, kf_t), kd=(kd, kd_t))
                    if own:
                        X["qf"] = rope(BQ[0], BQ[1], rc, rc_tk, rs, rs_tk, qfr)

                def upd(t, pr, bank):
                    X = rx[(own, t)]
                    kd, kd_t = X["kd"]
                    v, v_t = X["v"]

                    def kvmm():
                        ins = None
                        for hh in range(2):
                            h = 2 * pr + hh
                            ins = nc.tensor.matmul(bank[0][:, hh * 256:(hh + 1) * 256], kd[:, h * 128:(h + 1) * 128],
                                                   v[:, h * 256:(h + 1) * 256], start=True, stop=True)
                        return ins
                    S.op(S.pe, kvmm, reads=[kd_t, v_t], writes=[bank[1]])
                    for hh in range(2):
                        h = 2 * pr + hh
                        gC = float(GAMMAS[h] ** 128)
                        S.op(S.dve, lambda: nc.vector.scalar_tensor_tensor(
                            out=R32[:, h * 256:(h + 1) * 256], in0=R32[:, h * 256:(h + 1) * 256], scalar=gC,
                            in1=bank[0][:, hh * 256:(hh + 1) * 256], op0=ALU.mult, op1=ALU.add),
                            reads=[bank[1], R32_t], writes=[R32_t])
                return r0, r1, upd

            pes.close()

            with ExitStack() as es3:
                la = mk_alloc(es3, "left")
                qkTr = Ring(la, "qkT", [128, 1024], BF16, 4)
                osbr = Ring(la, "osb", [128, 1024], F32, 3)
                qfr = Ring(la, "qf", [128, 512], BF16, 3)
                Pr = Ring(la, "rP", [128, 512], BF16, 3)
                sglr = Ring(la, "sgl", [128, 1024], BF16, 4)
                rbr = Ring(la, "rb", [128, 1024], BF16, 2)
                Rbfr = Ring(la, "Rbf", [128, 1024], BF16, RD)
                S.op(S.pool, lambda: nc.gpsimd.tensor_copy(out=Rbfr.tiles[0][:], in_=R32[:]), reads=[R32_t], writes=[Rbfr.trks[0]])
                s4 = {k: Ring(la, "r4" + k, [128, 4], F32, 3) for k in ("ss", "t", "sd", "rr", "rstd")}
                o0, o1, oupd = mk_stages(hT_own, True, NT)

                def og(t):
                    sgl, sgl_t = sglr.next()
                    for hf, bk in ((0, BV0), (1, BV1)):
                        mm_group(bk[0][:], [(hT_own[:, c, t * 128:(t + 1) * 128], Wr[:, c, 2048 + hf * 512:2560 + hf * 512]) for c in range(8)],
                                 [Wr_t], bk[1])
                        S.op(S.act, lambda: nc.scalar.activation(out=sgl[:, hf * 512:(hf + 1) * 512], in_=bk[0][:], func=AF.Silu),
                             reads=[bk[1]], writes=[sgl_t])
                    S.op(S.pool, lambda: nc.gpsimd.tensor_tensor(out=sgl[:], in0=sgl[:], in1=gn_b[:], op=ALU.mult),
                         reads=[sgl_t, rc_t], writes=[sgl_t])
                    rx[(True, t)]["sgl"] = (sgl, sgl_t)
                    if t == NT - 1:
                        for k in range(4):
                            S.dma(S.pool, Wr[:, :, k * 512:(k + 1) * 512],
                                  w_in[:, 7680 + k * 512:7680 + (k + 1) * 512].rearrange("(c p) n -> p c n", p=128),
                                  wd_t, writes=[wd_t, Wr_t])
                        for k in range(2):
                            S.dma(S.pool, Wr[:, :, 2048 + k * 512:2048 + (k + 1) * 512],
                                  w_ro[:, k * 512:(k + 1) * 512].rearrange("(c p) n -> p c n", p=128), wd_t, writes=[wd_t, Wr_t])

                def o2(t):
                    X = rx[(True, t)]
                    qf, qf_t = X["qf"]
                    kf, kf_t = X["kf"]
                    qkT, qkT_t = qkTr.next()
                    transposes(qf[:], 4, qf_t, ptb[0], ptb_t[0], 0)
                    transposes(kf[:], 4, kf_t, ptb[0], ptb_t[0], 512)
                    S.op(S.act, lambda: nc.scalar.copy(out=qkT[:], in_=ptb[0]), reads=[ptb_t[0]], writes=[qkT_t])
                    X["qkT"] = (qkT, qkT_t)

                def o3(t):
                    X = rx[(True, t)]
                    qkT, qkT_t = X["qkT"]

                    def smm():
                        ins = None
                        for h in range(4):
                            ins = nc.tensor.matmul(BS[0][:, h * 128:(h + 1) * 128], qkT[:, 512 + h * 128:512 + (h + 1) * 128],
                                                   qkT[:, h * 128:(h + 1) * 128], start=True, stop=True)
                        return ins
                    S.op(S.pe, smm, reads=[qkT_t], writes=[BS[1]])
                    P, P_t = Pr.next()
                    S.op(S.dve, lambda: nc.vector.tensor_tensor(out=P[:], in0=BS[0][:], in1=MT[:], op=ALU.mult),
                         reads=[BS[1], rc_t], writes=[P_t])
                    X["P"] = (P, P_t)

                def o4(t):
                    X = rx[(True, t)]
                    qkT, qkT_t = X["qkT"]
                    P, P_t = X["P"]
                    v, v_t = X["v"]
                    Rbf, Rbf_t = Rbfr.tiles[t % RD], Rbfr.trks[t % RD]
                    for pr in range(2):
                        def omm():
                            ins = None
                            for hh in range(2):
                                h = 2 * pr + hh
                                o_ap = OB[pr][0][:, hh * 256:(hh + 1) * 256]
                                nc.tensor.matmul(o_ap, P[:, h * 128:(h + 1) * 128], v[:, h * 256:(h + 1) * 256], start=True, stop=False)
                                ins = nc.tensor.matmul(o_ap, qkT[:, h * 128:(h + 1) * 128], Rbf[:, h * 256:(h + 1) * 256],
                                                       start=False, stop=True)
                            return ins
                        S.op(S.pe, omm, reads=[P_t, v_t, qkT_t, Rbf_t], writes=[OB[pr][1]])

                def ou0(t):
                    oupd(t, 0, BS)

                def ou1(t):
                    oupd(t, 1, BS)
                    k = (t + 1) % RD
                    S.op(S.pool, lambda: nc.gpsimd.tensor_copy(out=Rbfr.tiles[k][:], in_=R32[:]), reads=[R32_t], writes=[Rbfr.trks[k]])

                def o5a(t):
                    X = rx[(True, t)]
                    osb, osb_t = osbr.next()
                    for pr in range(2):
                        S.op(S.act, lambda: nc.scalar.copy(out=osb[:, pr * 512:(pr + 1) * 512], in_=OB[pr][0][:]),
                             reads=[OB[pr][1]], writes=[osb_t])
                    X["osb"] = (osb, osb_t)

                def o5(t):
                    X = rx[(True, t)]
                    sgl, sgl_t = X["sgl"]
                    osb, osb_t = X["osb"]
                    ss4, ss4_t = s4["ss"].next()
                    t4, t4_t = s4["t"].next()
                    sd4, sd4_t = s4["sd"].next()
                    rr4, rr4_t = s4["rr"].next()
                    rstd4, rstd4_t = s4["rstd"].next()

                    def sq():
                        ins = None
                        for h in range(4):
                            ins = nc.scalar.activation(out=junk[:, h * 256:(h + 1) * 256], in_=osb[:, h * 256:(h + 1) * 256],
                                                       func=AF.Square, accum_out=ss4[:, h:h + 1])
                        return ins
                    junk, junk_t = junkr.next()
                    S.op(S.act, sq, reads=[osb_t], writes=[ss4_t, junk_t])
                    S.op(S.dve, lambda: nc.vector.tensor_tensor(out=t4[:], in0=ss4[:], in1=rt[:, 8:12], op=ALU.mult),
                         reads=[ss4_t, rc_t], writes=[t4_t])
                    if True:
                        rstd_pool(rr4[:], t4[:], 1.0, 4, t4_t, sd4[:], sd4_t, rr4_t)
                    else:
                        S.op(S.act, lambda: nc.scalar.activation(out=sd4[:], in_=t4[:], func=AF.Sqrt, bias=epsc[:, 0:1], scale=1.0),
                             reads=[t4_t, consts_t], writes=[sd4_t])
                        S.op(S.dve, lambda: nc.vector.reciprocal(out=rr4[:], in_=sd4[:]), reads=[sd4_t], writes=[rr4_t])
                    S.op(S.dve, lambda: nc.vector.tensor_tensor(out=rstd4[:], in0=rr4[:], in1=rt[:, 4:8], op=ALU.mult),
                         reads=[rr4_t, rc_t], writes=[rstd4_t])
                    rb, rb_t = rbr.next()
                    for h in range(4):
                        S.op(S.dve, lambda: nc.vector.scalar_tensor_tensor(
                            out=rb[:, h * 256:(h + 1) * 256], in0=osb[:, h * 256:(h + 1) * 256],
                            scalar=rstd4[:, h:h + 1], in1=sgl[:, h * 256:(h + 1) * 256], op0=ALU.mult, op1=ALU.mult),
                            reads=[osb_t, rstd4_t, sgl_t], writes=[rb_t])
                    X["rb"] = (rb, rb_t)

                def o6(t):
                    rb, rb_t = rx.pop((True, t))["rb"]
                    transposes(rb[:], 8, rb_t, ptb[0], ptb_t[0])
                    S.op(S.act, lambda: nc.scalar.copy(out=actT3[:, :, t * 128:(t + 1) * 128],
                                                       in_=ptb[0].rearrange("p (c t) -> p c t", c=8)),
                         reads=[ptb_t[0]], writes=[actT_t[t]])
                OB = [(po[0], po_t[0]), (po[1], po_t[1])]
                pipeline(NT, [o5, o6, ou1, o0, ou0, og, o2, o3, o4, o5a, o1], [7, 8, 2, 0, 2, 4, 3, 4, 5, 6, 1])
            S.barrier()
            if stop_after == "B":
                dbg_dump(actT[:], 16384, True, ra)
                finish()
                return nc


        wes = ExitStack()
        ges.enter_context(wes)
        wo = mk_alloc(wes, "right")("wo", [128, 8, 1024], BF16)
        wo_t = Trk("wo")
        with ExitStack() as es:
            ra = mk_alloc(es, "right")
            Wgl = Wr[:, :, 0:2048]
            wro = Wr[:, :, 2048:3072]
            wao = ra("wao", [128, 4, 1024], BF16)
            bg_b = ra("bg_b", [128, 2048], F32)
            wao_t = Trk("wao")
            bg_t = Trk("bg")
            S.dma(S.sp, bg_b[:], bcast_rows(b_gate, 2048), bg_t, writes=[bg_t])
            for k in range(2):
                S.dma(S.pool, wao[:, :, k * 512:(k + 1) * 512], w_ao[:, k * 512:(k + 1) * 512].rearrange("(c p) n -> p c n", p=128),
                      wao_t, writes=[wao_t])
            for k in range(2):
                S.dma(S.pool, wo[:, :, k * 512:(k + 1) * 512], w_o[:, k * 512:(k + 1) * 512].rearrange("(c p) n -> p c n", p=128),
                      wo_t, writes=[wo_t])
            t0r = Ring(ra, "t0", [128, 512], F32, 4)
            grr = Ring(ra, "gr", [128, 512], F32, 4)
            m1r = Ring(ra, "m1", [128, 512], F32, 2)
            m2r = Ring(ra, "m2", [128, 512], F32, 2)
            mr = Ring(ra, "m", [128, 1024], BF16, 2)
            oTr = Ring(ra, "oTl", [128, 512], BF16, 3)
            oTl = {}

            def load_oT(t):
                if t < NT:
                    a, a_t = oTr.next()
                    S.dma(S.sp, a[:], scr_o[t], a_t, reads=[scro_t[t]], writes=[a_t])
                    oTl[t] = (a, a_t)
            load_oT(0)
            load_oT(1)
            for t in range(NT):
                load_oT(t + 2)
                oTt, oTt_t = oTl.pop(t)
                m, m_t = mr.next()
                for hf in range(2):
                    c0 = hf * 512
                    gb = [(pz[0], pz_t[0]), (pz[1], pz_t[1])] if hf == 0 else [(pz[2], pz_t[2]), (po[0], po_t[0])]
                    mm_group(gb[0][0][:], [(hT_own[:, c, t * 128:(t + 1) * 128], Wgl[:, c, c0:c0 + 512]) for c in range(8)], [wd_t], gb[0][1])
                    mm_group(gb[1][0][:], [(hT_own[:, c, t * 128:(t + 1) * 128], Wgl[:, c, 1024 + c0:1024 + c0 + 512]) for c in range(8)],
                             [wd_t], gb[1][1])
                    rbk = (ps[0], ps_t[0]) if hf == 0 else (po[1], po_t[1])
                    mm_group(rbk[0][:], [(actT3[:, c, t * 128:(t + 1) * 128], wro[:, c, c0:c0 + 512]) for c in range(8)],
                             [wd_t, actT_t[t]], rbk[1])
                    mm_group(ps[1][:], [(oTt[:, c * 128:(c + 1) * 128], wao[:, c, c0:c0 + 512]) for c in range(4)],
                             [wao_t, oTt_t], ps_t[1])
                    gates = []
                    for which, bank in ((0, 0), (1, 1)):
                        t0, t0_t = t0r.next()
                        gr, gr_t = grr.next()
                        S.op(S.dve, lambda: nc.vector.tensor_tensor(out=t0[:], in0=gb[bank][0][:],
                                                                    in1=bg_b[:, which * 1024 + c0:which * 1024 + c0 + 512], op=ALU.add),
                             reads=[gb[bank][1], bg_t], writes=[t0_t])
                        S.op(S.act, lambda: nc.scalar.activation(out=gr[:], in_=t0[:], func=AF.Sigmoid), reads=[t0_t], writes=[gr_t])
                        gates.append((gr, gr_t))
                    m1, m1_t = m1r.next()
                    m2, m2_t = m2r.next()
                    S.op(S.dve, lambda: nc.vector.tensor_tensor(out=m1[:], in0=rbk[0][:], in1=gates[0][0][:], op=ALU.mult),
                         reads=[rbk[1], gates[0][1]], writes=[m1_t])
                    S.op(S.dve, lambda: nc.vector.tensor_tensor(out=m2[:], in0=ps[1][:], in1=gates[1][0][:], op=ALU.mult),
                         reads=[ps_t[1], gates[1][1]], writes=[m2_t])
                    S.op(S.pool, lambda: nc.gpsimd.tensor_tensor(out=m[:, c0:c0 + 512], in0=m1[:], in1=m2[:], op=ALU.add),
                         reads=[m1_t, m2_t], writes=[m_t])
                transposes(m[:], 8, m_t)
                S.op(S.act, lambda: nc.scalar.copy(out=actT3[:, :, t * 128:(t + 1) * 128], in_=pt[:].rearrange("p (c t) -> p c t", c=8)),
                     reads=[pt_t], writes=[actT_t[t]])
            S.barrier()

        les.close()
        with ExitStack() as es2:
            rb2 = mk_alloc(es2, "left")
            x1_all = rb2("x1_all", [128, NT, D], F32)
            x1_t = [Trk("x1_%d" % i) for i in range(NT)]
            tmpn = {"ss": Ring(rb2, "ss", [128, 1], F32, 2), "sd": Ring(rb2, "sd", [128, 1], F32, 2),
                    "rs": Ring(rb2, "rs", [128, 1], F32, 2), "xn": Ring(rb2, "xn", [128, D], BF16, 2)}
            with ExitStack() as es:
                ra = mk_alloc(es, "right")
                xt = Ring(ra, "xt2", [128, D], F32, 3)
                gmlp_b = ra("gmlp_b", [128, D], F32)
                gmlp_t = Trk("gmlp")
                S.dma(S.sp, gmlp_b[:], bcast_rows(g_mlp, D), gmlp_t, writes=[gmlp_t])
                xnE = Ring(ra, "xnE", [128, D], BF16, 3)
                ex = {}

                def e_s0(t):
                    ss, ss_t = tmpn["ss"].next()
                    sd, sd_t = tmpn["sd"].next()
                    rs, rs_t = tmpn["rs"].next()
                    xn, xn_t = xnE.next()
                    xa = x1_all[:, t, :]
                    junk, junk_t = junkr.next()
                    S.op(S.act, lambda: nc.scalar.activation(out=junk[:], in_=xa, func=AF.Square, accum_out=ss[:, 0:1]),
                         reads=[x1_t[t]], writes=[ss_t, junk_t])
                    rstd_pool(rs[:, 0:1], ss[:, 0:1], 1.0 / D, 1, ss_t, sd[:, 0:1], sd_t, rs_t)
                    S.op(S.dve, lambda: nc.vector.scalar_tensor_tensor(out=xn[:], in0=xa, scalar=rs[:, 0:1], in1=gmlp_b[:],
                                                                       op0=ALU.mult, op1=ALU.mult),
                         reads=[x1_t[t], rs_t, gmlp_t], writes=[xn_t])
                    ex[t] = (xn, xn_t)

                def e_s1(t):
                    xn, xn_t = ex.pop(t)
                    k = t % 2
                    transposes(xn[:], 8, xn_t, ptb[k], ptb_t[k])
                    S.op(S.act, lambda: nc.scalar.copy(out=actT3[:, :, t * 128:(t + 1) * 128],
                                                       in_=ptb[k].rearrange("p (c t) -> p c t", c=8)),
                         reads=[ptb_t[k]], writes=[actT_t[t]])
                dx = {}

                def d_s0(t):
                    xa, xa_t = xt.next()
                    S.dma(S.sp, xa[:], x_own[t * 128:(t + 1) * 128, :], xa_t, writes=[xa_t])
                    for hf in range(2):
                        c0 = hf * 512
                        mm_group(pz[hf][:], [(actT3[:, c, t * 128:(t + 1) * 128], wo[:, c, c0:c0 + 512]) for c in range(8)],
                                 [wo_t, actT_t[t]], pz_t[hf])
                    dx[t] = (xa, xa_t)

                def d_s1(t):
                    xa, xa_t = dx.pop(t)
                    for hf in range(2):
                        c0 = hf * 512
                        S.op(S.dve, lambda: nc.vector.tensor_tensor(out=x1_all[:, t, c0:c0 + 512], in0=pz[hf][:], in1=xa[:, c0:c0 + 512], op=ALU.add),
                             reads=[pz_t[hf], xa_t], writes=[x1_t[t]])
                pipeline(NT, [d_s0, e_s0, e_s1, d_s1], [0, 2, 4, 1])
                S.barrier()
                if stop_after == "D":
                    dbg_dump(x1_all[:].rearrange("p t d -> p (t d)"), 16384, False, ra)
                    finish()
                    return nc

            wes.close()
            bes.close()
            fes = ExitStack()
            ges.enter_context(fes)
            fa = mk_alloc(fes, "right")
            gple_b = fa("gple_b", [128, D], F32)
            wpg = fa("wpg", [128, 8, 1024], BF16)
            wpp = fa("wpp", [128, 2, 1024], BF16)
            wf_t = Trk("wf")
            with ExitStack() as es:
                ra = mk_alloc(es, "right")
                NF = 8
                FW = 4096 // NF
                NJ = FW // 128
                wur = Ring(ra, "wu", [128, 8, FW], BF16, 2)
                wdr = Ring(ra, "wdn", [128, NJ, D], BF16, 2)
                aTr = Ring(ra, "aT", [128, NJ, TOK], BF16, 2)
                r1r = Ring(ra, "rl1", [128, 512], F32, 2)
                for fg in range(NF):
                    wu, wu_t = wur.next()
                    wdn, wdn_t = wdr.next()
                    aT, aT_t0 = aTr.next()
                    aT_tg = [Trk("aTtg%d" % i) for i in range(4)]
                    S.dma(S.pool, wu[:], w_up[:, fg * FW:(fg + 1) * FW].rearrange("(c p) n -> p c n", p=128), wu_t, writes=[wu_t])
                    S.dma(S.pool, wdn[:], w_down[fg * FW:(fg + 1) * FW, :].rearrange("(j p) n -> p j n", p=128), wdn_t, writes=[wdn_t])
                    if fg == 2:
                        S.dma(S.sp, gple_b[:], bcast_rows(g_ple, D), wf_t, writes=[wf_t])
                        for k in range(2):
                            S.dma(S.pool, wpg[:, :, k * 512:(k + 1) * 512], w_pg[:, k * 512:(k + 1) * 512].rearrange("(c p) n -> p c n", p=128),
                                  wf_t, writes=[wf_t])
                        S.dma(S.pool, wpp[:], w_pp[:, :].rearrange("(c p) n -> p c n", p=128), wf_t, writes=[wf_t])
                    k = 0
                    for tg in range(4):
                        for j in range(NJ):
                            bank = k % 3
                            k += 1
                            mm_group(pz[bank][:], [(wu[:, c, j * 128:(j + 1) * 128], actT3[:, c, tg * 512:(tg + 1) * 512]) for c in range(8)],
                                     [wu_t] + actT_t[tg * 4:(tg + 1) * 4], pz_t[bank])
                            r1, r1_t = r1r.next()
                            S.op(S.act, lambda: nc.scalar.activation(out=r1[:], in_=pz[bank][:], func=AF.Relu), reads=[pz_t[bank]], writes=[r1_t])
                            S.op(S.pool, lambda: nc.gpsimd.tensor_tensor(out=aT[:, j, tg * 512:(tg + 1) * 512], in0=r1[:], in1=r1[:], op=ALU.mult),
                                 reads=[r1_t], writes=[aT_tg[tg], aT_t0] if (tg, j) == (0, 0) else [aT_tg[tg]])
                    for t in range(NT):
                        for hf in range(2):
                            c0 = hf * 512
                            mm_group(ps[hf][:], [(aT[:, j, t * 128:(t + 1) * 128], wdn[:, j, c0:c0 + 512]) for j in range(NJ)],
                                     [aT_tg[t // 4], aT_t0, wdn_t], ps_t[hf])
                            S.op(S.dve, lambda: nc.vector.tensor_tensor(out=x1_all[:, t, c0:c0 + 512], in0=ps[hf][:], in1=x1_all[:, t, c0:c0 + 512], op=ALU.add),
                                 reads=[ps_t[hf], x1_t[t]], writes=[x1_t[t]])
                S.barrier()
                if stop_after == "E":
                    dbg_dump(x1_all[:].rearrange("p t d -> p (t d)"), 16384, False, ra)
                    finish()
                    return nc

            with ExitStack() as es:
                ra = mk_alloc(es, "right")
                h3r = Ring(ra, "h3T", [128, 8, 128], BF16, 4)
                pbr = Ring(ra, "pb", [128, 256], BF16, 4)
                pTr = Ring(ra, "pT", [128, 256], BF16, 4)
                gpr = Ring(ra, "gp", [128, 512], F32, 2)
                tfr = Ring(ra, "tf", [128, 512], F32, 2)
                otr = Ring(ra, "ot", [128, D], F32, 2)
                tmpn["xn"] = Ring(ra, "xnF", [128, D], BF16, 4)
                fx = {}

                def f_s0(t):
                    ss, ss_t = tmpn["ss"].next()
                    sd, sd_t = tmpn["sd"].next()
                    rs, rs_t = tmpn["rs"].next()
                    xn, xn_t = tmpn["xn"].next()
                    xa = x1_all[:, t, :]
                    junk, junk_t = junkr.next()
                    S.op(S.act, lambda: nc.scalar.activation(out=junk[:], in_=xa, func=AF.Square, accum_out=ss[:, 0:1]),
                         reads=[x1_t[t]], writes=[ss_t, junk_t])
                    rstd_pool(rs[:, 0:1], ss[:, 0:1], 1.0 / D, 1, ss_t, sd[:, 0:1], sd_t, rs_t)
                    S.op(S.dve, lambda: nc.vector.scalar_tensor_tensor(out=xn[:], in0=xa, scalar=rs[:, 0:1], in1=gple_b[:],
                                                                       op0=ALU.mult, op1=ALU.mult),
                         reads=[x1_t[t], rs_t, wf_t], writes=[xn_t])
                    pb, pb_t = pbr.next()
                    S.dma(S.pool, pb[:], p_own[t * 128:(t + 1) * 128, :], pb_t, writes=[pb_t])
                    fx[t] = dict(xn=(xn, xn_t), pb=(pb, pb_t))

                def f_s1(t):
                    xn, xn_t = fx[t]["xn"]
                    pb, pb_t = fx[t]["pb"]
                    h3, h3_t = h3r.next()
                    pT, pT_t = pTr.next()
                    transposes(xn[:], 8, xn_t, ptb[0], ptb_t[0])
                    S.op(S.dve, lambda: nc.vector.tensor_copy(out=h3[:], in_=ptb[0].rearrange("p (c t) -> p c t", c=8)),
                         reads=[ptb_t[0]], writes=[h3_t])
                    transposes(pb[:], 2, pb_t, ptb[1], ptb_t[1])
                    S.op(S.dve, lambda: nc.vector.tensor_copy(out=pT[:], in_=ptb[1][:, 0:256]), reads=[ptb_t[1]], writes=[pT_t])
                    fx[t] = dict(h3=(h3, h3_t), pT=(pT, pT_t))

                fbank = [(pz[0], pz_t[0]), (pz[1], pz_t[1])]
                fbank2 = [(ps[0], ps_t[0]), (ps[1], ps_t[1])]

                def f_s2(t):
                    h3, h3_t = fx[t]["h3"]
                    pT, pT_t = fx[t]["pT"]
                    for hf in range(2):
                        c0 = hf * 512
                        mm_group(fbank[hf][0][:], [(h3[:, c, :], wpg[:, c, c0:c0 + 512]) for c in range(8)], [h3_t, wf_t], fbank[hf][1])
                        mm_group(fbank2[hf][0][:], [(pT[:, c * 128:(c + 1) * 128], wpp[:, c, c0:c0 + 512]) for c in range(2)], [pT_t, wf_t], fbank2[hf][1])

                def f_s3(t):
                    fx.pop(t)
                    ot, ot_t = otr.next()
                    for hf in range(2):
                        c0 = hf * 512
                        gp, gp_t = gpr.next()
                        tf, tf_t = tfr.next()
                        S.op(S.act, lambda: nc.scalar.activation(out=gp[:], in_=fbank[hf][0][:], func=AF.Sigmoid), reads=[fbank[hf][1]], writes=[gp_t])
                        S.op(S.dve, lambda: nc.vector.tensor_tensor(out=tf[:], in0=fbank2[hf][0][:], in1=gp[:], op=ALU.mult),
                             reads=[fbank2[hf][1], gp_t], writes=[tf_t])
                        S.op(S.pool, lambda: nc.gpsimd.tensor_tensor(out=ot[:, c0:c0 + 512], in0=tf[:], in1=x1_all[:, t, c0:c0 + 512], op=ALU.add),
                             reads=[tf_t, x1_t[t]], writes=[ot_t])
                    S.dma(S.sp, y_out[t * 128:(t + 1) * 128, :], ot[:], ot_t, reads=[ot_t])
                pipeline(NT, [f_s0, f_s2, f_s1, f_s3], [0, 4, 2, 5])
                finish()
    return nc


def _tables(half):
    f32 = np.float32
    pos = (np.arange(4096, dtype=np.int64) + (half * 2048 - 2048)).astype(np.float64)
    ret_freq = 1.0 / np.power(10000.0, np.linspace(0.0, 1.0, 64, dtype=np.float64))
    rope_freq = np.power(10000.0, -np.arange(0, 128, 2, dtype=np.float64) / 128.0)

    def cs(freq):
        ang = pos[:, None] * freq[None, :]
        c = np.cos(ang).astype(f32)
        s = np.sin(ang).astype(f32)
        return np.concatenate([c, c], 1), np.concatenate([-s, s], 1)
    rc, rs = cs(ret_freq)
    ac, as_ = cs(rope_freq)
    idx = np.arange(128, dtype=np.float64)
    lg = np.log1p(-np.exp2(-5.0 - np.arange(4, dtype=np.float64)))
    dk = 128.0 ** -0.5
    mt = np.zeros((128, 4, 128), np.float64)
    for h in range(4):
        col = dk * np.exp(-lg[h] * (idx + 1.0))
        mt[:, h, :] = col[:, None] * (idx[None, :] >= idx[:, None])
    rtab = np.zeros((128, 12), np.float64)
    for h in range(4):
        rtab[:, h] = dk * np.exp(lg[h] * (127.0 - idx))
        rtab[:, 4 + h] = np.exp(lg[h] * (idx + 1.0))
        rtab[:, 8 + h] = np.exp(2.0 * lg[h] * (idx + 1.0)) / 256.0
    k = idx[:, None]
    q = idx[None, :]
    mask = np.concatenate([(k >= q), (k <= q)], 1).astype(f32)
    return dict(rcos=rc, rsin=rs, acos=ac, asin=as_, mt_tab=mt.reshape(128, 512).astype(f32), rtab=rtab.astype(f32),
                mask01=mask, ident=np.eye(128, dtype=f32), flag=np.full((128, 4), float(half), f32))


_NC_CACHE = {}


def _get_nc(stop_after=None):
    if stop_after not in _NC_CACHE:
        _NC_CACHE[stop_after] = build_program(stop_after)
    return _NC_CACHE[stop_after]


def make_in_maps(x, p, w_in, b_gate, g_mix, q_gain, k_gain, ret_gn, w_ret_out, w_att_out, w_o,
                 g_mlp, w_up, w_down, g_ple, w_ple_proj, w_ple_gate):
    f = lambda a: np.ascontiguousarray(np.asarray(a, dtype=np.float32))
    x = f(x)
    p = f(p)
    shared = dict(
        w_in=f(w_in[0]), w_ro=f(w_ret_out[0]), w_ao=f(w_att_out[0]), w_o=f(w_o[0]), w_up=f(w_up[0]), w_down=f(w_down[0]),
        w_pp=f(w_ple_proj[0]), w_pg=f(w_ple_gate[0]), g_mix=f(g_mix[0]).reshape(1, -1), g_mlp=f(g_mlp[0]).reshape(1, -1),
        g_ple=f(g_ple[0]).reshape(1, -1), b_gate=f(b_gate[0]).reshape(1, -1), q_gain=f(q_gain[0]).reshape(1, -1),
        k_gain=f(k_gain[0]).reshape(1, -1), ret_gn=f(ret_gn[0]).reshape(1, -1))
    tabs = [_tables(0), _tables(1)]
    in_maps = []
    for core in range(8):
        b, half = core // 2, core % 2
        m = dict(shared)
        m["x_own"] = np.ascontiguousarray(x[b, half * 2048:(half + 1) * 2048])
        m["x_pre"] = np.ascontiguousarray(x[b, 0:2048]) if half == 1 else np.zeros((2048, 1024), np.float32)
        m["p_own"] = np.ascontiguousarray(p[0, b, half * 2048:(half + 1) * 2048])
        m.update(tabs[half])
        in_maps.append(m)
    return in_maps


def kernel(**inputs):
    in_maps = make_in_maps(**inputs)
    nc = _get_nc(None)
    res = run_bass_kernel_spmd(nc, in_maps, core_ids=list(range(8)))
    out = np.zeros((4, 4096, 1024), np.float32)
    for core in range(8):
        b, half = core // 2, core % 2
        out[b, half * 2048:(half + 1) * 2048] = res.results[core]["y"]
    return out
```

```python
import os
import numpy as np
import concourse.bass as bass
import concourse.mybir as mybir
from concourse.bass_utils import run_bass_kernel_spmd
from contextlib import ExitStack

F32 = mybir.dt.float32
BF16 = mybir.dt.bfloat16
AF = mybir.ActivationFunctionType
ALU = mybir.AluOpType

EPS = 1e-6
D = 1024
TOK = 2048
NT = 16
GAMMAS = [1.0 - 2.0 ** (-5.0 - h) for h in range(4)]
ATT_SCALE = 128.0 ** -0.5


class SemObj:
    def __init__(self, sem, step):
        self.sem = sem
        self.val = 0
        self.step = step


class Trk:
    __slots__ = ("name", "w", "r", "dsem")

    def __init__(self, name):
        self.name = name
        self.w = None
        self.r = {}
        self.dsem = None


class Eng:
    def __init__(self, name, h, prog):
        self.name = name
        self.h = h
        self.prog = prog
        self.seen = {}


class Sched:
    def __init__(self, nc, es):
        self.nc = nc
        self.es = es
        self.nsem = 0
        self.pe = Eng("pe", nc.tensor, self._sem("pe", 1))
        self.act = Eng("act", nc.scalar, self._sem("act", 1))
        self.dve = Eng("dve", nc.vector, self._sem("dve", 1))
        self.pool = Eng("pool", nc.gpsimd, self._sem("pool", 1))
        self.sp = Eng("sp", nc.sync, None)
        self.engs = [self.pe, self.act, self.dve, self.pool, self.sp]
        self.dsems = []

    def _sem(self, name, step):
        self.nsem += 1
        s = self.es.enter_context(self.nc.semaphore("s_%s_%d" % (name, self.nsem)))
        return SemObj(s, step)

    def _deps(self, reads, writes):
        deps = {}

        def add(so, v):
            if deps.get(so, 0) < v:
                deps[so] = v
        for t in reads:
            if t.w is not None:
                add(*t.w)
        for t in writes:
            if t.w is not None:
                add(*t.w)
            for so, v in t.r.items():
                add(so, v)
        return deps

    def _wait(self, eng, deps):
        for so, v in deps.items():
            if so is eng.prog and eng is self.pe:
                continue
            if eng.seen.get(so, 0) >= v:
                continue
            eng.h.wait_ge(so.sem, v)
            eng.seen[so] = v

    def _mark(self, ev, reads, writes):
        so, v = ev
        for t in reads:
            if t.r.get(so, 0) < v:
                t.r[so] = v
        for t in writes:
            t.w = ev
            t.r = {}

    def op(self, eng, fn, reads=(), writes=()):
        self._wait(eng, self._deps(reads, writes))
        ins = fn()
        eng.prog.val += 1
        ins.then_inc(eng.prog.sem, 1)
        self._mark((eng.prog, eng.prog.val), reads, writes)

    def dma(self, eng, out, in_, owner, reads=(), writes=()):
        if owner.dsem is None:
            owner.dsem = {}
        kind = "sw" if eng is self.pool else "hw"
        if kind not in owner.dsem:
            owner.dsem[kind] = self._sem("d", 16)
            self.dsems.append(owner.dsem[kind])
        so = owner.dsem[kind]
        deps = self._deps(reads, writes)
        if so in deps and all((t.w is None or t.w[0] is so) and not t.r for t in writes) \
                and all(t.w is None or t.w[0] is not so for t in reads):
            del deps[so]
        self._wait(eng, deps)
        ins = eng.h.dma_start(out=out, in_=in_)
        so.val += 16
        ins.then_inc(so.sem, 16)
        self._mark((so, so.val), reads, writes)

    def barrier(self):
        sems = [e.prog for e in self.engs if e.prog is not None and e.prog.val > 0]
        sems += [s for s in self.dsems if s.val > 0]
        for e in self.engs:
            self._wait(e, {so: so.val for so in sems})


class Ring:
    def __init__(self, alloc, name, shape, dtype, n):
        self.tiles = [alloc("%s_%d" % (name, i), shape, dtype) for i in range(n)]
        self.trks = [Trk("%s_%d" % (name, i)) for i in range(n)]
        self.i = 0
        self.n = n

    def next(self):
        k = self.i % self.n
        self.i += 1
        return self.tiles[k], self.trks[k]


def swap_view(ap, nh):
    a = ap.ap
    return bass.AP(ap.tensor, ap.offset + 64, [list(a[0]), [128, nh], [-64, 2], [1, 64]])


def hv(ap, nh, w):
    return ap.rearrange("p (h w) -> p h w", h=nh, w=w)


def build_program(stop_after=None):
    nc = bass.Bass("TRN2", target_bir_lowering=False)

    def din(name, shape):
        return nc.dram_tensor(name, shape, F32, kind="ExternalInput").ap()

    x_own = din("x_own", [TOK, D])
    x_pre = din("x_pre", [TOK, D])
    p_own = din("p_own", [TOK, 256])
    w_in = din("w_in", [D, 9728])
    w_ro = din("w_ro", [1024, D])
    w_ao = din("w_ao", [512, D])
    w_o = din("w_o", [D, D])
    w_up = din("w_up", [D, 4096])
    w_down = din("w_down", [4096, D])
    w_pp = din("w_pp", [256, D])
    w_pg = din("w_pg", [D, D])
    g_mix = din("g_mix", [1, D])
    g_mlp = din("g_mlp", [1, D])
    g_ple = din("g_ple", [1, D])
    b_gate = din("b_gate", [1, 2048])
    q_gain = din("q_gain", [1, 384])
    k_gain = din("k_gain", [1, 384])
    ret_gn = din("ret_gn", [1, D])
    rcos = din("rcos", [4096, 128])
    rsin = din("rsin", [4096, 128])
    acos = din("acos", [4096, 128])
    asin = din("asin", [4096, 128])
    mt_tab = din("mt_tab", [128, 512])
    rtab = din("rtab", [128, 12])
    mask01 = din("mask01", [128, 256])
    ident_in = din("ident", [128, 128])
    flag_in = din("flag", [128, 4])
    y_out = nc.dram_tensor("y", [TOK, D], F32, kind="ExternalOutput").ap()
    scr = nc.dram_tensor("scr_att", [3, TOK, 528], F32, kind="Internal").ap()
    scr_o = nc.dram_tensor("scr_o", [NT, 128, 512], BF16, kind="Internal").ap()
    dbg = None
    if stop_after is not None:
        dbg = nc.dram_tensor("dbg", [128, 16384], F32, kind="ExternalOutput").ap()

    def bcast_rows(ap, n):
        return bass.AP(ap.tensor, ap.offset, [[0, 128], [1, n]])

    ges = ExitStack()
    with ges:
        S = Sched(nc, ges)

        uniq = [0]

        def mk_alloc(es, side):
            def alloc(name, shape, dtype):
                uniq[0] += 1
                return es.enter_context(nc.sbuf_tensor("sb%d_%s" % (uniq[0], name), shape, dtype, side=side))
            return alloc
        galloc = mk_alloc(ges, "left")

        def palloc(name, shape, dtype):
            return ges.enter_context(nc.psum_tensor(name, shape, dtype))

        pz = [palloc("pz%d" % i, [128, 512], F32) for i in range(3)]
        pz_t = [Trk("pz%d" % i) for i in range(3)]
        pt = palloc("pt", [128, 1024], BF16)
        pt_t = Trk("pt")
        ps = [palloc("ps%d" % i, [128, 512], F32) for i in range(2)]
        ps_t = [Trk("ps%d" % i) for i in range(2)]
        po = [palloc("po%d" % i, [128, 512], F32) for i in range(2)]
        po_t = [Trk("po%d" % i) for i in range(2)]

        consts_t = Trk("consts")
        ident = galloc("ident", [128, 128], BF16)
        msk = galloc("msk", [128, 256], BF16)
        flag = galloc("flag", [128, 4], BF16)
        epsc = galloc("epsc", [128, 1], F32)
        junkr = Ring(galloc, "junk", [128, 1024], BF16, 2)
        S.dma(S.pool, ident[:], ident_in[:, :], consts_t, writes=[consts_t])
        S.dma(S.pool, msk[:], mask01[:, :], consts_t, writes=[consts_t])
        S.dma(S.pool, flag[:], flag_in[:, :], consts_t, writes=[consts_t])
        S.op(S.pool, lambda: nc.gpsimd.memset(epsc[:], EPS), writes=[consts_t])
        nhalf = galloc("nhalf", [128, 4], F32)
        S.op(S.pool, lambda: nc.gpsimd.memset(nhalf[:], -0.5), writes=[consts_t])

        def rstd_pool(out_ap, in_ap, scale, n, in_t, mid, mid_t, out_t):
            S.op(S.pool, lambda: nc.gpsimd.tensor_scalar(out=mid, in0=in_ap, scalar1=float(scale), scalar2=EPS,
                                                         op0=ALU.mult, op1=ALU.add),
                 reads=[in_t], writes=[mid_t])
            S.op(S.pool, lambda: nc.gpsimd.tensor_tensor(out=out_ap, in0=mid, in1=nhalf[:, 0:n], op=ALU.pow),
                 reads=[mid_t, consts_t], writes=[out_t])

        actT = galloc("actT", [128, 8 * TOK], BF16)
        actT3 = actT[:].rearrange("p (c t) -> p c t", c=8)
        actT_t = [Trk("actT%d" % i) for i in range(NT)]
        les = ExitStack()
        ges.enter_context(les)
        lalloc = mk_alloc(les, "left")
        hT_own = lalloc("hT_own", [128, 8, TOK], BF16)
        pes = ExitStack()
        ges.enter_context(pes)
        hT_pre = mk_alloc(pes, "left")("hT_pre", [128, 8, TOK], BF16)

        def mm_group(out_ap, pairs, reads, out_trk):
            def fn():
                n = len(pairs)
                ins = None
                for i, (l, r) in enumerate(pairs):
                    ins = nc.tensor.matmul(out_ap, l, r, start=(i == 0), stop=(i == n - 1))
                return ins
            S.op(S.pe, fn, reads=reads, writes=[out_trk])

        def transposes(src_ap, nblk, src_trk, dst=None, dst_t=None, off=0):
            if dst is None:
                dst, dst_t = pt[:], pt_t

            def fn():
                ins = None
                for k in range(nblk):
                    ins = nc.tensor.transpose(dst[:, off + k * 128:off + (k + 1) * 128], src_ap[:, k * 128:(k + 1) * 128], ident[:])
                return ins
            S.op(S.pe, fn, reads=[src_trk, consts_t], writes=[dst_t])

        def pipeline(n, stages, skews=None):
            skews = skews or list(range(len(stages)))
            for step in range(n + max(skews)):
                for f, sk in reversed(list(zip(stages, skews))):
                    i = step - sk
                    if 0 <= i < n:
                        f(i)

        ptb = [pt[:], po[1][:].bitcast(BF16)]
        ptb_t = [pt_t, po_t[1]]

        def norm_transpose(tmp, x_ap, x_trk, g_b, g_trk, dst_ap, dst_trk):
            ss, ss_t = tmp["ss"].next()
            sd, sd_t = tmp["sd"].next()
            rs, rs_t = tmp["rs"].next()
            xn, xn_t = tmp["xn"].next()
            junk, junk_t = junkr.next()
            S.op(S.act, lambda: nc.scalar.activation(out=junk[:], in_=x_ap, func=AF.Square, accum_out=ss[:, 0:1]),
                 reads=[x_trk], writes=[ss_t, junk_t])
            rstd_pool(rs[:, 0:1], ss[:, 0:1], 1.0 / D, 1, ss_t, sd[:, 0:1], sd_t, rs_t)
            S.op(S.dve, lambda: nc.vector.scalar_tensor_tensor(out=xn[:], in0=x_ap, scalar=rs[:, 0:1], in1=g_b[:],
                                                               op0=ALU.mult, op1=ALU.mult),
                 reads=[x_trk, rs_t, g_trk], writes=[xn_t])
            transposes(xn[:], 8, xn_t)
            S.op(S.act, lambda: nc.scalar.copy(out=dst_ap, in_=pt[:].rearrange("p (c t) -> p c t", c=8)),
                 reads=[pt_t], writes=[dst_trk])

        def dbg_dump(ap_bf16_or_f32, ncols, is_bf16, alloc):
            t_ = Trk("dbgt")
            CH = 2048
            stg = alloc("dbg_stg", [128, CH], F32)
            for c0 in range(0, ncols, CH):
                w = min(CH, ncols - c0)
                S.op(S.dve, lambda: nc.vector.tensor_copy(stg[:, 0:w], ap_bf16_or_f32[:, c0:c0 + w]), writes=[t_])
                S.dma(S.sp, dbg[:, c0:c0 + w], stg[:, 0:w], t_, reads=[t_])
            S.barrier()

        def finish():
            S.barrier()
            for so in S.dsems:
                S.sp.h.wait_ge(so.sem, so.val)

        res = ExitStack()
        ges.enter_context(res)
        R32 = mk_alloc(res, "right")("R32", [128, 1024], F32)
        R32_t = Trk("R32")
        S.op(S.pool, lambda: nc.gpsimd.memset(R32[:], 0.0), writes=[R32_t])
        aes = ExitStack()
        ges.enter_context(aes)
        aa = mk_alloc(aes, "right")
        wK = aa("wK", [128, 8, 512], BF16)
        wV = aa("wV", [128, 8, 512], BF16)
        wQ = aa("wQ", [128, 8, 512], BF16)
        wK_t, wV_t, wQ_t = Trk("wK"), Trk("wV"), Trk("wQ")
        with ExitStack() as es:
            ra = mk_alloc(es, "right")
            gmix_b = ra("gmix_b", [128, D], F32)
            gmix_t = Trk("gmix")
            S.dma(S.sp, gmix_b[:], bcast_rows(g_mix, D), gmix_t, writes=[gmix_t])
            tmp = {"ss": Ring(ra, "ss", [128, 1], F32, 2), "sd": Ring(ra, "sd", [128, 1], F32, 2),
                   "rs": Ring(ra, "rs", [128, 1], F32, 2), "xn": Ring(ra, "xn", [128, D], BF16, 2)}
            xt = Ring(ra, "xt", [128, D], F32, 3)
            tmp["xn"] = Ring(ra, "xn3", [128, D], BF16, 3)
            WrKV = ra("WrKV", [128, 8, 1536], BF16)
            WrKV_t = Trk("WrKV")
            for k in range(3):
                S.dma(S.pool, WrKV[:, :, k * 512:(k + 1) * 512],
                      w_in[:, 512 + k * 512:512 + (k + 1) * 512].rearrange("(c p) n -> p c n", p=128), WrKV_t, writes=[WrKV_t])
            for dst_, dst_t_, col0_ in ((wK, wK_t, 4608 + 2 * 512), (wV, wV_t, 6144 + 2 * 512), (wQ, wQ_t, 3072 + 2 * 512)):
                S.dma(S.pool, dst_[:], w_in[:, col0_:col0_ + 512].rearrange("(c p) n -> p c n", p=128), dst_t_, writes=[dst_t_])
            prt = ra("prt", [128, 12], F32)
            prt_t = Trk("prt")
            S.dma(S.sp, prt[:], rtab[:, :], prt_t, writes=[prt_t])
            prc = Ring(ra, "prc", [128, 128], F32, 3)
            prs = Ring(ra, "prs", [128, 128], F32, 3)
            ptA = Ring(ra, "ptA", [128, 512], F32, 2)
            ptB = Ring(ra, "ptB", [128, 512], F32, 2)
            pkf = Ring(ra, "pkf", [128, 512], BF16, 3)
            pkd = Ring(ra, "pkd", [128, 512], BF16, 3)
            pvr = Ring(ra, "pv", [128, 1024], BF16, 5)
            PBK, PBV0, PBV1 = (pz[0], pz_t[0]), (pz[2], pz_t[2]), (ps[0], ps_t[0])
            PKV = [(ps[1], ps_t[1]), (po[0], po_t[0])]
            px = {}

            def pp0(t):
                rc, rc_tk = prc.next()
                rs_, rs_tk = prs.next()
                S.dma(S.sp, rc[:], rcos[t * 128:(t + 1) * 128, :], rc_tk, writes=[rc_tk])
                S.dma(S.sp, rs_[:], rsin[t * 128:(t + 1) * 128, :], rs_tk, writes=[rs_tk])
                for bk, c0 in ((PBK, 0), (PBV0, 512), (PBV1, 1024)):
                    mm_group(bk[0][:], [(hT_pre[:, c, t * 128:(t + 1) * 128], WrKV[:, c, c0:c0 + 512]) for c in range(8)],
                             [WrKV_t, hpre_tt[t]], bk[1])
                px[t] = dict(tab=(rc, rc_tk, rs_, rs_tk))

            def pp1(t):
                X = px[t]
                rc, rc_tk, rs_, rs_tk = X["tab"]
                v, v_t = pvr.next()
                S.op(S.act, lambda: nc.scalar.copy(out=v[:, 0:512], in_=PBV0[0][:]), reads=[PBV0[1]], writes=[v_t])
                S.op(S.act, lambda: nc.scalar.copy(out=v[:, 512:1024], in_=PBV1[0][:]), reads=[PBV1[1]], writes=[v_t])
                tA, tA_t = ptA.next()
                tB, tB_t = ptB.next()
                kf, kf_t = pkf.next()
                kd, kd_t = pkd.next()
                z = PBK[0]
                cb = rc[:].unsqueeze(1).to_broadcast([128, 4, 128])
                sb_ = rs_[:].rearrange("p (a f) -> p a f", a=2).unsqueeze(1).to_broadcast([128, 4, 2, 64])
                S.op(S.dve, lambda: nc.vector.tensor_tensor(out=hv(tA[:], 4, 128), in0=hv(z[:], 4, 128), in1=cb, op=ALU.mult),
                     reads=[PBK[1], rc_tk], writes=[tA_t])
                S.op(S.dve, lambda: nc.vector.tensor_tensor(out=tB[:].rearrange("p (h a f) -> p h a f", h=4, a=2),
                                                            in0=swap_view(z[:], 4), in1=sb_, op=ALU.mult),
                     reads=[PBK[1], rs_tk], writes=[tB_t])
                S.op(S.pool, lambda: nc.gpsimd.tensor_tensor(out=kf[:], in0=tA[:], in1=tB[:], op=ALU.add),
                     reads=[tA_t, tB_t], writes=[kf_t])
                S.op(S.pool, lambda: nc.gpsimd.tensor_tensor(out=hv(kd[:], 4, 128), in0=hv(kf[:], 4, 128),
                                                             in1=prt[:, 0:4].unsqueeze(2).to_broadcast([128, 4, 128]), op=ALU.mult),
                     reads=[kf_t, prt_t], writes=[kd_t])
                X.update(v=(v, v_t), kd=(kd, kd_t))

            def ppu(t, pr):
                X = px[t]
                kd, kd_t = X["kd"]
                v, v_t = X["v"]
                bank = PKV[pr]

                def kvmm():
                    ins = None
                    for hh in range(2):
                        h = 2 * pr + hh
                        ins = nc.tensor.matmul(bank[0][:, hh * 256:(hh + 1) * 256], kd[:, h * 128:(h + 1) * 128],
                                               v[:, h * 256:(h + 1) * 256], start=True, stop=True)
                    return ins
                S.op(S.pe, kvmm, reads=[kd_t, v_t], writes=[bank[1]])
                for hh in range(2):
                    h = 2 * pr + hh
                    gC = float(GAMMAS[h] ** 128)
                    S.op(S.dve, lambda: nc.vector.scalar_tensor_tensor(
                        out=R32[:, h * 256:(h + 1) * 256], in0=R32[:, h * 256:(h + 1) * 256], scalar=gC,
                        in1=bank[0][:, hh * 256:(hh + 1) * 256], op0=ALU.mult, op1=ALU.add),
                        reads=[bank[1], R32_t], writes=[R32_t])
                if pr == 1:
                    px.pop(t)

            def pre_only(f, *a):
                return lambda t: f(t, *a) if t < NT else None
            hpre_tt = [Trk("hpre%d" % i) for i in range(NT)]
            hown_t = Trk("hown")
            ctxA = {}

            def a_s0(t):
                src = x_pre if t < NT else x_own
                tt = t % NT
                xa, xa_t = xt.next()
                S.dma(S.sp, xa[:], src[tt * 128:(tt + 1) * 128, :], xa_t, writes=[xa_t])
                ss, ss_t = tmp["ss"].next()
                sd, sd_t = tmp["sd"].next()
                rs, rs_t = tmp["rs"].next()
                xn, xn_t = tmp["xn"].next()
                junk, junk_t = junkr.next()
                S.op(S.act, lambda: nc.scalar.activation(out=junk[:], in_=xa[:], func=AF.Square, accum_out=ss[:, 0:1]),
                     reads=[xa_t], writes=[ss_t, junk_t])
                rstd_pool(rs[:, 0:1], ss[:, 0:1], 1.0 / D, 1, ss_t, sd[:, 0:1], sd_t, rs_t)
                S.op(S.dve, lambda: nc.vector.scalar_tensor_tensor(out=xn[:], in0=xa[:], scalar=rs[:, 0:1], in1=gmix_b[:],
                                                                   op0=ALU.mult, op1=ALU.mult),
                     reads=[xa_t, rs_t, gmix_t], writes=[xn_t])
                ctxA[t] = (xn, xn_t)

            def a_s1(t):
                xn, xn_t = ctxA.pop(t)
                tt = t % NT
                k = t % 2
                transposes(xn[:], 8, xn_t, ptb[k], ptb_t[k])
                dstT = hT_pre if t < NT else hT_own
                S.op(S.act, lambda: nc.scalar.copy(out=dstT[:, :, tt * 128:(tt + 1) * 128],
                                                   in_=ptb[k].rearrange("p (c t) -> p c t", c=8)),
                     reads=[ptb_t[k]], writes=[hpre_tt[t] if t < NT else hown_t])
            pipeline(2 * NT, [a_s0, pre_only(pp0), pre_only(ppu, 1), pre_only(ppu, 0), a_s1, pre_only(pp1)], [0, 4, 7, 7, 2, 5])
            S.barrier()
            if stop_after == "A":
                dbg_dump(hT_own[:].rearrange("p c t -> p (c t)"), 16384, True, ra)
                finish()
                return nc

        with ExitStack() as es:
            ra = mk_alloc(es, "right")
            qg_b = ra("qg_b", [128, 384], F32)
            kg_b = ra("kg_b", [128, 384], F32)
            gain_t = Trk("gains")
            S.dma(S.sp, qg_b[:], bcast_rows(q_gain, 384), gain_t, writes=[gain_t])
            S.dma(S.sp, kg_b[:], bcast_rows(k_gain, 384), gain_t, writes=[gain_t])
            vaug = ra("vaug", [128, 32, 4, 132], BF16)
            vaug_t = Trk("vaug")
            kst = actT[:].rearrange("p (i h t) -> p i h t", i=32, h=4)
            kst_t = Trk("kst")
            ctr = Ring(ra, "ct", [128, 128], F32, 3)
            strg = Ring(ra, "st", [128, 128], F32, 3)
            cgr = Ring(ra, "cg", [128, 128], F32, 3)
            sgr = Ring(ra, "sg", [128, 128], F32, 3)
            ss4r = Ring(ra, "ss4", [128, 4], F32, 3)
            ln4r = Ring(ra, "ln4", [128, 4], F32, 3)
            rs4r = Ring(ra, "rs4", [128, 4], F32, 3)
            tAr = Ring(ra, "tA", [128, 512], F32, 2)
            tBr = Ring(ra, "tB", [128, 512], F32, 2)
            qkfr = Ring(ra, "qkf", [128, 512], BF16, 5)
            qTr = Ring(ra, "qT", [128, 512], BF16, 4)
            Er = Ring(ra, "E", [128, 512], BF16, 4)
            Pr = Ring(ra, "P", [128, 512], BF16, 8)
            utr = Ring(ra, "ut", [128, 4, 132], F32, 2)
            for ut_, ut_t_ in zip(utr.tiles, utr.trks):
                S.op(S.pool, lambda: nc.gpsimd.memset(ut_[:], 0.0), writes=[ut_t_])

            def load_w(dst, dst_t, col0):
                S.dma(S.pool, dst[:], w_in[:, col0:col0 + 512].rearrange("(c p) n -> p c n", p=128), dst_t, writes=[dst_t])

            def hsel(base, d, c):
                if base < TOK:
                    return hT_pre[:, c, base:base + 127 * d + 1:d]
                b = base - TOK
                return hT_own[:, c, b:b + 127 * d + 1:d]

            def tables(base, d, gain_b, g):
                ct, ct_t = ctr.next()
                st, st_t = strg.next()
                S.dma(S.sp, ct[:], acos[base:base + 127 * d + 1:d, :], ct_t, writes=[ct_t])
                S.dma(S.sp, st[:], asin[base:base + 127 * d + 1:d, :], st_t, writes=[st_t])
                cg, cg_t = cgr.next()
                sg, sg_t = sgr.next()
                gsl = gain_b[:, g * 128:(g + 1) * 128]
                gsw = bass.AP(gsl.tensor, gsl.offset + 64, [list(gsl.ap[0]), [-64, 2], [1, 64]])
                S.op(S.pool, lambda: nc.gpsimd.tensor_tensor(out=cg[:], in0=ct[:], in1=gsl, op=ALU.mult),
                     reads=[ct_t, gain_t], writes=[cg_t])
                S.op(S.pool, lambda: nc.gpsimd.tensor_tensor(out=sg[:].rearrange("p (a f) -> p a f", a=2),
                                                             in0=st[:].rearrange("p (a f) -> p a f", a=2), in1=gsw, op=ALU.mult),
                     reads=[st_t, gain_t], writes=[sg_t])
                return cg, cg_t, sg, sg_t

            def norm_rope(z, z_t, cg, cg_t, sg, sg_t):
                ss4, ss4_t = ss4r.next()
                ln4, ln4_t = ln4r.next()
                rs4, rs4_t = rs4r.next()
                tA, tA_t = tAr.next()
                tB, tB_t = tBr.next()
                of, of_t = qkfr.next()

                def sq():
                    ins = None
                    for h in range(4):
                        ins = nc.scalar.activation(out=junk[:, h * 128:(h + 1) * 128], in_=z[:, h * 128:(h + 1) * 128], func=AF.Square,
                                                   accum_out=ss4[:, h:h + 1])
                    return ins
                junk, junk_t = junkr.next()
                S.op(S.act, sq, reads=[z_t], writes=[ss4_t, junk_t])
                if False:
                    def pw():
                        nc.gpsimd.tensor_scalar(out=ln4[:], in0=ss4[:], scalar1=1.0 / 128, scalar2=EPS, op0=ALU.mult, op1=ALU.add)
                        return nc.gpsimd.tensor_tensor(out=rs4[:], in0=ln4[:], in1=nhalf[:, 0:4], op=ALU.pow)
                    S.op(S.pool, pw, reads=[ss4_t, consts_t], writes=[ln4_t, rs4_t])
                elif True:
                    S.op(S.act, lambda: nc.scalar.activation(out=ln4[:], in_=ss4[:], func=AF.Sqrt, bias=epsc[:, 0:1], scale=1.0 / 128),
                         reads=[ss4_t, consts_t], writes=[ln4_t])
                    S.op(S.dve, lambda: nc.vector.reciprocal(out=rs4[:], in_=ln4[:]), reads=[ln4_t], writes=[rs4_t])
                else:
                    S.op(S.act, lambda: nc.scalar.activation(out=ln4[:], in_=ss4[:], func=AF.Ln, bias=epsc[:, 0:1], scale=1.0 / 128),
                         reads=[ss4_t, consts_t], writes=[ln4_t])
                    S.op(S.act, lambda: nc.scalar.activation(out=rs4[:], in_=ln4[:], func=AF.Exp, scale=-0.5),
                         reads=[ln4_t], writes=[rs4_t])
                cgb = cg[:].unsqueeze(1).to_broadcast([128, 4, 128])
                sgb = sg[:].rearrange("p (a f) -> p a f", a=2).unsqueeze(1).to_broadcast([128, 4, 2, 64])
                S.op(S.dve, lambda: nc.vector.tensor_tensor(out=hv(tA[:], 4, 128), in0=hv(z, 4, 128), in1=cgb, op=ALU.mult),
                     reads=[z_t, cg_t], writes=[tA_t])
                S.op(S.dve, lambda: nc.vector.tensor_tensor(out=tB[:].rearrange("p (h a f) -> p h a f", h=4, a=2),
                                                            in0=swap_view(z, 4), in1=sgb, op=ALU.mult),
                     reads=[z_t, sg_t], writes=[tB_t])
                S.op(S.pool, lambda: nc.gpsimd.tensor_tensor(out=tA[:], in0=tA[:], in1=tB[:], op=ALU.add),
                     reads=[tA_t, tB_t], writes=[tA_t])
                S.op(S.dve, lambda: nc.vector.tensor_tensor(out=hv(of[:], 4, 128), in0=hv(tA[:], 4, 128),
                                                            in1=rs4[:].unsqueeze(2).to_broadcast([128, 4, 128]), op=ALU.mult),
                     reads=[tA_t, rs4_t], writes=[of_t])
                return of, of_t

            zkb = [(pz[0], pz_t[0]), (pz[1], pz_t[1])]
            zvb = [(pz[2], pz_t[2]), (ps[0], ps_t[0])]

            for g, d in ((2, 16), (1, 4), (0, 1)):
                Bt = 128 * d
                nb = TOK // Bt
                npre = d
                bases = [TOK - Bt + r for r in range(d)] + [TOK + n * Bt + r for n in range(nb) for r in range(d)]
                ntl = len(bases)
                S.op(S.pool, lambda: nc.gpsimd.memset(vaug[:, 0:ntl, :, 128:129], 1.0), writes=[vaug_t])
                S.op(S.pool, lambda: nc.gpsimd.tensor_copy(
                    out=vaug[:, 0:npre, :, 128:129],
                    in_=flag[:, 0:4].unsqueeze(1).unsqueeze(3).to_broadcast([128, npre, 4, 1])),
                    reads=[consts_t], writes=[vaug_t])
                cx = {}

                def k_s0(i):
                    base = bases[i]
                    tb = tables(base, d, kg_b, g)
                    zk, zk_t = zkb[i % 2]
                    zv, zv_t = zvb[i % 2]
                    mm_group(zk[:], [(hsel(base, d, c), wK[:, c, :]) for c in range(8)], [wK_t], zk_t)
                    mm_group(zv[:], [(hsel(base, d, c), wV[:, c, :]) for c in range(8)], [wV_t], zv_t)
                    cx[i] = tb

                def k_s1(i):
                    cg, cg_t, sg, sg_t = cx[i]
                    zk, zk_t = zkb[i % 2]
                    zv, zv_t = zvb[i % 2]
                    S.op(S.act, lambda: nc.scalar.copy(out=vaug[:, i, :, 0:128], in_=hv(zv[:], 4, 128)),
                         reads=[zv_t], writes=[vaug_t])
                    cx[i] = norm_rope(zk[:], zk_t, cg, cg_t, sg, sg_t)

                def k_s2(i):
                    kf, kf_t = cx.pop(i)
                    k = i % 2
                    transposes(kf[:], 4, kf_t, ptb[k], ptb_t[k])
                    S.op(S.dve, lambda: nc.vector.tensor_copy(out=kst[:, i, :, :], in_=ptb[k][:, 0:512].rearrange("p (h t) -> p h t", h=4)),
                         reads=[ptb_t[k]], writes=[kst_t])
                pipeline(ntl, [k_s0, k_s1, k_s2], [0, 1, 4])

                qx = {}
                nq = nb * d

                def q_s0(j):
                    i = npre + j
                    base = bases[i]
                    tb = tables(base, d, qg_b, g)
                    zq, zq_t = zkb[j % 2]
                    mm_group(zq[:], [(hsel(base, d, c), wQ[:, c, :]) for c in range(8)], [wQ_t], zq_t)
                    qx[j] = dict(tb=tb)

                def q_s1(j):
                    cg, cg_t, sg, sg_t = qx[j]["tb"]
                    zq, zq_t = zkb[j % 2]
                    qx[j]["qf"] = norm_rope(zq[:], zq_t, cg, cg_t, sg, sg_t)

                def q_s2(j):
                    qf, qf_t = qx[j]["qf"]
                    k = j % 2
                    transposes(qf[:], 4, qf_t, ptb[k], ptb_t[k])
                    qT, qT_t = qTr.next()
                    S.op(S.dve, lambda: nc.vector.tensor_copy(out=qT[:], in_=ptb[k][:, 0:512]), reads=[ptb_t[k]], writes=[qT_t])
                    qx[j]["qT"] = (qT, qT_t)

                sbank = [(pz[2], pz_t[2]), (ps[0], ps_t[0])]
                ubank = [(ps[1], ps_t[1]), (po[0], po_t[0])]

                def q_s3(j):
                    i = npre + j
                    qT, qT_t = qx[j]["qT"]
                    for pr in range(2):
                        sb_, sb_t = sbank[pr]

                        def smm():
                            ins = None
                            for hh in range(2):
                                h = 2 * pr + hh
                                for kk in range(2):
                                    ki = i - d if kk == 0 else i
                                    c0 = (hh * 2 + kk) * 128
                                    ins = nc.tensor.matmul(sb_[:, c0:c0 + 128], kst[:, ki, h, :], qT[:, h * 128:(h + 1) * 128],
                                                           start=True, stop=True)
                            return ins
                        S.op(S.pe, smm, reads=[kst_t, qT_t], writes=[sb_t])

                def q_s4(j):
                    Ps = []
                    for pr in range(2):
                        sb_, sb_t = sbank[pr]
                        E, E_t = Er.next()
                        P, P_t = Pr.next()
                        S.op(S.act, lambda: nc.scalar.activation(out=E[:], in_=sb_[:], func=AF.Exp, scale=ATT_SCALE),
                             reads=[sb_t], writes=[E_t])
                        meng, mh = (S.pool, nc.gpsimd) if pr == 0 else (S.dve, nc.vector)
                        S.op(meng, lambda: mh.tensor_tensor(
                            out=hv(P[:], 2, 256), in0=hv(E[:], 2, 256),
                            in1=msk[:].unsqueeze(1).to_broadcast([128, 2, 256]), op=ALU.mult),
                            reads=[E_t, consts_t], writes=[P_t])
                        Ps.append((P, P_t))
                    qx[j]["P"] = Ps

                def q_s5(j):
                    i = npre + j
                    for pr in range(2):
                        P, P_t = qx[j]["P"][pr]
                        ub, ub_t = ubank[pr]

                        def umm():
                            ins = None
                            for hh in range(2):
                                h = 2 * pr + hh
                                o_ap = ub[:, hh * 256:hh * 256 + 129]
                                nc.tensor.matmul(o_ap, P[:, hh * 256:hh * 256 + 128], vaug[:, i - d, h, 0:129], start=True, stop=False)
                                ins = nc.tensor.matmul(o_ap, P[:, hh * 256 + 128:hh * 256 + 256], vaug[:, i, h, 0:129], start=False, stop=True)
                            return ins
                        S.op(S.pe, umm, reads=[P_t, vaug_t], writes=[ub_t])

                def q_s6(j):
                    i = npre + j
                    ut, ut_t = utr.next()
                    for pr in range(2):
                        ub, ub_t = ubank[pr]
                        S.op(S.dve, lambda: nc.vector.tensor_copy(out=ut[:, 2 * pr:2 * pr + 2, 0:129],
                                                                  in_=hv(ub[:], 2, 256)[:, :, 0:129]),
                             reads=[ub_t], writes=[ut_t])
                    b0 = bases[i] - TOK
                    S.dma(S.sp, scr[g, b0:b0 + 127 * d + 1:d, :], ut[:].rearrange("p h w -> p (h w)"), ut_t, reads=[ut_t])
                    qx.pop(j)
                if g > 0:
                    load_w(wK, wK_t, 4608 + (g - 1) * 512)
                    load_w(wV, wV_t, 6144 + (g - 1) * 512)
                pipeline(nq, [q_s0, q_s2, q_s1, q_s3, q_s4, q_s5, q_s6], [0, 4, 1, 6, 7, 9, 10])
                if g > 0:
                    load_w(wQ, wQ_t, 3072 + (g - 1) * 512)
            S.barrier()

        aes.close()
        bes = ExitStack()
        ges.enter_context(bes)
        Wr = mk_alloc(bes, "right")("Wr", [128, 8, 3072], BF16)
        Wr_t = Trk("Wr")
        wd_t = Trk("wd1")
        for k in range(6):
            S.dma(S.pool, Wr[:, :, k * 512:(k + 1) * 512], w_in[:, k * 512:(k + 1) * 512].rearrange("(c p) n -> p c n", p=128),
                  Wr_t, writes=[Wr_t])
        with ExitStack() as es:
            ra = mk_alloc(es, "right")
            mgr = Ring(ra, "mg", [128, 3, 528], F32, 4)
            usr = Ring(ra, "us", [128, 528], F32, 2)
            rlr = Ring(ra, "rl", [128, 4], F32, 2)
            obr = Ring(ra, "ob", [128, 512], BF16, 4)
            otr_ = Ring(ra, "oTt", [128, 512], BF16, 2)
            scro_t = [Trk("scro%d" % i) for i in range(NT)]
            mx = {}

            def m_s0(t):
                mg, mg_t = mgr.next()
                S.dma(S.sp, mg[:], scr[:, t * 128:(t + 1) * 128, :].transpose([1, 0, 2]), mg_t, writes=[mg_t])
                mx[t] = (mg, mg_t)

            def m_s1(t):
                mg, mg_t = mx[t]
                us, us_t = usr.next()
                rl, rl_t = rlr.next()
                ob, ob_t = obr.next()
                S.op(S.pool, lambda: nc.gpsimd.tensor_tensor(out=us[:], in0=mg[:, 0, :], in1=mg[:, 1, :], op=ALU.add),
                     reads=[mg_t], writes=[us_t])
                S.op(S.pool, lambda: nc.gpsimd.tensor_tensor(out=us[:], in0=us[:], in1=mg[:, 2, :], op=ALU.add),
                     reads=[mg_t, us_t], writes=[us_t])
                us3 = us[:].rearrange("p (h w) -> p h w", h=4)
                S.op(S.dve, lambda: nc.vector.reciprocal(out=rl[:].unsqueeze(2), in_=us3[:, :, 128:129]), reads=[us_t], writes=[rl_t])
                S.op(S.dve, lambda: nc.vector.tensor_tensor(out=hv(ob[:], 4, 128), in0=us3[:, :, 0:128],
                                                            in1=rl[:].unsqueeze(2).to_broadcast([128, 4, 128]), op=ALU.mult),
                     reads=[us_t, rl_t], writes=[ob_t])
                mx[t] = (ob, ob_t)

            def m_s2(t):
                ob, ob_t = mx.pop(t)
                k = t % 2
                transposes(ob[:], 4, ob_t, ptb[k], ptb_t[k])
                ot_, ot_t_ = otr_.next()
                S.op(S.act, lambda: nc.scalar.copy(out=ot_[:], in_=ptb[k][:, 0:512]), reads=[ptb_t[k]], writes=[ot_t_])
                S.dma(S.sp, scr_o[t], ot_[:], ot_t_, reads=[ot_t_], writes=[scro_t[t]])
            pipeline(NT, [m_s0, m_s1, m_s2], [0, 2, 4])
            S.barrier()

        with ExitStack() as es:
            ra = mk_alloc(es, "right")
            rc_t = Trk("rconst")
            MT = ra("MT", [128, 512], F32)
            rt = ra("rt", [128, 12], F32)
            gn_b = ra("gn_b", [128, D], F32)
            S.dma(S.sp, MT[:], mt_tab[:, :], rc_t, writes=[rc_t])
            S.dma(S.sp, rt[:], rtab[:, :], rc_t, writes=[rc_t])
            S.dma(S.sp, gn_b[:], bcast_rows(ret_gn, D), rc_t, writes=[rc_t])
            RD = 5
            rcr = Ring(ra, "rc", [128, 128], F32, 3)
            rsr = Ring(ra, "rs_", [128, 128], F32, 3)
            tAr = Ring(ra, "rtA", [128, 512], F32, 2)
            tBr = Ring(ra, "rtB", [128, 512], F32, 2)
            kfr = Ring(ra, "kf", [128, 512], BF16, 3)
            kdr = Ring(ra, "kd", [128, 512], BF16, 3)
            vr = Ring(ra, "v", [128, 1024], BF16, 5)

            def rope(z, z_t, rc, rc_tk, rs, rs_tk, ring):
                tA, tA_t = tAr.next()
                tB, tB_t = tBr.next()
                of, of_t = ring.next()
                cb = rc[:].unsqueeze(1).to_broadcast([128, 4, 128])
                sb_ = rs[:].rearrange("p (a f) -> p a f", a=2).unsqueeze(1).to_broadcast([128, 4, 2, 64])
                S.op(S.dve, lambda: nc.vector.tensor_tensor(out=hv(tA[:], 4, 128), in0=hv(z[:], 4, 128), in1=cb, op=ALU.mult),
                     reads=[z_t, rc_tk], writes=[tA_t])
                S.op(S.dve, lambda: nc.vector.tensor_tensor(out=tB[:].rearrange("p (h a f) -> p h a f", h=4, a=2),
                                                            in0=swap_view(z[:], 4), in1=sb_, op=ALU.mult),
                     reads=[z_t, rs_tk], writes=[tB_t])
                S.op(S.pool, lambda: nc.gpsimd.tensor_tensor(out=of[:], in0=tA[:], in1=tB[:], op=ALU.add),
                     reads=[tA_t, tB_t], writes=[of_t])
                return of, of_t

            BK, BQ, BV0, BV1 = (pz[0], pz_t[0]), (pz[1], pz_t[1]), (pz[2], pz_t[2]), (ps[0], ps_t[0])
            BS = (ps[1], ps_t[1])
            rx = {}

            def mk_stages(hsrc, own, toff):
                def hT_(t, c):
                    return hsrc[:, c, t * 128:(t + 1) * 128]

                def r0(t):
                    rc, rc_tk = rcr.next()
                    rs, rs_tk = rsr.next()
                    S.dma(S.sp, rc[:], rcos[(toff + t) * 128:(toff + t + 1) * 128, :], rc_tk, writes=[rc_tk])
                    S.dma(S.sp, rs[:], rsin[(toff + t) * 128:(toff + t + 1) * 128, :], rs_tk, writes=[rs_tk])
                    mm_group(BK[0][:], [(hT_(t, c), Wr[:, c, 512:1024]) for c in range(8)], [Wr_t], BK[1])
                    if own:
                        mm_group(BQ[0][:], [(hT_(t, c), Wr[:, c, 0:512]) for c in range(8)], [Wr_t], BQ[1])
                    mm_group(BV0[0][:], [(hT_(t, c), Wr[:, c, 1024:1536]) for c in range(8)], [Wr_t], BV0[1])
                    mm_group(BV1[0][:], [(hT_(t, c), Wr[:, c, 1536:2048]) for c in range(8)], [Wr_t], BV1[1])
                    rx[(own, t)] = dict(tab=(rc, rc_tk, rs, rs_tk))

                def r1(t):
                    X = rx[(own, t)]
                    rc, rc_tk, rs, rs_tk = X["tab"]
                    v, v_t = vr.next()
                    S.op(S.act, lambda: nc.scalar.copy(out=v[:, 0:512], in_=BV0[0][:]), reads=[BV0[1]], writes=[v_t])
                    S.op(S.act, lambda: nc.scalar.copy(out=v[:, 512:1024], in_=BV1[0][:]), reads=[BV1[1]], writes=[v_t])
                    kf, kf_t = rope(BK[0], BK[1], rc, rc_tk, rs, rs_tk, kfr)
                    kd, kd_t = kdr.next()
                    S.op(S.pool, lambda: nc.gpsimd.tensor_tensor(out=hv(kd[:], 4, 128), in0=hv(kf[:], 4, 128),
                                                                 in1=rt[:, 0:4].unsqueeze(2).to_broadcast([128, 4, 128]), op=ALU.mult),
                         reads=[kf_t, rc_t], writes=[kd_t])
                    X.update(v=(v, v_t), kf=(kf, kf_t), kd=(kd, kd_t))
                    if own:
                        X["qf"] = rope(BQ[0], BQ[1], rc, rc_tk, rs, rs_tk, qfr)

                def upd(t, pr, bank):
                    X = rx[(own, t)]
                    kd, kd_t = X["kd"]
                    v, v_t = X["v"]

                    def kvmm():
                        ins = None
                        for hh in range(2):
                            h = 2 * pr + hh
                            ins = nc.tensor.matmul(bank[0][:, hh * 256:(hh + 1) * 256], kd[:, h * 128:(h + 1) * 128],
                                                   v[:, h * 256:(h + 1) * 256], start=True, stop=True)
                        return ins
                    S.op(S.pe, kvmm, reads=[kd_t, v_t], writes=[bank[1]])
                    for hh in range(2):
                        h = 2 * pr + hh
                        gC = float(GAMMAS[h] ** 128)
                        S.op(S.dve, lambda: nc.vector.scalar_tensor_tensor(
                            out=R32[:, h * 256:(h + 1) * 256], in0=R32[:, h * 256:(h + 1) * 256], scalar=gC,
                            in1=bank[0][:, hh * 256:(hh + 1) * 256], op0=ALU.mult, op1=ALU.add),
                            reads=[bank[1], R32_t], writes=[R32_t])
                return r0, r1, upd

            pes.close()

            with ExitStack() as es3:
                la = mk_alloc(es3, "left")
                qkTr = Ring(la, "qkT", [128, 1024], BF16, 4)
                osbr = Ring(la, "osb", [128, 1024], F32, 3)
                qfr = Ring(la, "qf", [128, 512], BF16, 3)
                Pr = Ring(la, "rP", [128, 512], BF16, 3)
                sglr = Ring(la, "sgl", [128, 1024], BF16, 4)
                rbr = Ring(la, "rb", [128, 1024], BF16, 2)
                Rbfr = Ring(la, "Rbf", [128, 1024], BF16, RD)
                S.op(S.pool, lambda: nc.gpsimd.tensor_copy(out=Rbfr.tiles[0][:], in_=R32[:]), reads=[R32_t], writes=[Rbfr.trks[0]])
                s4 = {k: Ring(la, "r4" + k, [128, 4], F32, 3) for k in ("ss", "t", "sd", "rr", "rstd")}
                o0, o1, oupd = mk_stages(hT_own, True, NT)

                def og(t):
                    sgl, sgl_t = sglr.next()
                    for hf, bk in ((0, BV0), (1, BV1)):
                        mm_group(bk[0][:], [(hT_own[:, c, t * 128:(t + 1) * 128], Wr[:, c, 2048 + hf * 512:2560 + hf * 512]) for c in range(8)],
                                 [Wr_t], bk[1])
                        S.op(S.act, lambda: nc.scalar.activation(out=sgl[:, hf * 512:(hf + 1) * 512], in_=bk[0][:], func=AF.Silu),
                             reads=[bk[1]], writes=[sgl_t])
                    S.op(S.pool, lambda: nc.gpsimd.tensor_tensor(out=sgl[:], in0=sgl[:], in1=gn_b[:], op=ALU.mult),
                         reads=[sgl_t, rc_t], writes=[sgl_t])
                    rx[(True, t)]["sgl"] = (sgl, sgl_t)
                    if t == NT - 1:
                        for k in range(4):
                            S.dma(S.pool, Wr[:, :, k * 512:(k + 1) * 512],
                                  w_in[:, 7680 + k * 512:7680 + (k + 1) * 512].rearrange("(c p) n -> p c n", p=128),
                                  wd_t, writes=[wd_t, Wr_t])
                        for k in range(2):
                            S.dma(S.pool, Wr[:, :, 2048 + k * 512:2048 + (k + 1) * 512],
                                  w_ro[:, k * 512:(k + 1) * 512].rearrange("(c p) n -> p c n", p=128), wd_t, writes=[wd_t, Wr_t])

                def o2(t):
                    X = rx[(True, t)]
                    qf, qf_t = X["qf"]
                    kf, kf_t = X["kf"]
                    qkT, qkT_t = qkTr.next()
                    transposes(qf[:], 4, qf_t, ptb[0], ptb_t[0], 0)
                    transposes(kf[:], 4, kf_t, ptb[0], ptb_t[0], 512)
                    S.op(S.act, lambda: nc.scalar.copy(out=qkT[:], in_=ptb[0]), reads=[ptb_t[0]], writes=[qkT_t])
                    X["qkT"] = (qkT, qkT_t)

                def o3(t):
                    X = rx[(True, t)]
                    qkT, qkT_t = X["qkT"]

                    def smm():
                        ins = None
                        for h in range(4):
                            ins = nc.tensor.matmul(BS[0][:, h * 128:(h + 1) * 128], qkT[:, 512 + h * 128:512 + (h + 1) * 128],
                                                   qkT[:, h * 128:(h + 1) * 128], start=True, stop=True)
                        return ins
                    S.op(S.pe, smm, reads=[qkT_t], writes=[BS[1]])
                    P, P_t = Pr.next()
                    S.op(S.dve, lambda: nc.vector.tensor_tensor(out=P[:], in0=BS[0][:], in1=MT[:], op=ALU.mult),
                         reads=[BS[1], rc_t], writes=[P_t])
                    X["P"] = (P, P_t)

                def o4(t):
                    X = rx[(True, t)]
                    qkT, qkT_t = X["qkT"]
                    P, P_t = X["P"]
                    v, v_t = X["v"]
                    Rbf, Rbf_t = Rbfr.tiles[t % RD], Rbfr.trks[t % RD]
                    for pr in range(2):
                        def omm():
                            ins = None
                            for hh in range(2):
                                h = 2 * pr + hh
                                o_ap = OB[pr][0][:, hh * 256:(hh + 1) * 256]
                                nc.tensor.matmul(o_ap, P[:, h * 128:(h + 1) * 128], v[:, h * 256:(h + 1) * 256], start=True, stop=False)
                                ins = nc.tensor.matmul(o_ap, qkT[:, h * 128:(h + 1) * 128], Rbf[:, h * 256:(h + 1) * 256],
                                                       start=False, stop=True)
                            return ins
                        S.op(S.pe, omm, reads=[P_t, v_t, qkT_t, Rbf_t], writes=[OB[pr][1]])

                def ou0(t):
                    oupd(t, 0, BS)

                def ou1(t):
                    oupd(t, 1, BS)
                    k = (t + 1) % RD
                    S.op(S.pool, lambda: nc.gpsimd.tensor_copy(out=Rbfr.tiles[k][:], in_=R32[:]), reads=[R32_t], writes=[Rbfr.trks[k]])

                def o5a(t):
                    X = rx[(True, t)]
                    osb, osb_t = osbr.next()
                    for pr in range(2):
                        S.op(S.act, lambda: nc.scalar.copy(out=osb[:, pr * 512:(pr + 1) * 512], in_=OB[pr][0][:]),
                             reads=[OB[pr][1]], writes=[osb_t])
                    X["osb"] = (osb, osb_t)

                def o5(t):
                    X = rx[(True, t)]
                    sgl, sgl_t = X["sgl"]
                    osb, osb_t = X["osb"]
                    ss4, ss4_t = s4["ss"].next()
                    t4, t4_t = s4["t"].next()
                    sd4, sd4_t = s4["sd"].next()
                    rr4, rr4_t = s4["rr"].next()
                    rstd4, rstd4_t = s4["rstd"].next()

                    def sq():
                        ins = None
                        for h in range(4):
                            ins = nc.scalar.activation(out=junk[:, h * 256:(h + 1) * 256], in_=osb[:, h * 256:(h + 1) * 256],
                                                       func=AF.Square, accum_out=ss4[:, h:h + 1])
                        return ins
                    junk, junk_t = junkr.next()
                    S.op(S.act, sq, reads=[osb_t], writes=[ss4_t, junk_t])
                    S.op(S.dve, lambda: nc.vector.tensor_tensor(out=t4[:], in0=ss4[:], in1=rt[:, 8:12], op=ALU.mult),
                         reads=[ss4_t, rc_t], writes=[t4_t])
                    if True:
                        rstd_pool(rr4[:], t4[:], 1.0, 4, t4_t, sd4[:], sd4_t, rr4_t)
                    else:
                        S.op(S.act, lambda: nc.scalar.activation(out=sd4[:], in_=t4[:], func=AF.Sqrt, bias=epsc[:, 0:1], scale=1.0),
                             reads=[t4_t, consts_t], writes=[sd4_t])
                        S.op(S.dve, lambda: nc.vector.reciprocal(out=rr4[:], in_=sd4[:]), reads=[sd4_t], writes=[rr4_t])
                    S.op(S.dve, lambda: nc.vector.tensor_tensor(out=rstd4[:], in0=rr4[:], in1=rt[:, 4:8], op=ALU.mult),
                         reads=[rr4_t, rc_t], writes=[rstd4_t])
                    rb, rb_t = rbr.next()
                    for h in range(4):
                        S.op(S.dve, lambda: nc.vector.scalar_tensor_tensor(
                            out=rb[:, h * 256:(h + 1) * 256], in0=osb[:, h * 256:(h + 1) * 256],
                            scalar=rstd4[:, h:h + 1], in1=sgl[:, h * 256:(h + 1) * 256], op0=ALU.mult, op1=ALU.mult),
                            reads=[osb_t, rstd4_t, sgl_t], writes=[rb_t])
                    X["rb"] = (rb, rb_t)

                def o6(t):
                    rb, rb_t = rx.pop((True, t))["rb"]
                    transposes(rb[:], 8, rb_t, ptb[0], ptb_t[0])
                    S.op(S.act, lambda: nc.scalar.copy(out=actT3[:, :, t * 128:(t + 1) * 128],
                                                       in_=ptb[0].rearrange("p (c t) -> p c t", c=8)),
                         reads=[ptb_t[0]], writes=[actT_t[t]])
                OB = [(po[0], po_t[0]), (po[1], po_t[1])]
                pipeline(NT, [o5, o6, ou1, o0, ou0, og, o2, o3, o4, o5a, o1], [7, 8, 2, 0, 2, 4, 3, 4, 5, 6, 1])
            S.barrier()
            if stop_after == "B":
                dbg_dump(actT[:], 16384, True, ra)
                finish()
                return nc


        wes = ExitStack()
        ges.enter_context(wes)
        wo = mk_alloc(wes, "right")("wo", [128, 8, 1024], BF16)
        wo_t = Trk("wo")
        with ExitStack() as es:
            ra = mk_alloc(es, "right")
            Wgl = Wr[:, :, 0:2048]
            wro = Wr[:, :, 2048:3072]
            wao = ra("wao", [128, 4, 1024], BF16)
            bg_b = ra("bg_b", [128, 2048], F32)
            wao_t = Trk("wao")
            bg_t = Trk("bg")
            S.dma(S.sp, bg_b[:], bcast_rows(b_gate, 2048), bg_t, writes=[bg_t])
            for k in range(2):
                S.dma(S.pool, wao[:, :, k * 512:(k + 1) * 512], w_ao[:, k * 512:(k + 1) * 512].rearrange("(c p) n -> p c n", p=128),
                      wao_t, writes=[wao_t])
            for k in range(2):
                S.dma(S.pool, wo[:, :, k * 512:(k + 1) * 512], w_o[:, k * 512:(k + 1) * 512].rearrange("(c p) n -> p c n", p=128),
                      wo_t, writes=[wo_t])
            t0r = Ring(ra, "t0", [128, 512], F32, 4)
            grr = Ring(ra, "gr", [128, 512], F32, 4)
            m1r = Ring(ra, "m1", [128, 512], F32, 2)
            m2r = Ring(ra, "m2", [128, 512], F32, 2)
            mr = Ring(ra, "m", [128, 1024], BF16, 2)
            oTr = Ring(ra, "oTl", [128, 512], BF16, 3)
            oTl = {}

            def load_oT(t):
                if t < NT:
                    a, a_t = oTr.next()
                    S.dma(S.sp, a[:], scr_o[t], a_t, reads=[scro_t[t]], writes=[a_t])
                    oTl[t] = (a, a_t)
            load_oT(0)
            load_oT(1)
            for t in range(NT):
                load_oT(t + 2)
                oTt, oTt_t = oTl.pop(t)
                m, m_t = mr.next()
                for hf in range(2):
                    c0 = hf * 512
                    gb = [(pz[0], pz_t[0]), (pz[1], pz_t[1])] if hf == 0 else [(pz[2], pz_t[2]), (po[0], po_t[0])]
                    mm_group(gb[0][0][:], [(hT_own[:, c, t * 128:(t + 1) * 128], Wgl[:, c, c0:c0 + 512]) for c in range(8)], [wd_t], gb[0][1])
                    mm_group(gb[1][0][:], [(hT_own[:, c, t * 128:(t + 1) * 128], Wgl[:, c, 1024 + c0:1024 + c0 + 512]) for c in range(8)],
                             [wd_t], gb[1][1])
                    mm_group(ps[0][:], [(actT3[:, c, t * 128:(t + 1) * 128], wro[:, c, c0:c0 + 512]) for c in range(8)],
                             [wd_t, actT_t[t]], ps_t[0])
                    mm_group(ps[1][:], [(oTt[:, c * 128:(c + 1) * 128], wao[:, c, c0:c0 + 512]) for c in range(4)],
                             [wao_t, oTt_t], ps_t[1])
                    gates = []
                    for which, bank in ((0, 0), (1, 1)):
                        t0, t0_t = t0r.next()
                        gr, gr_t = grr.next()
                        S.op(S.dve, lambda: nc.vector.tensor_tensor(out=t0[:], in0=gb[bank][0][:],
                                                                    in1=bg_b[:, which * 1024 + c0:which * 1024 + c0 + 512], op=ALU.add),
                             reads=[gb[bank][1], bg_t], writes=[t0_t])
                        S.op(S.act, lambda: nc.scalar.activation(out=gr[:], in_=t0[:], func=AF.Sigmoid), reads=[t0_t], writes=[gr_t])
                        gates.append((gr, gr_t))
                    m1, m1_t = m1r.next()
                    m2, m2_t = m2r.next()
                    S.op(S.dve, lambda: nc.vector.tensor_tensor(out=m1[:], in0=ps[0][:], in1=gates[0][0][:], op=ALU.mult),
                         reads=[ps_t[0], gates[0][1]], writes=[m1_t])
                    S.op(S.dve, lambda: nc.vector.tensor_tensor(out=m2[:], in0=ps[1][:], in1=gates[1][0][:], op=ALU.mult),
                         reads=[ps_t[1], gates[1][1]], writes=[m2_t])
                    S.op(S.pool, lambda: nc.gpsimd.tensor_tensor(out=m[:, c0:c0 + 512], in0=m1[:], in1=m2[:], op=ALU.add),
                         reads=[m1_t, m2_t], writes=[m_t])
                transposes(m[:], 8, m_t)
                S.op(S.act, lambda: nc.scalar.copy(out=actT3[:, :, t * 128:(t + 1) * 128], in_=pt[:].rearrange("p (c t) -> p c t", c=8)),
                     reads=[pt_t], writes=[actT_t[t]])
            S.barrier()

        les.close()
        with ExitStack() as es2:
            rb2 = mk_alloc(es2, "left")
            x1_all = rb2("x1_all", [128, NT, D], F32)
            x1_t = [Trk("x1_%d" % i) for i in range(NT)]
            tmpn = {"ss": Ring(rb2, "ss", [128, 1], F32, 2), "sd": Ring(rb2, "sd", [128, 1], F32, 2),
                    "rs": Ring(rb2, "rs", [128, 1], F32, 2), "xn": Ring(rb2, "xn", [128, D], BF16, 2)}
            with ExitStack() as es:
                ra = mk_alloc(es, "right")
                xt = Ring(ra, "xt2", [128, D], F32, 3)
                gmlp_b = ra("gmlp_b", [128, D], F32)
                gmlp_t = Trk("gmlp")
                S.dma(S.sp, gmlp_b[:], bcast_rows(g_mlp, D), gmlp_t, writes=[gmlp_t])
                xnE = Ring(ra, "xnE", [128, D], BF16, 3)
                ex = {}

                def e_s0(t):
                    ss, ss_t = tmpn["ss"].next()
                    sd, sd_t = tmpn["sd"].next()
                    rs, rs_t = tmpn["rs"].next()
                    xn, xn_t = xnE.next()
                    xa = x1_all[:, t, :]
                    junk, junk_t = junkr.next()
                    S.op(S.act, lambda: nc.scalar.activation(out=junk[:], in_=xa, func=AF.Square, accum_out=ss[:, 0:1]),
                         reads=[x1_t[t]], writes=[ss_t, junk_t])
                    rstd_pool(rs[:, 0:1], ss[:, 0:1], 1.0 / D, 1, ss_t, sd[:, 0:1], sd_t, rs_t)
                    S.op(S.dve, lambda: nc.vector.scalar_tensor_tensor(out=xn[:], in0=xa, scalar=rs[:, 0:1], in1=gmlp_b[:],
                                                                       op0=ALU.mult, op1=ALU.mult),
                         reads=[x1_t[t], rs_t, gmlp_t], writes=[xn_t])
                    ex[t] = (xn, xn_t)

                def e_s1(t):
                    xn, xn_t = ex.pop(t)
                    k = t % 2
                    transposes(xn[:], 8, xn_t, ptb[k], ptb_t[k])
                    S.op(S.act, lambda: nc.scalar.copy(out=actT3[:, :, t * 128:(t + 1) * 128],
                                                       in_=ptb[k].rearrange("p (c t) -> p c t", c=8)),
                         reads=[ptb_t[k]], writes=[actT_t[t]])
                dx = {}

                def d_s0(t):
                    xa, xa_t = xt.next()
                    S.dma(S.sp, xa[:], x_own[t * 128:(t + 1) * 128, :], xa_t, writes=[xa_t])
                    for hf in range(2):
                        c0 = hf * 512
                        mm_group(pz[hf][:], [(actT3[:, c, t * 128:(t + 1) * 128], wo[:, c, c0:c0 + 512]) for c in range(8)],
                                 [wo_t, actT_t[t]], pz_t[hf])
                    dx[t] = (xa, xa_t)

                def d_s1(t):
                    xa, xa_t = dx.pop(t)
                    for hf in range(2):
                        c0 = hf * 512
                        S.op(S.dve, lambda: nc.vector.tensor_tensor(out=x1_all[:, t, c0:c0 + 512], in0=pz[hf][:], in1=xa[:, c0:c0 + 512], op=ALU.add),
                             reads=[pz_t[hf], xa_t], writes=[x1_t[t]])
                pipeline(NT, [d_s0, e_s0, e_s1, d_s1], [0, 2, 4, 1])
                S.barrier()
                if stop_after == "D":
                    dbg_dump(x1_all[:].rearrange("p t d -> p (t d)"), 16384, False, ra)
                    finish()
                    return nc

            wes.close()
            bes.close()
            fes = ExitStack()
            ges.enter_context(fes)
            fa = mk_alloc(fes, "right")
            gple_b = fa("gple_b", [128, D], F32)
            wpg = fa("wpg", [128, 8, 1024], BF16)
            wpp = fa("wpp", [128, 2, 1024], BF16)
            wf_t = Trk("wf")
            with ExitStack() as es:
                ra = mk_alloc(es, "right")
                NF = 8
                FW = 4096 // NF
                NJ = FW // 128
                wur = Ring(ra, "wu", [128, 8, FW], BF16, 2)
                wdr = Ring(ra, "wdn", [128, NJ, D], BF16, 2)
                aTr = Ring(ra, "aT", [128, NJ, TOK], BF16, 2)
                r1r = Ring(ra, "rl1", [128, 512], F32, 2)
                for fg in range(NF):
                    wu, wu_t = wur.next()
                    wdn, wdn_t = wdr.next()
                    aT, aT_t0 = aTr.next()
                    aT_tg = [Trk("aTtg%d" % i) for i in range(4)]
                    S.dma(S.pool, wu[:], w_up[:, fg * FW:(fg + 1) * FW].rearrange("(c p) n -> p c n", p=128), wu_t, writes=[wu_t])
                    S.dma(S.pool, wdn[:], w_down[fg * FW:(fg + 1) * FW, :].rearrange("(j p) n -> p j n", p=128), wdn_t, writes=[wdn_t])
                    if fg == 2:
                        S.dma(S.sp, gple_b[:], bcast_rows(g_ple, D), wf_t, writes=[wf_t])
                        for k in range(2):
                            S.dma(S.pool, wpg[:, :, k * 512:(k + 1) * 512], w_pg[:, k * 512:(k + 1) * 512].rearrange("(c p) n -> p c n", p=128),
                                  wf_t, writes=[wf_t])
                        S.dma(S.pool, wpp[:], w_pp[:, :].rearrange("(c p) n -> p c n", p=128), wf_t, writes=[wf_t])
                    k = 0
                    for tg in range(4):
                        for j in range(NJ):
                            bank = k % 3
                            k += 1
                            mm_group(pz[bank][:], [(wu[:, c, j * 128:(j + 1) * 128], actT3[:, c, tg * 512:(tg + 1) * 512]) for c in range(8)],
                                     [wu_t] + actT_t[tg * 4:(tg + 1) * 4], pz_t[bank])
                            r1, r1_t = r1r.next()
                            S.op(S.act, lambda: nc.scalar.activation(out=r1[:], in_=pz[bank][:], func=AF.Relu), reads=[pz_t[bank]], writes=[r1_t])
                            S.op(S.pool, lambda: nc.gpsimd.tensor_tensor(out=aT[:, j, tg * 512:(tg + 1) * 512], in0=r1[:], in1=r1[:], op=ALU.mult),
                                 reads=[r1_t], writes=[aT_tg[tg], aT_t0] if (tg, j) == (0, 0) else [aT_tg[tg]])
                    for t in range(NT):
                        for hf in range(2):
                            c0 = hf * 512
                            mm_group(ps[hf][:], [(aT[:, j, t * 128:(t + 1) * 128], wdn[:, j, c0:c0 + 512]) for j in range(NJ)],
                                     [aT_tg[t // 4], aT_t0, wdn_t], ps_t[hf])
                            S.op(S.dve, lambda: nc.vector.tensor_tensor(out=x1_all[:, t, c0:c0 + 512], in0=ps[hf][:], in1=x1_all[:, t, c0:c0 + 512], op=ALU.add),
                                 reads=[ps_t[hf], x1_t[t]], writes=[x1_t[t]])
                S.barrier()
                if stop_after == "E":
                    dbg_dump(x1_all[:].rearrange("p t d -> p (t d)"), 16384, False, ra)
                    finish()
                    return nc

            with ExitStack() as es:
                ra = mk_alloc(es, "right")
                h3r = Ring(ra, "h3T", [128, 8, 128], BF16, 4)
                pbr = Ring(ra, "pb", [128, 256], BF16, 4)
                pTr = Ring(ra, "pT", [128, 256], BF16, 4)
                gpr = Ring(ra, "gp", [128, 512], F32, 2)
                tfr = Ring(ra, "tf", [128, 512], F32, 2)
                otr = Ring(ra, "ot", [128, D], F32, 2)
                tmpn["xn"] = Ring(ra, "xnF", [128, D], BF16, 4)
                fx = {}

                def f_s0(t):
                    ss, ss_t = tmpn["ss"].next()
                    sd, sd_t = tmpn["sd"].next()
                    rs, rs_t = tmpn["rs"].next()
                    xn, xn_t = tmpn["xn"].next()
                    xa = x1_all[:, t, :]
                    junk, junk_t = junkr.next()
                    S.op(S.act, lambda: nc.scalar.activation(out=junk[:], in_=xa, func=AF.Square, accum_out=ss[:, 0:1]),
                         reads=[x1_t[t]], writes=[ss_t, junk_t])
                    rstd_pool(rs[:, 0:1], ss[:, 0:1], 1.0 / D, 1, ss_t, sd[:, 0:1], sd_t, rs_t)
                    S.op(S.dve, lambda: nc.vector.scalar_tensor_tensor(out=xn[:], in0=xa, scalar=rs[:, 0:1], in1=gple_b[:],
                                                                       op0=ALU.mult, op1=ALU.mult),
                         reads=[x1_t[t], rs_t, wf_t], writes=[xn_t])
                    pb, pb_t = pbr.next()
                    S.dma(S.pool, pb[:], p_own[t * 128:(t + 1) * 128, :], pb_t, writes=[pb_t])
                    fx[t] = dict(xn=(xn, xn_t), pb=(pb, pb_t))

                def f_s1(t):
                    xn, xn_t = fx[t]["xn"]
                    pb, pb_t = fx[t]["pb"]
                    h3, h3_t = h3r.next()
                    pT, pT_t = pTr.next()
                    transposes(xn[:], 8, xn_t, ptb[0], ptb_t[0])
                    S.op(S.dve, lambda: nc.vector.tensor_copy(out=h3[:], in_=ptb[0].rearrange("p (c t) -> p c t", c=8)),
                         reads=[ptb_t[0]], writes=[h3_t])
                    transposes(pb[:], 2, pb_t, ptb[1], ptb_t[1])
                    S.op(S.dve, lambda: nc.vector.tensor_copy(out=pT[:], in_=ptb[1][:, 0:256]), reads=[ptb_t[1]], writes=[pT_t])
                    fx[t] = dict(h3=(h3, h3_t), pT=(pT, pT_t))

                fbank = [(pz[0], pz_t[0]), (pz[1], pz_t[1])]
                fbank2 = [(ps[0], ps_t[0]), (ps[1], ps_t[1])]

                def f_s2(t):
                    h3, h3_t = fx[t]["h3"]
                    pT, pT_t = fx[t]["pT"]
                    for hf in range(2):
                        c0 = hf * 512
                        mm_group(fbank[hf][0][:], [(h3[:, c, :], wpg[:, c, c0:c0 + 512]) for c in range(8)], [h3_t, wf_t], fbank[hf][1])
                        mm_group(fbank2[hf][0][:], [(pT[:, c * 128:(c + 1) * 128], wpp[:, c, c0:c0 + 512]) for c in range(2)], [pT_t, wf_t], fbank2[hf][1])

                def f_s3(t):
                    fx.pop(t)
                    ot, ot_t = otr.next()
                    for hf in range(2):
                        c0 = hf * 512
                        gp, gp_t = gpr.next()
                        tf, tf_t = tfr.next()
                        S.op(S.act, lambda: nc.scalar.activation(out=gp[:], in_=fbank[hf][0][:], func=AF.Sigmoid), reads=[fbank[hf][1]], writes=[gp_t])
                        S.op(S.dve, lambda: nc.vector.tensor_tensor(out=tf[:], in0=fbank2[hf][0][:], in1=gp[:], op=ALU.mult),
                             reads=[fbank2[hf][1], gp_t], writes=[tf_t])
                        S.op(S.pool, lambda: nc.gpsimd.tensor_tensor(out=ot[:, c0:c0 + 512], in0=tf[:], in1=x1_all[:, t, c0:c0 + 512], op=ALU.add),
                             reads=[tf_t, x1_t[t]], writes=[ot_t])
                    S.dma(S.sp, y_out[t * 128:(t + 1) * 128, :], ot[:], ot_t, reads=[ot_t])
                pipeline(NT, [f_s0, f_s2, f_s1, f_s3], [0, 4, 2, 5])
                finish()
    return nc


def _tables(half):
    f32 = np.float32
    pos = (np.arange(4096, dtype=np.int64) + (half * 2048 - 2048)).astype(np.float64)
    ret_freq = 1.0 / np.power(10000.0, np.linspace(0.0, 1.0, 64, dtype=np.float64))
    rope_freq = np.power(10000.0, -np.arange(0, 128, 2, dtype=np.float64) / 128.0)

    def cs(freq):
        ang = pos[:, None] * freq[None, :]
        c = np.cos(ang).astype(f32)
        s = np.sin(ang).astype(f32)
        return np.concatenate([c, c], 1), np.concatenate([-s, s], 1)
    rc, rs = cs(ret_freq)
    ac, as_ = cs(rope_freq)
    idx = np.arange(128, dtype=np.float64)
    lg = np.log1p(-np.exp2(-5.0 - np.arange(4, dtype=np.float64)))
    dk = 128.0 ** -0.5
    mt = np.zeros((128, 4, 128), np.float64)
    for h in range(4):
        col = dk * np.exp(-lg[h] * (idx + 1.0))
        mt[:, h, :] = col[:, None] * (idx[None, :] >= idx[:, None])
    rtab = np.zeros((128, 12), np.float64)
    for h in range(4):
        rtab[:, h] = dk * np.exp(lg[h] * (127.0 - idx))
        rtab[:, 4 + h] = np.exp(lg[h] * (idx + 1.0))
        rtab[:, 8 + h] = np.exp(2.0 * lg[h] * (idx + 1.0)) / 256.0
    k = idx[:, None]
    q = idx[None, :]
    mask = np.concatenate([(k >= q), (k <= q)], 1).astype(f32)
    return dict(rcos=rc, rsin=rs, acos=ac, asin=as_, mt_tab=mt.reshape(128, 512).astype(f32), rtab=rtab.astype(f32),
                mask01=mask, ident=np.eye(128, dtype=f32), flag=np.full((128, 4), float(half), f32))


_NC_CACHE = {}


def _get_nc(stop_after=None):
    if stop_after not in _NC_CACHE:
        _NC_CACHE[stop_after] = build_program(stop_after)
    return _NC_CACHE[stop_after]


def make_in_maps(x, p, w_in, b_gate, g_mix, q_gain, k_gain, ret_gn, w_ret_out, w_att_out, w_o,
                 g_mlp, w_up, w_down, g_ple, w_ple_proj, w_ple_gate):
    f = lambda a: np.ascontiguousarray(np.asarray(a, dtype=np.float32))
    x = f(x)
    p = f(p)
    shared = dict(
        w_in=f(w_in[0]), w_ro=f(w_ret_out[0]), w_ao=f(w_att_out[0]), w_o=f(w_o[0]), w_up=f(w_up[0]), w_down=f(w_down[0]),
        w_pp=f(w_ple_proj[0]), w_pg=f(w_ple_gate[0]), g_mix=f(g_mix[0]).reshape(1, -1), g_mlp=f(g_mlp[0]).reshape(1, -1),
        g_ple=f(g_ple[0]).reshape(1, -1), b_gate=f(b_gate[0]).reshape(1, -1), q_gain=f(q_gain[0]).reshape(1, -1),
        k_gain=f(k_gain[0]).reshape(1, -1), ret_gn=f(ret_gn[0]).reshape(1, -1))
    tabs = [_tables(0), _tables(1)]
    in_maps = []
    for core in range(8):
        b, half = core // 2, core % 2
        m = dict(shared)
        m["x_own"] = np.ascontiguousarray(x[b, half * 2048:(half + 1) * 2048])
        m["x_pre"] = np.ascontiguousarray(x[b, 0:2048]) if half == 1 else np.zeros((2048, 1024), np.float32)
        m["p_own"] = np.ascontiguousarray(p[0, b, half * 2048:(half + 1) * 2048])
        m.update(tabs[half])
        in_maps.append(m)
    return in_maps


def kernel(**inputs):
    in_maps = make_in_maps(**inputs)
    nc = _get_nc(None)
    res = run_bass_kernel_spmd(nc, in_maps, core_ids=list(range(8)))
    out = np.zeros((4, 4096, 1024), np.float32)
    for core in range(8):
        b, half = core // 2, core % 2
        out[b, half * 2048:(half + 1) * 2048] = res.results[core]["y"]
    return out
```

```python
import os
import numpy as np
import concourse.bass as bass
import concourse.mybir as mybir
from concourse.bass_utils import run_bass_kernel_spmd
from contextlib import ExitStack

F32 = mybir.dt.float32
BF16 = mybir.dt.bfloat16
AF = mybir.ActivationFunctionType
ALU = mybir.AluOpType

EPS = 1e-6
D = 1024
TOK = 2048
NT = 16
GAMMAS = [1.0 - 2.0 ** (-5.0 - h) for h in range(4)]
ATT_SCALE = 128.0 ** -0.5


class SemObj:
    def __init__(self, sem, step):
        self.sem = sem
        self.val = 0
        self.step = step


class Trk:
    __slots__ = ("name", "w", "r", "dsem")

    def __init__(self, name):
        self.name = name
        self.w = None
        self.r = {}
        self.dsem = None


class Eng:
    def __init__(self, name, h, prog):
        self.name = name
        self.h = h
        self.prog = prog
        self.seen = {}


class Sched:
    def __init__(self, nc, es):
        self.nc = nc
        self.es = es
        self.nsem = 0
        self.pe = Eng("pe", nc.tensor, self._sem("pe", 1))
        self.act = Eng("act", nc.scalar, self._sem("act", 1))
        self.dve = Eng("dve", nc.vector, self._sem("dve", 1))
        self.pool = Eng("pool", nc.gpsimd, self._sem("pool", 1))
        self.sp = Eng("sp", nc.sync, None)
        self.engs = [self.pe, self.act, self.dve, self.pool, self.sp]
        self.dsems = []

    def _sem(self, name, step):
        self.nsem += 1
        s = self.es.enter_context(self.nc.semaphore("s_%s_%d" % (name, self.nsem)))
        return SemObj(s, step)

    def _deps(self, reads, writes):
        deps = {}

        def add(so, v):
            if deps.get(so, 0) < v:
                deps[so] = v
        for t in reads:
            if t.w is not None:
                add(*t.w)
        for t in writes:
            if t.w is not None:
                add(*t.w)
            for so, v in t.r.items():
                add(so, v)
        return deps

    def _wait(self, eng, deps):
        for so, v in deps.items():
            if so is eng.prog and eng is self.pe:
                continue
            if eng.seen.get(so, 0) >= v:
                continue
            eng.h.wait_ge(so.sem, v)
            eng.seen[so] = v

    def _mark(self, ev, reads, writes):
        so, v = ev
        for t in reads:
            if t.r.get(so, 0) < v:
                t.r[so] = v
        for t in writes:
            t.w = ev
            t.r = {}

    def op(self, eng, fn, reads=(), writes=()):
        self._wait(eng, self._deps(reads, writes))
        ins = fn()
        eng.prog.val += 1
        ins.then_inc(eng.prog.sem, 1)
        self._mark((eng.prog, eng.prog.val), reads, writes)

    def dma(self, eng, out, in_, owner, reads=(), writes=()):
        if owner.dsem is None:
            owner.dsem = {}
        kind = "sw" if eng is self.pool else "hw"
        if kind not in owner.dsem:
            owner.dsem[kind] = self._sem("d", 16)
            self.dsems.append(owner.dsem[kind])
        so = owner.dsem[kind]
        deps = self._deps(reads, writes)
        if so in deps and all((t.w is None or t.w[0] is so) and not t.r for t in writes) \
                and all(t.w is None or t.w[0] is not so for t in reads):
            del deps[so]
        self._wait(eng, deps)
        ins = eng.h.dma_start(out=out, in_=in_)
        so.val += 16
        ins.then_inc(so.sem, 16)
        self._mark((so, so.val), reads, writes)

    def barrier(self):
        sems = [e.prog for e in self.engs if e.prog is not None and e.prog.val > 0]
        sems += [s for s in self.dsems if s.val > 0]
        for e in self.engs:
            self._wait(e, {so: so.val for so in sems})


class Ring:
    def __init__(self, alloc, name, shape, dtype, n):
        self.tiles = [alloc("%s_%d" % (name, i), shape, dtype) for i in range(n)]
        self.trks = [Trk("%s_%d" % (name, i)) for i in range(n)]
        self.i = 0
        self.n = n

    def next(self):
        k = self.i % self.n
        self.i += 1
        return self.tiles[k], self.trks[k]


def swap_view(ap, nh):
    a = ap.ap
    return bass.AP(ap.tensor, ap.offset + 64, [list(a[0]), [128, nh], [-64, 2], [1, 64]])


def hv(ap, nh, w):
    return ap.rearrange("p (h w) -> p h w", h=nh, w=w)


def build_program(stop_after=None):
    nc = bass.Bass("TRN2", target_bir_lowering=False)

    def din(name, shape):
        return nc.dram_tensor(name, shape, F32, kind="ExternalInput").ap()

    x_own = din("x_own", [TOK, D])
    x_pre = din("x_pre", [TOK, D])
    p_own = din("p_own", [TOK, 256])
    w_in = din("w_in", [D, 9728])
    w_ro = din("w_ro", [1024, D])
    w_ao = din("w_ao", [512, D])
    w_o = din("w_o", [D, D])
    w_up = din("w_up", [D, 4096])
    w_down = din("w_down", [4096, D])
    w_pp = din("w_pp", [256, D])
    w_pg = din("w_pg", [D, D])
    g_mix = din("g_mix", [1, D])
    g_mlp = din("g_mlp", [1, D])
    g_ple = din("g_ple", [1, D])
    b_gate = din("b_gate", [1, 2048])
    q_gain = din("q_gain", [1, 384])
    k_gain = din("k_gain", [1, 384])
    ret_gn = din("ret_gn", [1, D])
    rcos = din("rcos", [4096, 128])
    rsin = din("rsin", [4096, 128])
    acos = din("acos", [4096, 128])
    asin = din("asin", [4096, 128])
    mt_tab = din("mt_tab", [128, 512])
    rtab = din("rtab", [128, 12])
    mask01 = din("mask01", [128, 256])
    ident_in = din("ident", [128, 128])
    flag_in = din("flag", [128, 4])
    y_out = nc.dram_tensor("y", [TOK, D], F32, kind="ExternalOutput").ap()
    scr = nc.dram_tensor("scr_att", [3, TOK, 528], F32, kind="Internal").ap()
    scr_o = nc.dram_tensor("scr_o", [NT, 128, 512], BF16, kind="Internal").ap()
    dbg = None
    if stop_after is not None:
        dbg = nc.dram_tensor("dbg", [128, 16384], F32, kind="ExternalOutput").ap()

    def bcast_rows(ap, n):
        return bass.AP(ap.tensor, ap.offset, [[0, 128], [1, n]])

    ges = ExitStack()
    with ges:
        S = Sched(nc, ges)

        uniq = [0]

        def mk_alloc(es, side):
            def alloc(name, shape, dtype):
                uniq[0] += 1
                return es.enter_context(nc.sbuf_tensor("sb%d_%s" % (uniq[0], name), shape, dtype, side=side))
            return alloc
        galloc = mk_alloc(ges, "left")

        def palloc(name, shape, dtype):
            return ges.enter_context(nc.psum_tensor(name, shape, dtype))

        pz = [palloc("pz%d" % i, [128, 512], F32) for i in range(3)]
        pz_t = [Trk("pz%d" % i) for i in range(3)]
        pt = palloc("pt", [128, 1024], BF16)
        pt_t = Trk("pt")
        ps = [palloc("ps%d" % i, [128, 512], F32) for i in range(2)]
        ps_t = [Trk("ps%d" % i) for i in range(2)]
        po = [palloc("po%d" % i, [128, 512], F32) for i in range(2)]
        po_t = [Trk("po%d" % i) for i in range(2)]

        consts_t = Trk("consts")
        ident = galloc("ident", [128, 128], BF16)
        msk = galloc("msk", [128, 256], BF16)
        flag = galloc("flag", [128, 4], BF16)
        epsc = galloc("epsc", [128, 1], F32)
        junkr = Ring(galloc, "junk", [128, 1024], BF16, 2)
        S.dma(S.pool, ident[:], ident_in[:, :], consts_t, writes=[consts_t])
        S.dma(S.pool, msk[:], mask01[:, :], consts_t, writes=[consts_t])
        S.dma(S.pool, flag[:], flag_in[:, :], consts_t, writes=[consts_t])
        S.op(S.pool, lambda: nc.gpsimd.memset(epsc[:], EPS), writes=[consts_t])
        nhalf = galloc("nhalf", [128, 4], F32)
        S.op(S.pool, lambda: nc.gpsimd.memset(nhalf[:], -0.5), writes=[consts_t])

        def rstd_pool(out_ap, in_ap, scale, n, in_t, mid, mid_t, out_t):
            S.op(S.pool, lambda: nc.gpsimd.tensor_scalar(out=mid, in0=in_ap, scalar1=float(scale), scalar2=EPS,
                                                         op0=ALU.mult, op1=ALU.add),
                 reads=[in_t], writes=[mid_t])
            S.op(S.pool, lambda: nc.gpsimd.tensor_tensor(out=out_ap, in0=mid, in1=nhalf[:, 0:n], op=ALU.pow),
                 reads=[mid_t, consts_t], writes=[out_t])

        actT = galloc("actT", [128, 8 * TOK], BF16)
        actT3 = actT[:].rearrange("p (c t) -> p c t", c=8)
        actT_t = [Trk("actT%d" % i) for i in range(NT)]
        les = ExitStack()
        ges.enter_context(les)
        lalloc = mk_alloc(les, "left")
        hT_own = lalloc("hT_own", [128, 8, TOK], BF16)
        pes = ExitStack()
        ges.enter_context(pes)
        hT_pre = mk_alloc(pes, "left")("hT_pre", [128, 8, TOK], BF16)

        def mm_group(out_ap, pairs, reads, out_trk):
            def fn():
                n = len(pairs)
                ins = None
                for i, (l, r) in enumerate(pairs):
                    ins = nc.tensor.matmul(out_ap, l, r, start=(i == 0), stop=(i == n - 1))
                return ins
            S.op(S.pe, fn, reads=reads, writes=[out_trk])

        def transposes(src_ap, nblk, src_trk, dst=None, dst_t=None, off=0):
            if dst is None:
                dst, dst_t = pt[:], pt_t

            def fn():
                ins = None
                for k in range(nblk):
                    ins = nc.tensor.transpose(dst[:, off + k * 128:off + (k + 1) * 128], src_ap[:, k * 128:(k + 1) * 128], ident[:])
                return ins
            S.op(S.pe, fn, reads=[src_trk, consts_t], writes=[dst_t])

        def pipeline(n, stages, skews=None):
            skews = skews or list(range(len(stages)))
            for step in range(n + max(skews)):
                for f, sk in reversed(list(zip(stages, skews))):
                    i = step - sk
                    if 0 <= i < n:
                        f(i)

        ptb = [pt[:], po[1][:].bitcast(BF16)]
        ptb_t = [pt_t, po_t[1]]

        def norm_transpose(tmp, x_ap, x_trk, g_b, g_trk, dst_ap, dst_trk):
            ss, ss_t = tmp["ss"].next()
            sd, sd_t = tmp["sd"].next()
            rs, rs_t = tmp["rs"].next()
            xn, xn_t = tmp["xn"].next()
            junk, junk_t = junkr.next()
            S.op(S.act, lambda: nc.scalar.activation(out=junk[:], in_=x_ap, func=AF.Square, accum_out=ss[:, 0:1]),
                 reads=[x_trk], writes=[ss_t, junk_t])
            rstd_pool(rs[:, 0:1], ss[:, 0:1], 1.0 / D, 1, ss_t, sd[:, 0:1], sd_t, rs_t)
            S.op(S.dve, lambda: nc.vector.scalar_tensor_tensor(out=xn[:], in0=x_ap, scalar=rs[:, 0:1], in1=g_b[:],
                                                               op0=ALU.mult, op1=ALU.mult),
                 reads=[x_trk, rs_t, g_trk], writes=[xn_t])
            transposes(xn[:], 8, xn_t)
            S.op(S.act, lambda: nc.scalar.copy(out=dst_ap, in_=pt[:].rearrange("p (c t) -> p c t", c=8)),
                 reads=[pt_t], writes=[dst_trk])

        def dbg_dump(ap_bf16_or_f32, ncols, is_bf16, alloc):
            t_ = Trk("dbgt")
            CH = 2048
            stg = alloc("dbg_stg", [128, CH], F32)
            for c0 in range(0, ncols, CH):
                w = min(CH, ncols - c0)
                S.op(S.dve, lambda: nc.vector.tensor_copy(stg[:, 0:w], ap_bf16_or_f32[:, c0:c0 + w]), writes=[t_])
                S.dma(S.sp, dbg[:, c0:c0 + w], stg[:, 0:w], t_, reads=[t_])
            S.barrier()

        def finish():
            S.barrier()
            for so in S.dsems:
                S.sp.h.wait_ge(so.sem, so.val)

        res = ExitStack()
        ges.enter_context(res)
        R32 = mk_alloc(res, "right")("R32", [128, 1024], F32)
        R32_t = Trk("R32")
        S.op(S.pool, lambda: nc.gpsimd.memset(R32[:], 0.0), writes=[R32_t])
        aes = ExitStack()
        ges.enter_context(aes)
        aa = mk_alloc(aes, "right")
        wK = aa("wK", [128, 8, 512], BF16)
        wV = aa("wV", [128, 8, 512], BF16)
        wQ = aa("wQ", [128, 8, 512], BF16)
        wK_t, wV_t, wQ_t = Trk("wK"), Trk("wV"), Trk("wQ")
        with ExitStack() as es:
            ra = mk_alloc(es, "right")
            gmix_b = ra("gmix_b", [128, D], F32)
            gmix_t = Trk("gmix")
            S.dma(S.sp, gmix_b[:], bcast_rows(g_mix, D), gmix_t, writes=[gmix_t])
            tmp = {"ss": Ring(ra, "ss", [128, 1], F32, 2), "sd": Ring(ra, "sd", [128, 1], F32, 2),
                   "rs": Ring(ra, "rs", [128, 1], F32, 2), "xn": Ring(ra, "xn", [128, D], BF16, 2)}
            xt = Ring(ra, "xt", [128, D], F32, 3)
            tmp["xn"] = Ring(ra, "xn3", [128, D], BF16, 3)
            WrKV = ra("WrKV", [128, 8, 1536], BF16)
            WrKV_t = Trk("WrKV")
            for k in range(3):
                S.dma(S.pool, WrKV[:, :, k * 512:(k + 1) * 512],
                      w_in[:, 512 + k * 512:512 + (k + 1) * 512].rearrange("(c p) n -> p c n", p=128), WrKV_t, writes=[WrKV_t])
            for dst_, dst_t_, col0_ in ((wK, wK_t, 4608 + 2 * 512), (wV, wV_t, 6144 + 2 * 512), (wQ, wQ_t, 3072 + 2 * 512)):
                S.dma(S.pool, dst_[:], w_in[:, col0_:col0_ + 512].rearrange("(c p) n -> p c n", p=128), dst_t_, writes=[dst_t_])
            prt = ra("prt", [128, 12], F32)
            prt_t = Trk("prt")
            S.dma(S.sp, prt[:], rtab[:, :], prt_t, writes=[prt_t])
            prc = Ring(ra, "prc", [128, 128], F32, 3)
            prs = Ring(ra, "prs", [128, 128], F32, 3)
            ptA = Ring(ra, "ptA", [128, 512], F32, 2)
            ptB = Ring(ra, "ptB", [128, 512], F32, 2)
            pkf = Ring(ra, "pkf", [128, 512], BF16, 3)
            pkd = Ring(ra, "pkd", [128, 512], BF16, 3)
            pvr = Ring(ra, "pv", [128, 1024], BF16, 5)
            PBK, PBV0, PBV1 = (pz[0], pz_t[0]), (pz[2], pz_t[2]), (ps[0], ps_t[0])
            PKV = [(ps[1], ps_t[1]), (po[0], po_t[0])]
            px = {}

            def pp0(t):
                rc, rc_tk = prc.next()
                rs_, rs_tk = prs.next()
                S.dma(S.sp, rc[:], rcos[t * 128:(t + 1) * 128, :], rc_tk, writes=[rc_tk])
                S.dma(S.sp, rs_[:], rsin[t * 128:(t + 1) * 128, :], rs_tk, writes=[rs_tk])
                for bk, c0 in ((PBK, 0), (PBV0, 512), (PBV1, 1024)):
                    mm_group(bk[0][:], [(hT_pre[:, c, t * 128:(t + 1) * 128], WrKV[:, c, c0:c0 + 512]) for c in range(8)],
                             [WrKV_t, hpre_tt[t]], bk[1])
                px[t] = dict(tab=(rc, rc_tk, rs_, rs_tk))

            def pp1(t):
                X = px[t]
                rc, rc_tk, rs_, rs_tk = X["tab"]
                v, v_t = pvr.next()
                S.op(S.act, lambda: nc.scalar.copy(out=v[:, 0:512], in_=PBV0[0][:]), reads=[PBV0[1]], writes=[v_t])
                S.op(S.act, lambda: nc.scalar.copy(out=v[:, 512:1024], in_=PBV1[0][:]), reads=[PBV1[1]], writes=[v_t])
                tA, tA_t = ptA.next()
                tB, tB_t = ptB.next()
                kf, kf_t = pkf.next()
                kd, kd_t = pkd.next()
                z = PBK[0]
                cb = rc[:].unsqueeze(1).to_broadcast([128, 4, 128])
                sb_ = rs_[:].rearrange("p (a f) -> p a f", a=2).unsqueeze(1).to_broadcast([128, 4, 2, 64])
                S.op(S.dve, lambda: nc.vector.tensor_tensor(out=hv(tA[:], 4, 128), in0=hv(z[:], 4, 128), in1=cb, op=ALU.mult),
                     reads=[PBK[1], rc_tk], writes=[tA_t])
                S.op(S.dve, lambda: nc.vector.tensor_tensor(out=tB[:].rearrange("p (h a f) -> p h a f", h=4, a=2),
                                                            in0=swap_view(z[:], 4), in1=sb_, op=ALU.mult),
                     reads=[PBK[1], rs_tk], writes=[tB_t])
                S.op(S.pool, lambda: nc.gpsimd.tensor_tensor(out=kf[:], in0=tA[:], in1=tB[:], op=ALU.add),
                     reads=[tA_t, tB_t], writes=[kf_t])
                S.op(S.pool, lambda: nc.gpsimd.tensor_tensor(out=hv(kd[:], 4, 128), in0=hv(kf[:], 4, 128),
                                                             in1=prt[:, 0:4].unsqueeze(2).to_broadcast([128, 4, 128]), op=ALU.mult),
                     reads=[kf_t, prt_t], writes=[kd_t])
                X.update(v=(v, v_t), kd=(kd, kd_t))

            def ppu(t, pr):
                X = px[t]
                kd, kd_t = X["kd"]
                v, v_t = X["v"]
                bank = PKV[pr]

                def kvmm():
                    ins = None
                    for hh in range(2):
                        h = 2 * pr + hh
                        ins = nc.tensor.matmul(bank[0][:, hh * 256:(hh + 1) * 256], kd[:, h * 128:(h + 1) * 128],
                                               v[:, h * 256:(h + 1) * 256], start=True, stop=True)
                    return ins
                S.op(S.pe, kvmm, reads=[kd_t, v_t], writes=[bank[1]])
                for hh in range(2):
                    h = 2 * pr + hh
                    gC = float(GAMMAS[h] ** 128)
                    S.op(S.dve, lambda: nc.vector.scalar_tensor_tensor(
                        out=R32[:, h * 256:(h + 1) * 256], in0=R32[:, h * 256:(h + 1) * 256], scalar=gC,
                        in1=bank[0][:, hh * 256:(hh + 1) * 256], op0=ALU.mult, op1=ALU.add),
                        reads=[bank[1], R32_t], writes=[R32_t])
                if pr == 1:
                    px.pop(t)

            def pre_only(f, *a):
                return lambda t: f(t, *a) if t < NT else None
            hpre_tt = [Trk("hpre%d" % i) for i in range(NT)]
            hown_t = Trk("hown")
            ctxA = {}

            def a_s0(t):
                src = x_pre if t < NT else x_own
                tt = t % NT
                xa, xa_t = xt.next()
                S.dma(S.sp, xa[:], src[tt * 128:(tt + 1) * 128, :], xa_t, writes=[xa_t])
                ss, ss_t = tmp["ss"].next()
                sd, sd_t = tmp["sd"].next()
                rs, rs_t = tmp["rs"].next()
                xn, xn_t = tmp["xn"].next()
                junk, junk_t = junkr.next()
                S.op(S.act, lambda: nc.scalar.activation(out=junk[:], in_=xa[:], func=AF.Square, accum_out=ss[:, 0:1]),
                     reads=[xa_t], writes=[ss_t, junk_t])
                rstd_pool(rs[:, 0:1], ss[:, 0:1], 1.0 / D, 1, ss_t, sd[:, 0:1], sd_t, rs_t)
                S.op(S.dve, lambda: nc.vector.scalar_tensor_tensor(out=xn[:], in0=xa[:], scalar=rs[:, 0:1], in1=gmix_b[:],
                                                                   op0=ALU.mult, op1=ALU.mult),
                     reads=[xa_t, rs_t, gmix_t], writes=[xn_t])
                ctxA[t] = (xn, xn_t)

            def a_s1(t):
                xn, xn_t = ctxA.pop(t)
                tt = t % NT
                k = t % 2
                transposes(xn[:], 8, xn_t, ptb[k], ptb_t[k])
                dstT = hT_pre if t < NT else hT_own
                S.op(S.act, lambda: nc.scalar.copy(out=dstT[:, :, tt * 128:(tt + 1) * 128],
                                                   in_=ptb[k].rearrange("p (c t) -> p c t", c=8)),
                     reads=[ptb_t[k]], writes=[hpre_tt[t] if t < NT else hown_t])
            pipeline(2 * NT, [a_s0, pre_only(pp0), pre_only(ppu, 1), pre_only(ppu, 0), a_s1, pre_only(pp1)], [0, 4, 7, 7, 2, 5])
            S.barrier()
            if stop_after == "A":
                dbg_dump(hT_own[:].rearrange("p c t -> p (c t)"), 16384, True, ra)
                finish()
                return nc

        with ExitStack() as es:
            ra = mk_alloc(es, "right")
            qg_b = ra("qg_b", [128, 384], F32)
            kg_b = ra("kg_b", [128, 384], F32)
            gain_t = Trk("gains")
            S.dma(S.sp, qg_b[:], bcast_rows(q_gain, 384), gain_t, writes=[gain_t])
            S.dma(S.sp, kg_b[:], bcast_rows(k_gain, 384), gain_t, writes=[gain_t])
            vaug = ra("vaug", [128, 32, 4, 132], BF16)
            vaug_t = Trk("vaug")
            kst = actT[:].rearrange("p (i h t) -> p i h t", i=32, h=4)
            kst_t = Trk("kst")
            ctr = Ring(ra, "ct", [128, 128], F32, 3)
            strg = Ring(ra, "st", [128, 128], F32, 3)
            cgr = Ring(ra, "cg", [128, 128], F32, 3)
            sgr = Ring(ra, "sg", [128, 128], F32, 3)
            ss4r = Ring(ra, "ss4", [128, 4], F32, 3)
            ln4r = Ring(ra, "ln4", [128, 4], F32, 3)
            rs4r = Ring(ra, "rs4", [128, 4], F32, 3)
            tAr = Ring(ra, "tA", [128, 512], F32, 2)
            tBr = Ring(ra, "tB", [128, 512], F32, 2)
            qkfr = Ring(ra, "qkf", [128, 512], BF16, 5)
            qTr = Ring(ra, "qT", [128, 512], BF16, 4)
            Er = Ring(ra, "E", [128, 512], BF16, 4)
            Pr = Ring(ra, "P", [128, 512], BF16, 8)
            utr = Ring(ra, "ut", [128, 4, 132], F32, 2)
            for ut_, ut_t_ in zip(utr.tiles, utr.trks):
                S.op(S.pool, lambda: nc.gpsimd.memset(ut_[:], 0.0), writes=[ut_t_])

            def load_w(dst, dst_t, col0):
                S.dma(S.pool, dst[:], w_in[:, col0:col0 + 512].rearrange("(c p) n -> p c n", p=128), dst_t, writes=[dst_t])

            def hsel(base, d, c):
                if base < TOK:
                    return hT_pre[:, c, base:base + 127 * d + 1:d]
                b = base - TOK
                return hT_own[:, c, b:b + 127 * d + 1:d]

            def tables(base, d, gain_b, g):
                ct, ct_t = ctr.next()
                st, st_t = strg.next()
                S.dma(S.sp, ct[:], acos[base:base + 127 * d + 1:d, :], ct_t, writes=[ct_t])
                S.dma(S.sp, st[:], asin[base:base + 127 * d + 1:d, :], st_t, writes=[st_t])
                cg, cg_t = cgr.next()
                sg, sg_t = sgr.next()
                gsl = gain_b[:, g * 128:(g + 1) * 128]
                gsw = bass.AP(gsl.tensor, gsl.offset + 64, [list(gsl.ap[0]), [-64, 2], [1, 64]])
                S.op(S.pool, lambda: nc.gpsimd.tensor_tensor(out=cg[:], in0=ct[:], in1=gsl, op=ALU.mult),
                     reads=[ct_t, gain_t], writes=[cg_t])
                S.op(S.pool, lambda: nc.gpsimd.tensor_tensor(out=sg[:].rearrange("p (a f) -> p a f", a=2),
                                                             in0=st[:].rearrange("p (a f) -> p a f", a=2), in1=gsw, op=ALU.mult),
                     reads=[st_t, gain_t], writes=[sg_t])
                return cg, cg_t, sg, sg_t

            def norm_rope(z, z_t, cg, cg_t, sg, sg_t):
                ss4, ss4_t = ss4r.next()
                ln4, ln4_t = ln4r.next()
                rs4, rs4_t = rs4r.next()
                tA, tA_t = tAr.next()
                tB, tB_t = tBr.next()
                of, of_t = qkfr.next()

                def sq():
                    ins = None
                    for h in range(4):
                        ins = nc.scalar.activation(out=junk[:, h * 128:(h + 1) * 128], in_=z[:, h * 128:(h + 1) * 128], func=AF.Square,
                                                   accum_out=ss4[:, h:h + 1])
                    return ins
                junk, junk_t = junkr.next()
                S.op(S.act, sq, reads=[z_t], writes=[ss4_t, junk_t])
                if False:
                    def pw():
                        nc.gpsimd.tensor_scalar(out=ln4[:], in0=ss4[:], scalar1=1.0 / 128, scalar2=EPS, op0=ALU.mult, op1=ALU.add)
                        return nc.gpsimd.tensor_tensor(out=rs4[:], in0=ln4[:], in1=nhalf[:, 0:4], op=ALU.pow)
                    S.op(S.pool, pw, reads=[ss4_t, consts_t], writes=[ln4_t, rs4_t])
                elif True:
                    S.op(S.act, lambda: nc.scalar.activation(out=ln4[:], in_=ss4[:], func=AF.Sqrt, bias=epsc[:, 0:1], scale=1.0 / 128),
                         reads=[ss4_t, consts_t], writes=[ln4_t])
                    S.op(S.dve, lambda: nc.vector.reciprocal(out=rs4[:], in_=ln4[:]), reads=[ln4_t], writes=[rs4_t])
                else:
                    S.op(S.act, lambda: nc.scalar.activation(out=ln4[:], in_=ss4[:], func=AF.Ln, bias=epsc[:, 0:1], scale=1.0 / 128),
                         reads=[ss4_t, consts_t], writes=[ln4_t])
                    S.op(S.act, lambda: nc.scalar.activation(out=rs4[:], in_=ln4[:], func=AF.Exp, scale=-0.5),
                         reads=[ln4_t], writes=[rs4_t])
                cgb = cg[:].unsqueeze(1).to_broadcast([128, 4, 128])
                sgb = sg[:].rearrange("p (a f) -> p a f", a=2).unsqueeze(1).to_broadcast([128, 4, 2, 64])
                S.op(S.dve, lambda: nc.vector.tensor_tensor(out=hv(tA[:], 4, 128), in0=hv(z, 4, 128), in1=cgb, op=ALU.mult),
                     reads=[z_t, cg_t], writes=[tA_t])
                S.op(S.dve, lambda: nc.vector.tensor_tensor(out=tB[:].rearrange("p (h a f) -> p h a f", h=4, a=2),
                                                            in0=swap_view(z, 4), in1=sgb, op=ALU.mult),
                     reads=[z_t, sg_t], writes=[tB_t])
                S.op(S.pool, lambda: nc.gpsimd.tensor_tensor(out=tA[:], in0=tA[:], in1=tB[:], op=ALU.add),
                     reads=[tA_t, tB_t], writes=[tA_t])
                S.op(S.dve, lambda: nc.vector.tensor_tensor(out=hv(of[:], 4, 128), in0=hv(tA[:], 4, 128),
                                                            in1=rs4[:].unsqueeze(2).to_broadcast([128, 4, 128]), op=ALU.mult),
                     reads=[tA_t, rs4_t], writes=[of_t])
                return of, of_t

            zkb = [(pz[0], pz_t[0]), (pz[1], pz_t[1])]
            zvb = [(pz[2], pz_t[2]), (ps[0], ps_t[0])]

            for g, d in ((2, 16), (1, 4), (0, 1)):
                Bt = 128 * d
                nb = TOK // Bt
                npre = d
                bases = [TOK - Bt + r for r in range(d)] + [TOK + n * Bt + r for n in range(nb) for r in range(d)]
                ntl = len(bases)
                S.op(S.pool, lambda: nc.gpsimd.memset(vaug[:, 0:ntl, :, 128:129], 1.0), writes=[vaug_t])
                S.op(S.pool, lambda: nc.gpsimd.tensor_copy(
                    out=vaug[:, 0:npre, :, 128:129],
                    in_=flag[:, 0:4].unsqueeze(1).unsqueeze(3).to_broadcast([128, npre, 4, 1])),
                    reads=[consts_t], writes=[vaug_t])
                cx = {}

                def k_s0(i):
                    base = bases[i]
                    tb = tables(base, d, kg_b, g)
                    zk, zk_t = zkb[i % 2]
                    zv, zv_t = zvb[i % 2]
                    mm_group(zk[:], [(hsel(base, d, c), wK[:, c, :]) for c in range(8)], [wK_t], zk_t)
                    mm_group(zv[:], [(hsel(base, d, c), wV[:, c, :]) for c in range(8)], [wV_t], zv_t)
                    cx[i] = tb

                def k_s1(i):
                    cg, cg_t, sg, sg_t = cx[i]
                    zk, zk_t = zkb[i % 2]
                    zv, zv_t = zvb[i % 2]
                    S.op(S.act, lambda: nc.scalar.copy(out=vaug[:, i, :, 0:128], in_=hv(zv[:], 4, 128)),
                         reads=[zv_t], writes=[vaug_t])
                    cx[i] = norm_rope(zk[:], zk_t, cg, cg_t, sg, sg_t)

                def k_s2(i):
                    kf, kf_t = cx.pop(i)
                    k = i % 2
                    transposes(kf[:], 4, kf_t, ptb[k], ptb_t[k])
                    S.op(S.dve, lambda: nc.vector.tensor_copy(out=kst[:, i, :, :], in_=ptb[k][:, 0:512].rearrange("p (h t) -> p h t", h=4)),
                         reads=[ptb_t[k]], writes=[kst_t])
                pipeline(ntl, [k_s0, k_s1, k_s2], [0, 1, 4])

                qx = {}
                nq = nb * d

                def q_s0(j):
                    i = npre + j
                    base = bases[i]
                    tb = tables(base, d, qg_b, g)
                    zq, zq_t = zkb[j % 2]
                    mm_group(zq[:], [(hsel(base, d, c), wQ[:, c, :]) for c in range(8)], [wQ_t], zq_t)
                    qx[j] = dict(tb=tb)

                def q_s1(j):
                    cg, cg_t, sg, sg_t = qx[j]["tb"]
                    zq, zq_t = zkb[j % 2]
                    qx[j]["qf"] = norm_rope(zq[:], zq_t, cg, cg_t, sg, sg_t)

                def q_s2(j):
                    qf, qf_t = qx[j]["qf"]
                    k = j % 2
                    transposes(qf[:], 4, qf_t, ptb[k], ptb_t[k])
                    qT, qT_t = qTr.next()
                    S.op(S.dve, lambda: nc.vector.tensor_copy(out=qT[:], in_=ptb[k][:, 0:512]), reads=[ptb_t[k]], writes=[qT_t])
                    qx[j]["qT"] = (qT, qT_t)

                sbank = [(pz[2], pz_t[2]), (ps[0], ps_t[0])]
                ubank = [(ps[1], ps_t[1]), (po[0], po_t[0])]

                def q_s3(j):
                    i = npre + j
                    qT, qT_t = qx[j]["qT"]
                    for pr in range(2):
                        sb_, sb_t = sbank[pr]

                        def smm():
                            ins = None
                            for hh in range(2):
                                h = 2 * pr + hh
                                for kk in range(2):
                                    ki = i - d if kk == 0 else i
                                    c0 = (hh * 2 + kk) * 128
                                    ins = nc.tensor.matmul(sb_[:, c0:c0 + 128], kst[:, ki, h, :], qT[:, h * 128:(h + 1) * 128],
                                                           start=True, stop=True)
                            return ins
                        S.op(S.pe, smm, reads=[kst_t, qT_t], writes=[sb_t])

                def q_s4(j):
                    Ps = []
                    for pr in range(2):
                        sb_, sb_t = sbank[pr]
                        E, E_t = Er.next()
                        P, P_t = Pr.next()
                        S.op(S.act, lambda: nc.scalar.activation(out=E[:], in_=sb_[:], func=AF.Exp, scale=ATT_SCALE),
                             reads=[sb_t], writes=[E_t])
                        meng, mh = (S.pool, nc.gpsimd) if pr == 0 else (S.dve, nc.vector)
                        S.op(meng, lambda: mh.tensor_tensor(
                            out=hv(P[:], 2, 256), in0=hv(E[:], 2, 256),
                            in1=msk[:].unsqueeze(1).to_broadcast([128, 2, 256]), op=ALU.mult),
                            reads=[E_t, consts_t], writes=[P_t])
                        Ps.append((P, P_t))
                    qx[j]["P"] = Ps

                def q_s5(j):
                    i = npre + j
                    for pr in range(2):
                        P, P_t = qx[j]["P"][pr]
                        ub, ub_t = ubank[pr]

                        def umm():
                            ins = None
                            for hh in range(2):
                                h = 2 * pr + hh
                                o_ap = ub[:, hh * 256:hh * 256 + 129]
                                nc.tensor.matmul(o_ap, P[:, hh * 256:hh * 256 + 128], vaug[:, i - d, h, 0:129], start=True, stop=False)
                                ins = nc.tensor.matmul(o_ap, P[:, hh * 256 + 128:hh * 256 + 256], vaug[:, i, h, 0:129], start=False, stop=True)
                            return ins
                        S.op(S.pe, umm, reads=[P_t, vaug_t], writes=[ub_t])

                def q_s6(j):
                    i = npre + j
                    ut, ut_t = utr.next()
                    for pr in range(2):
                        ub, ub_t = ubank[pr]
                        S.op(S.dve, lambda: nc.vector.tensor_copy(out=ut[:, 2 * pr:2 * pr + 2, 0:129],
                                                                  in_=hv(ub[:], 2, 256)[:, :, 0:129]),
                             reads=[ub_t], writes=[ut_t])
                    b0 = bases[i] - TOK
                    S.dma(S.sp, scr[g, b0:b0 + 127 * d + 1:d, :], ut[:].rearrange("p h w -> p (h w)"), ut_t, reads=[ut_t])
                    qx.pop(j)
                if g > 0:
                    load_w(wK, wK_t, 4608 + (g - 1) * 512)
                    load_w(wV, wV_t, 6144 + (g - 1) * 512)
                pipeline(nq, [q_s0, q_s2, q_s1, q_s3, q_s4, q_s5, q_s6], [0, 4, 1, 6, 7, 9, 10])
                if g > 0:
                    load_w(wQ, wQ_t, 3072 + (g - 1) * 512)
            S.barrier()

        aes.close()
        bes = ExitStack()
        ges.enter_context(bes)
        Wr = mk_alloc(bes, "right")("Wr", [128, 8, 3072], BF16)
        Wr_t = Trk("Wr")
        wd_t = Trk("wd1")
        for k in range(6):
            S.dma(S.pool, Wr[:, :, k * 512:(k + 1) * 512], w_in[:, k * 512:(k + 1) * 512].rearrange("(c p) n -> p c n", p=128),
                  Wr_t, writes=[Wr_t])
        with ExitStack() as es:
            ra = mk_alloc(es, "right")
            mgr = Ring(ra, "mg", [128, 3, 528], F32, 4)
            usr = Ring(ra, "us", [128, 528], F32, 2)
            rlr = Ring(ra, "rl", [128, 4], F32, 2)
            obr = Ring(ra, "ob", [128, 512], BF16, 4)
            otr_ = Ring(ra, "oTt", [128, 512], BF16, 2)
            scro_t = [Trk("scro%d" % i) for i in range(NT)]
            mx = {}

            def m_s0(t):
                mg, mg_t = mgr.next()
                S.dma(S.sp, mg[:], scr[:, t * 128:(t + 1) * 128, :].transpose([1, 0, 2]), mg_t, writes=[mg_t])
                mx[t] = (mg, mg_t)

            def m_s1(t):
                mg, mg_t = mx[t]
                us, us_t = usr.next()
                rl, rl_t = rlr.next()
                ob, ob_t = obr.next()
                S.op(S.dve, lambda: nc.vector.tensor_tensor(out=us[:], in0=mg[:, 0, :], in1=mg[:, 1, :], op=ALU.add),
                     reads=[mg_t], writes=[us_t])
                S.op(S.pool, lambda: nc.gpsimd.tensor_tensor(out=us[:], in0=us[:], in1=mg[:, 2, :], op=ALU.add),
                     reads=[mg_t, us_t], writes=[us_t])
                us3 = us[:].rearrange("p (h w) -> p h w", h=4)
                S.op(S.dve, lambda: nc.vector.reciprocal(out=rl[:].unsqueeze(2), in_=us3[:, :, 128:129]), reads=[us_t], writes=[rl_t])
                S.op(S.dve, lambda: nc.vector.tensor_tensor(out=hv(ob[:], 4, 128), in0=us3[:, :, 0:128],
                                                            in1=rl[:].unsqueeze(2).to_broadcast([128, 4, 128]), op=ALU.mult),
                     reads=[us_t, rl_t], writes=[ob_t])
                mx[t] = (ob, ob_t)

            def m_s2(t):
                ob, ob_t = mx.pop(t)
                k = t % 2
                transposes(ob[:], 4, ob_t, ptb[k], ptb_t[k])
                ot_, ot_t_ = otr_.next()
                S.op(S.act, lambda: nc.scalar.copy(out=ot_[:], in_=ptb[k][:, 0:512]), reads=[ptb_t[k]], writes=[ot_t_])
                S.dma(S.sp, scr_o[t], ot_[:], ot_t_, reads=[ot_t_], writes=[scro_t[t]])
            pipeline(NT, [m_s0, m_s1, m_s2], [0, 2, 4])
            S.barrier()

        with ExitStack() as es:
            ra = mk_alloc(es, "right")
            rc_t = Trk("rconst")
            MT = ra("MT", [128, 512], F32)
            rt = ra("rt", [128, 12], F32)
            gn_b = ra("gn_b", [128, D], F32)
            S.dma(S.sp, MT[:], mt_tab[:, :], rc_t, writes=[rc_t])
            S.dma(S.sp, rt[:], rtab[:, :], rc_t, writes=[rc_t])
            S.dma(S.sp, gn_b[:], bcast_rows(ret_gn, D), rc_t, writes=[rc_t])
            RD = 5
            rcr = Ring(ra, "rc", [128, 128], F32, 3)
            rsr = Ring(ra, "rs_", [128, 128], F32, 3)
            tAr = Ring(ra, "rtA", [128, 512], F32, 2)
            tBr = Ring(ra, "rtB", [128, 512], F32, 2)
            kfr = Ring(ra, "kf", [128, 512], BF16, 3)
            kdr = Ring(ra, "kd", [128, 512], BF16, 3)
            vr = Ring(ra, "v", [128, 1024], BF16, 5)

            def rope(z, z_t, rc, rc_tk, rs, rs_tk, ring):
                tA, tA_t = tAr.next()
                tB, tB_t = tBr.next()
                of, of_t = ring.next()
                cb = rc[:].unsqueeze(1).to_broadcast([128, 4, 128])
                sb_ = rs[:].rearrange("p (a f) -> p a f", a=2).unsqueeze(1).to_broadcast([128, 4, 2, 64])
                S.op(S.dve, lambda: nc.vector.tensor_tensor(out=hv(tA[:], 4, 128), in0=hv(z[:], 4, 128), in1=cb, op=ALU.mult),
                     reads=[z_t, rc_tk], writes=[tA_t])
                S.op(S.dve, lambda: nc.vector.tensor_tensor(out=tB[:].rearrange("p (h a f) -> p h a f", h=4, a=2),
                                                            in0=swap_view(z[:], 4), in1=sb_, op=ALU.mult),
                     reads=[z_t, rs_tk], writes=[tB_t])
                S.op(S.pool, lambda: nc.gpsimd.tensor_tensor(out=of[:], in0=tA[:], in1=tB[:], op=ALU.add),
                     reads=[tA_t, tB_t], writes=[of_t])
                return of, of_t

            BK, BQ, BV0, BV1 = (pz[0], pz_t[0]), (pz[1], pz_t[1]), (pz[2], pz_t[2]), (ps[0], ps_t[0])
            BS = (ps[1], ps_t[1])
            rx = {}

            def mk_stages(hsrc, own, toff):
                def hT_(t, c):
                    return hsrc[:, c, t * 128:(t + 1) * 128]

                def r0(t):
                    rc, rc_tk = rcr.next()
                    rs, rs_tk = rsr.next()
                    S.dma(S.sp, rc[:], rcos[(toff + t) * 128:(toff + t + 1) * 128, :], rc_tk, writes=[rc_tk])
                    S.dma(S.sp, rs[:], rsin[(toff + t) * 128:(toff + t + 1) * 128, :], rs_tk, writes=[rs_tk])
                    mm_group(BK[0][:], [(hT_(t, c), Wr[:, c, 512:1024]) for c in range(8)], [Wr_t], BK[1])
                    if own:
                        mm_group(BQ[0][:], [(hT_(t, c), Wr[:, c, 0:512]) for c in range(8)], [Wr_t], BQ[1])
                    mm_group(BV0[0][:], [(hT_(t, c), Wr[:, c, 1024:1536]) for c in range(8)], [Wr_t], BV0[1])
                    mm_group(BV1[0][:], [(hT_(t, c), Wr[:, c, 1536:2048]) for c in range(8)], [Wr_t], BV1[1])
                    rx[(own, t)] = dict(tab=(rc, rc_tk, rs, rs_tk))

                def r1(t):
                    X = rx[(own, t)]
                    rc, rc_tk, rs, rs_tk = X["tab"]
                    v, v_t = vr.next()
                    S.op(S.act, lambda: nc.scalar.copy(out=v[:, 0:512], in_=BV0[0][:]), reads=[BV0[1]], writes=[v_t])
                    S.op(S.act, lambda: nc.scalar.copy(out=v[:, 512:1024], in_=BV1[0][:]), reads=[BV1[1]], writes=[v_t])
                    kf, kf_t = rope(BK[0], BK[1], rc, rc_tk, rs, rs_tk, kfr)
                    kd, kd_t = kdr.next()
                    S.op(S.pool, lambda: nc.gpsimd.tensor_tensor(out=hv(kd[:], 4, 128), in0=hv(kf[:], 4, 128),
                                                                 in1=rt[:, 0:4].unsqueeze(2).to_broadcast([128, 4, 128]), op=ALU.mult),
                         reads=[kf_t, rc_t], writes=[kd_t])
                    X.update(v=(v, v_t), kf=(kf, kf_t), kd=(kd, kd_t))
                    if own:
                        X["qf"] = rope(BQ[0], BQ[1], rc, rc_tk, rs, rs_tk, qfr)

                def upd(t, pr, bank):
                    X = rx[(own, t)]
                    kd, kd_t = X["kd"]
                    v, v_t = X["v"]

                    def kvmm():
                        ins = None
                        for hh in range(2):
                            h = 2 * pr + hh
                            ins = nc.tensor.matmul(bank[0][:, hh * 256:(hh + 1) * 256], kd[:, h * 128:(h + 1) * 128],
                                                   v[:, h * 256:(h + 1) * 256], start=True, stop=True)
                        return ins
                    S.op(S.pe, kvmm, reads=[kd_t, v_t], writes=[bank[1]])
                    for hh in range(2):
                        h = 2 * pr + hh
                        gC = float(GAMMAS[h] ** 128)
                        S.op(S.dve, lambda: nc.vector.scalar_tensor_tensor(
                            out=R32[:, h * 256:(h + 1) * 256], in0=R32[:, h * 256:(h + 1) * 256], scalar=gC,
                            in1=bank[0][:, hh * 256:(hh + 1) * 256], op0=ALU.mult, op1=ALU.add),
                            reads=[bank[1], R32_t], writes=[R32_t])
                return r0, r1, upd

            pes.close()

            with ExitStack() as es3:
                la = mk_alloc(es3, "left")
                qkTr = Ring(la, "qkT", [128, 1024], BF16, 4)
                osbr = Ring(la, "osb", [128, 1024], F32, 3)
                qfr = Ring(la, "qf", [128, 512], BF16, 3)
                Pr = Ring(la, "rP", [128, 512], BF16, 3)
                sglr = Ring(la, "sgl", [128, 1024], BF16, 4)
                rbr = Ring(la, "rb", [128, 1024], BF16, 2)
                Rbfr = Ring(la, "Rbf", [128, 1024], BF16, RD)
                S.op(S.pool, lambda: nc.gpsimd.tensor_copy(out=Rbfr.tiles[0][:], in_=R32[:]), reads=[R32_t], writes=[Rbfr.trks[0]])
                s4 = {k: Ring(la, "r4" + k, [128, 4], F32, 3) for k in ("ss", "t", "sd", "rr", "rstd")}
                o0, o1, oupd = mk_stages(hT_own, True, NT)

                def og(t):
                    sgl, sgl_t = sglr.next()
                    for hf, bk in ((0, BV0), (1, BV1)):
                        mm_group(bk[0][:], [(hT_own[:, c, t * 128:(t + 1) * 128], Wr[:, c, 2048 + hf * 512:2560 + hf * 512]) for c in range(8)],
                                 [Wr_t], bk[1])
                        S.op(S.act, lambda: nc.scalar.activation(out=sgl[:, hf * 512:(hf + 1) * 512], in_=bk[0][:], func=AF.Silu),
                             reads=[bk[1]], writes=[sgl_t])
                    S.op(S.pool, lambda: nc.gpsimd.tensor_tensor(out=sgl[:], in0=sgl[:], in1=gn_b[:], op=ALU.mult),
                         reads=[sgl_t, rc_t], writes=[sgl_t])
                    rx[(True, t)]["sgl"] = (sgl, sgl_t)
                    if t == NT - 1:
                        for k in range(4):
                            S.dma(S.pool, Wr[:, :, k * 512:(k + 1) * 512],
                                  w_in[:, 7680 + k * 512:7680 + (k + 1) * 512].rearrange("(c p) n -> p c n", p=128),
                                  wd_t, writes=[wd_t, Wr_t])
                        for k in range(2):
                            S.dma(S.pool, Wr[:, :, 2048 + k * 512:2048 + (k + 1) * 512],
                                  w_ro[:, k * 512:(k + 1) * 512].rearrange("(c p) n -> p c n", p=128), wd_t, writes=[wd_t, Wr_t])

                def o2(t):
                    X = rx[(True, t)]
                    qf, qf_t = X["qf"]
                    kf, kf_t = X["kf"]
                    qkT, qkT_t = qkTr.next()
                    transposes(qf[:], 4, qf_t, ptb[0], ptb_t[0], 0)
                    transposes(kf[:], 4, kf_t, ptb[0], ptb_t[0], 512)
                    S.op(S.act, lambda: nc.scalar.copy(out=qkT[:], in_=ptb[0]), reads=[ptb_t[0]], writes=[qkT_t])
                    X["qkT"] = (qkT, qkT_t)

                def o3(t):
                    X = rx[(True, t)]
                    qkT, qkT_t = X["qkT"]

                    def smm():
                        ins = None
                        for h in range(4):
                            ins = nc.tensor.matmul(BS[0][:, h * 128:(h + 1) * 128], qkT[:, 512 + h * 128:512 + (h + 1) * 128],
                                                   qkT[:, h * 128:(h + 1) * 128], start=True, stop=True)
                        return ins
                    S.op(S.pe, smm, reads=[qkT_t], writes=[BS[1]])
                    P, P_t = Pr.next()
                    S.op(S.dve, lambda: nc.vector.tensor_tensor(out=P[:], in0=BS[0][:], in1=MT[:], op=ALU.mult),
                         reads=[BS[1], rc_t], writes=[P_t])
                    X["P"] = (P, P_t)

                def o4(t):
                    X = rx[(True, t)]
                    qkT, qkT_t = X["qkT"]
                    P, P_t = X["P"]
                    v, v_t = X["v"]
                    Rbf, Rbf_t = Rbfr.tiles[t % RD], Rbfr.trks[t % RD]
                    for pr in range(2):
                        def omm():
                            ins = None
                            for hh in range(2):
                                h = 2 * pr + hh
                                o_ap = OB[pr][0][:, hh * 256:(hh + 1) * 256]
                                nc.tensor.matmul(o_ap, P[:, h * 128:(h + 1) * 128], v[:, h * 256:(h + 1) * 256], start=True, stop=False)
                                ins = nc.tensor.matmul(o_ap, qkT[:, h * 128:(h + 1) * 128], Rbf[:, h * 256:(h + 1) * 256],
                                                       start=False, stop=True)
                            return ins
                        S.op(S.pe, omm, reads=[P_t, v_t, qkT_t, Rbf_t], writes=[OB[pr][1]])

                def ou0(t):
                    oupd(t, 0, BS)

                def ou1(t):
                    oupd(t, 1, BS)
                    k = (t + 1) % RD
                    S.op(S.pool, lambda: nc.gpsimd.tensor_copy(out=Rbfr.tiles[k][:], in_=R32[:]), reads=[R32_t], writes=[Rbfr.trks[k]])

                def o5a(t):
                    X = rx[(True, t)]
                    osb, osb_t = osbr.next()
                    for pr in range(2):
                        S.op(S.act, lambda: nc.scalar.copy(out=osb[:, pr * 512:(pr + 1) * 512], in_=OB[pr][0][:]),
                             reads=[OB[pr][1]], writes=[osb_t])
                    X["osb"] = (osb, osb_t)

                def o5(t):
                    X = rx[(True, t)]
                    sgl, sgl_t = X["sgl"]
                    osb, osb_t = X["osb"]
                    ss4, ss4_t = s4["ss"].next()
                    t4, t4_t = s4["t"].next()
                    sd4, sd4_t = s4["sd"].next()
                    rr4, rr4_t = s4["rr"].next()
                    rstd4, rstd4_t = s4["rstd"].next()

                    def sq():
                        ins = None
                        for h in range(4):
                            ins = nc.scalar.activation(out=junk[:, h * 256:(h + 1) * 256], in_=osb[:, h * 256:(h + 1) * 256],
                                                       func=AF.Square, accum_out=ss4[:, h:h + 1])
                        return ins
                    junk, junk_t = junkr.next()
                    S.op(S.act, sq, reads=[osb_t], writes=[ss4_t, junk_t])
                    S.op(S.dve, lambda: nc.vector.tensor_tensor(out=t4[:], in0=ss4[:], in1=rt[:, 8:12], op=ALU.mult),
                         reads=[ss4_t, rc_t], writes=[t4_t])
                    if True:
                        rstd_pool(rr4[:], t4[:], 1.0, 4, t4_t, sd4[:], sd4_t, rr4_t)
                    else:
                        S.op(S.act, lambda: nc.scalar.activation(out=sd4[:], in_=t4[:], func=AF.Sqrt, bias=epsc[:, 0:1], scale=1.0),
                             reads=[t4_t, consts_t], writes=[sd4_t])
                        S.op(S.dve, lambda: nc.vector.reciprocal(out=rr4[:], in_=sd4[:]), reads=[sd4_t], writes=[rr4_t])
                    S.op(S.dve, lambda: nc.vector.tensor_tensor(out=rstd4[:], in0=rr4[:], in1=rt[:, 4:8], op=ALU.mult),
                         reads=[rr4_t, rc_t], writes=[rstd4_t])
                    rb, rb_t = rbr.next()
                    for h in range(4):
                        S.op(S.dve, lambda: nc.vector.scalar_tensor_tensor(
                            out=rb[:, h * 256:(h + 1) * 256], in0=osb[:, h * 256:(h + 1) * 256],
                            scalar=rstd4[:, h:h + 1], in1=sgl[:, h * 256:(h + 1) * 256], op0=ALU.mult, op1=ALU.mult),
                            reads=[osb_t, rstd4_t, sgl_t], writes=[rb_t])
                    X["rb"] = (rb, rb_t)

                def o6(t):
                    rb, rb_t = rx.pop((True, t))["rb"]
                    transposes(rb[:], 8, rb_t, ptb[0], ptb_t[0])
                    S.op(S.act, lambda: nc.scalar.copy(out=actT3[:, :, t * 128:(t + 1) * 128],
                                                       in_=ptb[0].rearrange("p (c t) -> p c t", c=8)),
                         reads=[ptb_t[0]], writes=[actT_t[t]])
                OB = [(po[0], po_t[0]), (po[1], po_t[1])]
                pipeline(NT, [o5, o6, ou1, o0, ou0, og, o2, o3, o4, o5a, o1], [7, 8, 2, 0, 2, 4, 3, 4, 5, 6, 1])
            S.barrier()
            if stop_after == "B":
                dbg_dump(actT[:], 16384, True, ra)
                finish()
                return nc


        wes = ExitStack()
        ges.enter_context(wes)
        wo = mk_alloc(wes, "right")("wo", [128, 8, 1024], BF16)
        wo_t = Trk("wo")
        with ExitStack() as es:
            ra = mk_alloc(es, "right")
            Wgl = Wr[:, :, 0:2048]
            wro = Wr[:, :, 2048:3072]
            wao = ra("wao", [128, 4, 1024], BF16)
            bg_b = ra("bg_b", [128, 2048], F32)
            wao_t = Trk("wao")
            bg_t = Trk("bg")
            S.dma(S.sp, bg_b[:], bcast_rows(b_gate, 2048), bg_t, writes=[bg_t])
            for k in range(2):
                S.dma(S.pool, wao[:, :, k * 512:(k + 1) * 512], w_ao[:, k * 512:(k + 1) * 512].rearrange("(c p) n -> p c n", p=128),
                      wao_t, writes=[wao_t])
            for k in range(2):
                S.dma(S.pool, wo[:, :, k * 512:(k + 1) * 512], w_o[:, k * 512:(k + 1) * 512].rearrange("(c p) n -> p c n", p=128),
                      wo_t, writes=[wo_t])
            t0r = Ring(ra, "t0", [128, 512], F32, 4)
            grr = Ring(ra, "gr", [128, 512], F32, 4)
            m1r = Ring(ra, "m1", [128, 512], F32, 2)
            m2r = Ring(ra, "m2", [128, 512], F32, 2)
            mr = Ring(ra, "m", [128, 1024], BF16, 2)
            oTr = Ring(ra, "oTl", [128, 512], BF16, 3)
            oTl = {}

            def load_oT(t):
                if t < NT:
                    a, a_t = oTr.next()
                    S.dma(S.sp, a[:], scr_o[t], a_t, reads=[scro_t[t]], writes=[a_t])
                    oTl[t] = (a, a_t)
            load_oT(0)
            load_oT(1)
            for t in range(NT):
                load_oT(t + 2)
                oTt, oTt_t = oTl.pop(t)
                m, m_t = mr.next()
                for hf in range(2):
                    c0 = hf * 512
                    gb = [(pz[0], pz_t[0]), (pz[1], pz_t[1])] if hf == 0 else [(pz[2], pz_t[2]), (po[0], po_t[0])]
                    mm_group(gb[0][0][:], [(hT_own[:, c, t * 128:(t + 1) * 128], Wgl[:, c, c0:c0 + 512]) for c in range(8)], [wd_t], gb[0][1])
                    mm_group(gb[1][0][:], [(hT_own[:, c, t * 128:(t + 1) * 128], Wgl[:, c, 1024 + c0:1024 + c0 + 512]) for c in range(8)],
                             [wd_t], gb[1][1])
                    mm_group(ps[0][:], [(actT3[:, c, t * 128:(t + 1) * 128], wro[:, c, c0:c0 + 512]) for c in range(8)],
                             [wd_t, actT_t[t]], ps_t[0])
                    mm_group(ps[1][:], [(oTt[:, c * 128:(c + 1) * 128], wao[:, c, c0:c0 + 512]) for c in range(4)],
                             [wao_t, oTt_t], ps_t[1])
                    gates = []
                    for which, bank in ((0, 0), (1, 1)):
                        t0, t0_t = t0r.next()
                        gr, gr_t = grr.next()
                        S.op(S.dve, lambda: nc.vector.tensor_tensor(out=t0[:], in0=gb[bank][0][:],
                                                                    in1=bg_b[:, which * 1024 + c0:which * 1024 + c0 + 512], op=ALU.add),
                             reads=[gb[bank][1], bg_t], writes=[t0_t])
                        S.op(S.act, lambda: nc.scalar.activation(out=gr[:], in_=t0[:], func=AF.Sigmoid), reads=[t0_t], writes=[gr_t])
                        gates.append((gr, gr_t))
                    m1, m1_t = m1r.next()
                    m2, m2_t = m2r.next()
                    S.op(S.dve, lambda: nc.vector.tensor_tensor(out=m1[:], in0=ps[0][:], in1=gates[0][0][:], op=ALU.mult),
                         reads=[ps_t[0], gates[0][1]], writes=[m1_t])
                    S.op(S.dve, lambda: nc.vector.tensor_tensor(out=m2[:], in0=ps[1][:], in1=gates[1][0][:], op=ALU.mult),
                         reads=[ps_t[1], gates[1][1]], writes=[m2_t])
                    S.op(S.pool, lambda: nc.gpsimd.tensor_tensor(out=m[:, c0:c0 + 512], in0=m1[:], in1=m2[:], op=ALU.add),
                         reads=[m1_t, m2_t], writes=[m_t])
                transposes(m[:], 8, m_t)
                S.op(S.act, lambda: nc.scalar.copy(out=actT3[:, :, t * 128:(t + 1) * 128], in_=pt[:].rearrange("p (c t) -> p c t", c=8)),
                     reads=[pt_t], writes=[actT_t[t]])
            S.barrier()

        les.close()
        with ExitStack() as es2:
            rb2 = mk_alloc(es2, "left")
            x1_all = rb2("x1_all", [128, NT, D], F32)
            x1_t = [Trk("x1_%d" % i) for i in range(NT)]
            tmpn = {"ss": Ring(rb2, "ss", [128, 1], F32, 2), "sd": Ring(rb2, "sd", [128, 1], F32, 2),
                    "rs": Ring(rb2, "rs", [128, 1], F32, 2), "xn": Ring(rb2, "xn", [128, D], BF16, 2)}
            with ExitStack() as es:
                ra = mk_alloc(es, "right")
                xt = Ring(ra, "xt2", [128, D], F32, 3)
                gmlp_b = ra("gmlp_b", [128, D], F32)
                gmlp_t = Trk("gmlp")
                S.dma(S.sp, gmlp_b[:], bcast_rows(g_mlp, D), gmlp_t, writes=[gmlp_t])
                xnE = Ring(ra, "xnE", [128, D], BF16, 3)
                ex = {}

                def e_s0(t):
                    ss, ss_t = tmpn["ss"].next()
                    sd, sd_t = tmpn["sd"].next()
                    rs, rs_t = tmpn["rs"].next()
                    xn, xn_t = xnE.next()
                    xa = x1_all[:, t, :]
                    junk, junk_t = junkr.next()
                    S.op(S.act, lambda: nc.scalar.activation(out=junk[:], in_=xa, func=AF.Square, accum_out=ss[:, 0:1]),
                         reads=[x1_t[t]], writes=[ss_t, junk_t])
                    rstd_pool(rs[:, 0:1], ss[:, 0:1], 1.0 / D, 1, ss_t, sd[:, 0:1], sd_t, rs_t)
                    S.op(S.dve, lambda: nc.vector.scalar_tensor_tensor(out=xn[:], in0=xa, scalar=rs[:, 0:1], in1=gmlp_b[:],
                                                                       op0=ALU.mult, op1=ALU.mult),
                         reads=[x1_t[t], rs_t, gmlp_t], writes=[xn_t])
                    ex[t] = (xn, xn_t)

                def e_s1(t):
                    xn, xn_t = ex.pop(t)
                    k = t % 2
                    transposes(xn[:], 8, xn_t, ptb[k], ptb_t[k])
                    S.op(S.act, lambda: nc.scalar.copy(out=actT3[:, :, t * 128:(t + 1) * 128],
                                                       in_=ptb[k].rearrange("p (c t) -> p c t", c=8)),
                         reads=[ptb_t[k]], writes=[actT_t[t]])
                dx = {}

                def d_s0(t):
                    xa, xa_t = xt.next()
                    S.dma(S.sp, xa[:], x_own[t * 128:(t + 1) * 128, :], xa_t, writes=[xa_t])
                    for hf in range(2):
                        c0 = hf * 512
                        mm_group(pz[hf][:], [(actT3[:, c, t * 128:(t + 1) * 128], wo[:, c, c0:c0 + 512]) for c in range(8)],
                                 [wo_t, actT_t[t]], pz_t[hf])
                    dx[t] = (xa, xa_t)

                def d_s1(t):
                    xa, xa_t = dx.pop(t)
                    for hf in range(2):
                        c0 = hf * 512
                        S.op(S.dve, lambda: nc.vector.tensor_tensor(out=x1_all[:, t, c0:c0 + 512], in0=pz[hf][:], in1=xa[:, c0:c0 + 512], op=ALU.add),
                             reads=[pz_t[hf], xa_t], writes=[x1_t[t]])
                pipeline(NT, [d_s0, e_s0, e_s1, d_s1], [0, 2, 4, 1])
                S.barrier()
                if stop_after == "D":
                    dbg_dump(x1_all[:].rearrange("p t d -> p (t d)"), 16384, False, ra)
                    finish()
                    return nc

            wes.close()
            bes.close()
            fes = ExitStack()
            ges.enter_context(fes)
            fa = mk_alloc(fes, "right")
            gple_b = fa("gple_b", [128, D], F32)
            wpg = fa("wpg", [128, 8, 1024], BF16)
            wpp = fa("wpp", [128, 2, 1024], BF16)
            wf_t = Trk("wf")
            with ExitStack() as es:
                ra = mk_alloc(es, "right")
                NF = 8
                FW = 4096 // NF
                NJ = FW // 128
                wur = Ring(ra, "wu", [128, 8, FW], BF16, 2)
                wdr = Ring(ra, "wdn", [128, NJ, D], BF16, 2)
                aTr = Ring(ra, "aT", [128, NJ, TOK], BF16, 2)
                r1r = Ring(ra, "rl1", [128, 512], F32, 2)
                for fg in range(NF):
                    wu, wu_t = wur.next()
                    wdn, wdn_t = wdr.next()
                    aT, aT_t0 = aTr.next()
                    aT_tg = [Trk("aTtg%d" % i) for i in range(4)]
                    S.dma(S.pool, wu[:], w_up[:, fg * FW:(fg + 1) * FW].rearrange("(c p) n -> p c n", p=128), wu_t, writes=[wu_t])
                    S.dma(S.pool, wdn[:], w_down[fg * FW:(fg + 1) * FW, :].rearrange("(j p) n -> p j n", p=128), wdn_t, writes=[wdn_t])
                    if fg == 2:
                        S.dma(S.sp, gple_b[:], bcast_rows(g_ple, D), wf_t, writes=[wf_t])
                        for k in range(2):
                            S.dma(S.pool, wpg[:, :, k * 512:(k + 1) * 512], w_pg[:, k * 512:(k + 1) * 512].rearrange("(c p) n -> p c n", p=128),
                                  wf_t, writes=[wf_t])
                        S.dma(S.pool, wpp[:], w_pp[:, :].rearrange("(c p) n -> p c n", p=128), wf_t, writes=[wf_t])
                    k = 0
                    for tg in range(4):
                        for j in range(NJ):
                            bank = k % 3
                            k += 1
                            mm_group(pz[bank][:], [(wu[:, c, j * 128:(j + 1) * 128], actT3[:, c, tg * 512:(tg + 1) * 512]) for c in range(8)],
                                     [wu_t] + actT_t[tg * 4:(tg + 1) * 4], pz_t[bank])
                            r1, r1_t = r1r.next()
                            S.op(S.act, lambda: nc.scalar.activation(out=r1[:], in_=pz[bank][:], func=AF.Relu), reads=[pz_t[bank]], writes=[r1_t])
                            S.op(S.pool, lambda: nc.gpsimd.tensor_tensor(out=aT[:, j, tg * 512:(tg + 1) * 512], in0=r1[:], in1=r1[:], op=ALU.mult),
                                 reads=[r1_t], writes=[aT_tg[tg], aT_t0] if (tg, j) == (0, 0) else [aT_tg[tg]])
                    for t in range(NT):
                        for hf in range(2):
                            c0 = hf * 512
                            mm_group(ps[hf][:], [(aT[:, j, t * 128:(t + 1) * 128], wdn[:, j, c0:c0 + 512]) for j in range(NJ)],
                                     [aT_tg[t // 4], aT_t0, wdn_t], ps_t[hf])
                            S.op(S.dve, lambda: nc.vector.tensor_tensor(out=x1_all[:, t, c0:c0 + 512], in0=ps[hf][:], in1=x1_all[:, t, c0:c0 + 512], op=ALU.add),
                                 reads=[ps_t[hf], x1_t[t]], writes=[x1_t[t]])
                S.barrier()
                if stop_after == "E":
                    dbg_dump(x1_all[:].rearrange("p t d -> p (t d)"), 16384, False, ra)
                    finish()
                    return nc

            with ExitStack() as es:
                ra = mk_alloc(es, "right")
                h3r = Ring(ra, "h3T", [128, 8, 128], BF16, 4)
                pbr = Ring(ra, "pb", [128, 256], BF16, 4)
                pTr = Ring(ra, "pT", [128, 256], BF16, 4)
                gpr = Ring(ra, "gp", [128, 512], F32, 2)
                tfr = Ring(ra, "tf", [128, 512], F32, 2)
                otr = Ring(ra, "ot", [128, D], F32, 2)
                tmpn["xn"] = Ring(ra, "xnF", [128, D], BF16, 4)
                fx = {}

                def f_s0(t):
                    ss, ss_t = tmpn["ss"].next()
                    sd, sd_t = tmpn["sd"].next()
                    rs, rs_t = tmpn["rs"].next()
                    xn, xn_t = tmpn["xn"].next()
                    xa = x1_all[:, t, :]
                    junk, junk_t = junkr.next()
                    S.op(S.act, lambda: nc.scalar.activation(out=junk[:], in_=xa, func=AF.Square, accum_out=ss[:, 0:1]),
                         reads=[x1_t[t]], writes=[ss_t, junk_t])
                    rstd_pool(rs[:, 0:1], ss[:, 0:1], 1.0 / D, 1, ss_t, sd[:, 0:1], sd_t, rs_t)
                    S.op(S.dve, lambda: nc.vector.scalar_tensor_tensor(out=xn[:], in0=xa, scalar=rs[:, 0:1], in1=gple_b[:],
                                                                       op0=ALU.mult, op1=ALU.mult),
                         reads=[x1_t[t], rs_t, wf_t], writes=[xn_t])
                    pb, pb_t = pbr.next()
                    S.dma(S.pool, pb[:], p_own[t * 128:(t + 1) * 128, :], pb_t, writes=[pb_t])
                    fx[t] = dict(xn=(xn, xn_t), pb=(pb, pb_t))

                def f_s1(t):
                    xn, xn_t = fx[t]["xn"]
                    pb, pb_t = fx[t]["pb"]
                    h3, h3_t = h3r.next()
                    pT, pT_t = pTr.next()
                    transposes(xn[:], 8, xn_t, ptb[0], ptb_t[0])
                    S.op(S.dve, lambda: nc.vector.tensor_copy(out=h3[:], in_=ptb[0].rearrange("p (c t) -> p c t", c=8)),
                         reads=[ptb_t[0]], writes=[h3_t])
                    transposes(pb[:], 2, pb_t, ptb[1], ptb_t[1])
                    S.op(S.dve, lambda: nc.vector.tensor_copy(out=pT[:], in_=ptb[1][:, 0:256]), reads=[ptb_t[1]], writes=[pT_t])
                    fx[t] = dict(h3=(h3, h3_t), pT=(pT, pT_t))

                fbank = [(pz[0], pz_t[0]), (pz[1], pz_t[1])]
                fbank2 = [(ps[0], ps_t[0]), (ps[1], ps_t[1])]

                def f_s2(t):
                    h3, h3_t = fx[t]["h3"]
                    pT, pT_t = fx[t]["pT"]
                    for hf in range(2):
                        c0 = hf * 512
                        mm_group(fbank[hf][0][:], [(h3[:, c, :], wpg[:, c, c0:c0 + 512]) for c in range(8)], [h3_t, wf_t], fbank[hf][1])
                        mm_group(fbank2[hf][0][:], [(pT[:, c * 128:(c + 1) * 128], wpp[:, c, c0:c0 + 512]) for c in range(2)], [pT_t, wf_t], fbank2[hf][1])

                def f_s3(t):
                    fx.pop(t)
                    ot, ot_t = otr.next()
                    for hf in range(2):
                        c0 = hf * 512
                        gp, gp_t = gpr.next()
                        tf, tf_t = tfr.next()
                        S.op(S.act, lambda: nc.scalar.activation(out=gp[:], in_=fbank[hf][0][:], func=AF.Sigmoid), reads=[fbank[hf][1]], writes=[gp_t])
                        S.op(S.dve, lambda: nc.vector.tensor_tensor(out=tf[:], in0=fbank2[hf][0][:], in1=gp[:], op=ALU.mult),
                             reads=[fbank2[hf][1], gp_t], writes=[tf_t])
                        S.op(S.pool, lambda: nc.gpsimd.tensor_tensor(out=ot[:, c0:c0 + 512], in0=tf[:], in1=x1_all[:, t, c0:c0 + 512], op=ALU.add),
                             reads=[tf_t, x1_t[t]], writes=[ot_t])
                    S.dma(S.sp, y_out[t * 128:(t + 1) * 128, :], ot[:], ot_t, reads=[ot_t])
                pipeline(NT, [f_s0, f_s2, f_s1, f_s3], [0, 4, 2, 5])
                finish()
    return nc


def _tables(half):
    f32 = np.float32
    pos = (np.arange(4096, dtype=np.int64) + (half * 2048 - 2048)).astype(np.float64)
    ret_freq = 1.0 / np.power(10000.0, np.linspace(0.0, 1.0, 64, dtype=np.float64))
    rope_freq = np.power(10000.0, -np.arange(0, 128, 2, dtype=np.float64) / 128.0)

    def cs(freq):
        ang = pos[:, None] * freq[None, :]
        c = np.cos(ang).astype(f32)
        s = np.sin(ang).astype(f32)
        return np.concatenate([c, c], 1), np.concatenate([-s, s], 1)
    rc, rs = cs(ret_freq)
    ac, as_ = cs(rope_freq)
    idx = np.arange(128, dtype=np.float64)
    lg = np.log1p(-np.exp2(-5.0 - np.arange(4, dtype=np.float64)))
    dk = 128.0 ** -0.5
    mt = np.zeros((128, 4, 128), np.float64)
    for h in range(4):
        col = dk * np.exp(-lg[h] * (idx + 1.0))
        mt[:, h, :] = col[:, None] * (idx[None, :] >= idx[:, None])
    rtab = np.zeros((128, 12), np.float64)
    for h in range(4):
        rtab[:, h] = dk * np.exp(lg[h] * (127.0 - idx))
        rtab[:, 4 + h] = np.exp(lg[h] * (idx + 1.0))
        rtab[:, 8 + h] = np.exp(2.0 * lg[h] * (idx + 1.0)) / 256.0
    k = idx[:, None]
    q = idx[None, :]
    mask = np.concatenate([(k >= q), (k <= q)], 1).astype(f32)
    return dict(rcos=rc, rsin=rs, acos=ac, asin=as_, mt_tab=mt.reshape(128, 512).astype(f32), rtab=rtab.astype(f32),
                mask01=mask, ident=np.eye(128, dtype=f32), flag=np.full((128, 4), float(half), f32))


_NC_CACHE = {}


def _get_nc(stop_after=None):
    if stop_after not in _NC_CACHE:
        _NC_CACHE[stop_after] = build_program(stop_after)
    return _NC_CACHE[stop_after]


def make_in_maps(x, p, w_in, b_gate, g_mix, q_gain, k_gain, ret_gn, w_ret_out, w_att_out, w_o,
                 g_mlp, w_up, w_down, g_ple, w_ple_proj, w_ple_gate):
    f = lambda a: np.ascontiguousarray(np.asarray(a, dtype=np.float32))
    x = f(x)
    p = f(p)
    shared = dict(
        w_in=f(w_in[0]), w_ro=f(w_ret_out[0]), w_ao=f(w_att_out[0]), w_o=f(w_o[0]), w_up=f(w_up[0]), w_down=f(w_down[0]),
        w_pp=f(w_ple_proj[0]), w_pg=f(w_ple_gate[0]), g_mix=f(g_mix[0]).reshape(1, -1), g_mlp=f(g_mlp[0]).reshape(1, -1),
        g_ple=f(g_ple[0]).reshape(1, -1), b_gate=f(b_gate[0]).reshape(1, -1), q_gain=f(q_gain[0]).reshape(1, -1),
        k_gain=f(k_gain[0]).reshape(1, -1), ret_gn=f(ret_gn[0]).reshape(1, -1))
    tabs = [_tables(0), _tables(1)]
    in_maps = []
    for core in range(8):
        b, half = core // 2, core % 2
        m = dict(shared)
        m["x_own"] = np.ascontiguousarray(x[b, half * 2048:(half + 1) * 2048])
        m["x_pre"] = np.ascontiguousarray(x[b, 0:2048]) if half == 1 else np.zeros((2048, 1024), np.float32)
        m["p_own"] = np.ascontiguousarray(p[0, b, half * 2048:(half + 1) * 2048])
        m.update(tabs[half])
        in_maps.append(m)
    return in_maps


def kernel(**inputs):
    in_maps = make_in_maps(**inputs)
    nc = _get_nc(None)
    res = run_bass_kernel_spmd(nc, in_maps, core_ids=list(range(8)))
    out = np.zeros((4, 4096, 1024), np.float32)
    for core in range(8):
        b, half = core // 2, core % 2
        out[b, half * 2048:(half + 1) * 2048] = res.results[core]["y"]
    return out
```
